# Optimizing a Trainium2 kernel written in Bass

```python
import math
import jax, jax.numpy as jnp
from jax import lax
import numpy as np

D_MODEL = 1024
BATCH = 2
SEQ = 16384
DEPTH = 2

CHUNK = 64
N_MEM = 256
D_MIX = D_MODEL
POOL_WINDOWS = (2, 4, 8, 16)
POOL_GROUPS = len(POOL_WINDOWS)
POOL_CH = 64
W_A = POOL_GROUPS * POOL_CH
SG_BLOCK = 2 * CHUNK
SG_HEADS = 4
SG_HEAD_DIM = 96
W_B = SG_HEADS * SG_HEAD_DIM
CONV_K = 31
W_C = D_MIX - W_A - W_B
D_IN = W_A + 2 * W_B + 2 * W_C
X_HEADS = 4
X_HEAD_DIM = D_MODEL // X_HEADS
D_FF = 4 * D_MODEL
EPS = 1e-6

kernel_name = "hybrid_pool_sgmlp_conformer_block"


def rms_norm(x, g):
    xf = x.astype(jnp.float32)
    y = xf * lax.rsqrt(jnp.mean(jnp.square(xf), axis=-1, keepdims=True) + EPS)
    return (y * g.astype(jnp.float32)).astype(x.dtype)


def layer_norm(x, g, b):
    xf = x.astype(jnp.float32)
    mu = jnp.mean(xf, axis=-1, keepdims=True)
    var = jnp.mean(jnp.square(xf - mu), axis=-1, keepdims=True)
    y = (xf - mu) * lax.rsqrt(var + EPS)
    return (y * g.astype(jnp.float32) + b.astype(jnp.float32)).astype(x.dtype)


def pool_mixer(a, pool_w, pool_scale):
    bsz, s, _ = a.shape
    ag = a.reshape(bsz, s, POOL_GROUPS, POOL_CH).astype(jnp.float32)
    cs = jnp.cumsum(ag, axis=1)
    pos = jnp.arange(1, s + 1, dtype=jnp.float32)[None, :, None]
    outs = []
    for g, w in enumerate(POOL_WINDOWS):
        c = cs[:, :, g]
        lag = jnp.pad(c[:, : s - w], ((0, 0), (w, 0), (0, 0)))
        cnt = jnp.minimum(pos, float(w))
        outs.append((c - lag) / cnt - ag[:, :, g])
    p = jnp.stack(outs, axis=2).astype(a.dtype)
    y = jnp.einsum('bsgc,gcd->bsgd', p, pool_w)
    return y.reshape(bsz, s, W_A) * pool_scale


def spatial_gating_mixer(z, ln_g, ln_b, sg_w, sg_b):
    bsz, s, _ = z.shape
    z = jax.nn.gelu(z)
    u, v = jnp.split(z, 2, axis=-1)
    v = layer_norm(v, ln_g, ln_b)
    nb = s // SG_BLOCK
    v = v.reshape(bsz, nb, SG_BLOCK, SG_HEADS, SG_HEAD_DIM)
    u = u.reshape(bsz, nb, SG_BLOCK, SG_HEADS, SG_HEAD_DIM)
    mask = jnp.tril(jnp.ones((SG_BLOCK, SG_BLOCK), dtype=sg_w.dtype))
    w = sg_w * mask[None]
    sv = jnp.einsum('hts,bnshd->bnthd', w, v) + jnp.transpose(sg_b)[None, None, :, :, None]
    return (u * sv).reshape(bsz, s, W_B)


def conformer_conv_mixer(z, conv_w, conv_b, ln_g, ln_b):
    a, g = jnp.split(z, 2, axis=-1)
    h = a * jax.nn.sigmoid(g)
    h = lax.conv_general_dilated(
        h, conv_w[:, None, :], window_strides=(1,), padding=[(CONV_K - 1, 0)],
        dimension_numbers=('NWC', 'WIO', 'NWC'), feature_group_count=W_C)
    h = h + conv_b
    h = layer_norm(h, ln_g, ln_b)
    return jax.nn.silu(h)


def memory_cross_attention(h, m, wq, wk, wv, wo):
    bsz, s, _ = h.shape
    q = (h @ wq).reshape(bsz, s, X_HEADS, X_HEAD_DIM)
    k = (m @ wk).reshape(bsz, N_MEM, X_HEADS, X_HEAD_DIM)
    v = (m @ wv).reshape(bsz, N_MEM, X_HEADS, X_HEAD_DIM)
    sc = jnp.einsum('bshd,bmhd->bhsm', q, k).astype(jnp.float32) * (1.0 / math.sqrt(X_HEAD_DIM))
    p = jax.nn.softmax(sc, axis=-1).astype(v.dtype)
    o = jnp.einsum('bhsm,bmhd->bshd', p, v).reshape(bsz, s, D_MODEL)
    return o @ wo


def setup_inputs(seed: int = 0) -> dict:
    key = jax.random.key(seed)
    ks = iter(jax.random.split(key, 40))
    L, D = DEPTH, D_MODEL

    def nrm(shape, scale):
        return jax.random.normal(next(ks), shape, dtype=jnp.float32) * scale

    def gain(shape):
        return 1.0 + nrm(shape, 0.05)

    return {
        "x": nrm((BATCH, SEQ, D), 1.0),
        "mem": nrm((BATCH, N_MEM, D), 1.0),
        "pre_mix_g": gain((L, D)),
        "w_in": nrm((L, D, D_IN), D ** -0.5),
        "b_in": nrm((L, D_IN), 0.02),
        "pool_w": nrm((L, POOL_GROUPS, POOL_CH, POOL_CH), POOL_CH ** -0.5),
        "pool_scale": gain((L, W_A)),
        "sg_ln_g": gain((L, W_B)),
        "sg_ln_b": nrm((L, W_B), 0.02),
        "sg_w": nrm((L, SG_HEADS, SG_BLOCK, SG_BLOCK), SG_BLOCK ** -0.5),
        "sg_b": gain((L, SG_HEADS, SG_BLOCK)),
        "conv_w": nrm((L, CONV_K, W_C), CONV_K ** -0.5),
        "conv_b": nrm((L, W_C), 0.02),
        "conv_ln_g": gain((L, W_C)),
        "conv_ln_b": nrm((L, W_C), 0.02),
        "w_out": nrm((L, D_MIX, D), D_MIX ** -0.5),
        "post_mix_g": gain((L, D)),
        "pre_x_g": gain((L, D)),
        "mem_g": gain((L, D)),
        "wq": nrm((L, D, D), D ** -0.5),
        "wk": nrm((L, D, D), D ** -0.5),
        "wv": nrm((L, D, D), D ** -0.5),
        "wo": nrm((L, D, D), D ** -0.5),
        "post_x_g": gain((L, D)),
        "pre_ff_g": gain((L, D)),
        "w_ff1": nrm((L, D, D_FF), D ** -0.5),
        "w_ff2": nrm((L, D_FF, D), D_FF ** -0.5),
        "post_ff_g": gain((L, D)),
    }


def reference(x, mem, pre_mix_g, w_in, b_in, pool_w, pool_scale, sg_ln_g, sg_ln_b,
              sg_w, sg_b, conv_w, conv_b, conv_ln_g, conv_ln_b, w_out, post_mix_g,
              pre_x_g, mem_g, wq, wk, wv, wo, post_x_g, pre_ff_g, w_ff1, w_ff2,
              post_ff_g):
    for l in range(DEPTH):
        h = rms_norm(x, pre_mix_g[l])
        z = h @ w_in[l] + b_in[l]
        z_a = z[..., :W_A]
        z_b = z[..., W_A:W_A + 2 * W_B]
        z_c = z[..., W_A + 2 * W_B:]
        y_a = pool_mixer(z_a, pool_w[l], pool_scale[l])
        y_b = spatial_gating_mixer(z_b, sg_ln_g[l], sg_ln_b[l], sg_w[l], sg_b[l])
        y_c = conformer_conv_mixer(z_c, conv_w[l], conv_b[l], conv_ln_g[l], conv_ln_b[l])
        y = jnp.concatenate([y_a, y_b, y_c], axis=-1) @ w_out[l]
        x = x + rms_norm(y, post_mix_g[l])
        h = rms_norm(x, pre_x_g[l])
        m = rms_norm(mem, mem_g[l])
        y = memory_cross_attention(h, m, wq[l], wk[l], wv[l], wo[l])
        x = x + rms_norm(y, post_x_g[l])
        h = rms_norm(x, pre_ff_g[l])
        y = jnp.square(jax.nn.relu(h @ w_ff1[l])) @ w_ff2[l]
        x = x + rms_norm(y, post_ff_g[l])
    return x
```

```python
import numpy as np
import concourse.bass as bass
import concourse.mybir as mybir
from concourse.bass_utils import run_bass_kernel_spmd

F32 = mybir.dt.float32
BF16 = mybir.dt.bfloat16
AF = mybir.ActivationFunctionType
ALU = mybir.AluOpType

D = 1024
SEQ = 16384
BATCH = 2
NCORE = 8
TPC = 4096
HALO = 128
SUB = 256
NSUBP = 4
PASS_T = SUB * NSUBP
NPASS = TPC // PASS_T
RES_T = HALO + PASS_T
DEPTH = 2
EPS = 1e-6
W_A, W_B, W_C = 256, 384, 384
D_IN = 1792
D_FF = 4096
CONV_K = 31
GELU_C = 0.7978845608028654

NL = 174
def G_(l, i, c): return l * NL + 8 * i + c
def BA_(l, c): return l * NL + 56 + c
def BU_(l, h): return l * NL + 58 + h
def BCA_(l, c): return l * NL + 62 + c
def BCG_(l, c): return l * NL + 65 + c
def PSC_(l, c): return l * NL + 68 + c
def IW_(l, c): return l * NL + 70 + c
def CB_(l, c): return l * NL + 72 + c
def CLG_(l, c): return l * NL + 75 + c
def CLB_(l, c): return l * NL + 78 + c
def CW_(l, c, k): return l * NL + 81 + c * 31 + k


class Buf:
    __slots__ = ("name", "last_w", "readers", "dsem", "dcnt", "excl")

    def __init__(self, name):
        self.name = name
        self.last_w = None
        self.readers = {}
        self.dsem = None
        self.dcnt = 0
        self.excl = False


class Sched:
    def __init__(self, nc):
        self.nc = nc
        self.eng = {"pe": nc.tensor, "act": nc.scalar, "dve": nc.vector, "pool": nc.gpsimd, "sp": nc.sync}
        self.semh = {}
        self.cnt = {}
        self.waited = {k: {} for k in self.eng}
        for k in self.eng:
            self.semh["e_" + k] = nc.alloc_semaphore("e_" + k)
            self.cnt[k] = 0
        self.nbuf = 0
        self.allb = []
        self.defer = False
        self.pend = []
        self.flags = set()
        self.eng_free = {k: 0.0 for k in self.eng}
        self.ev_t = {}

    def buf(self, name):
        self.nbuf += 1
        b = Buf(f"{name}_{self.nbuf}")
        self.allb.append(b)
        return b

    def _needs(self, e, reads, writes):
        own = "e_" + e
        need = {}

        def add(ev, allow_own):
            if ev is None:
                return
            k, v = ev
            if k == own and not allow_own:
                return
            if need.get(k, 0) < v:
                need[k] = v

        for b in reads:
            add(b.last_w, True)
            if b.excl:
                for k, v in b.readers.items():
                    add((k, v), False)
        for b in writes:
            add(b.last_w, False)
            for k, v in b.readers.items():
                add((k, v), False)
        return need

    def _do_waits(self, e, need):
        w = self.waited[e]
        h = self.eng[e]
        for k, v in need.items():
            if w.get(k, 0) < v:
                h.wait_ge(self.semh[k], v)
                w[k] = v

    def _record(self, ev, reads, writes):
        k, v = ev
        for b in reads:
            if b.readers.get(k, 0) < v:
                b.readers[k] = v
        for b in writes:
            b.last_w = ev
            b.readers = {}

    def _est(self, e, need, dur):
        t = self.eng_free[e]
        for k, v in need.items():
            tv = self.ev_t.get((k, v))
            if tv is not None and tv + 0.25 > t:
                t = tv + 0.25
        return t, t + dur

    def est_start(self, d):
        if d[0] in ("flag", "rel"):
            return -1.0
        if d[0] == "op":
            _, e, fn, reads, writes, dur = d
            return self._est(e, self._needs(e, reads, writes), dur)[0]
        _, out_buf, out_ap, pairs, reads, dur = d
        return self._est("pe", self._needs("pe", reads, [out_buf]), dur)[0]

    def commit(self, d):
        if d[0] == "flag":
            self.flags.add(d[1])
        elif d[0] == "rel":
            self.on_rel(d[1])
        elif d[0] == "op":
            self.op(d[1], d[2], d[3], d[4], dur=d[5], force=True)
        else:
            self.mm(d[1], d[2], d[3], d[4], force=True)

    def set_flag(self, name):
        if self.defer:
            self.pend.append(("flag", name))
        else:
            self.flags.add(name)

    def op(self, e, fn, reads=(), writes=(), dur=None, force=False):
        if dur is None:
            dur = 0.36 if e == "act" else 0.43
        if self.defer and not force:
            self.pend.append(("op", e, fn, tuple(reads), tuple(writes), dur))
            return
        need = self._needs(e, reads, writes)
        t0, t1 = self._est(e, need, dur)
        self._do_waits(e, need)
        inst = fn(self.eng[e])
        self.cnt[e] += 1
        inst.then_inc(self.semh["e_" + e], 1)
        self.eng_free[e] = t1
        self.ev_t[("e_" + e, self.cnt[e])] = t1
        self._record(("e_" + e, self.cnt[e]), reads, writes)

    def mm(self, out_buf, out_ap, pairs, reads, force=False):
        e = "pe"
        if self.defer and not force:
            self.pend.append(("mm", out_buf, out_ap, list(pairs), tuple(reads), 0.13 * len(pairs)))
            return
        need = self._needs(e, reads, [out_buf])
        t0, t1 = self._est(e, need, 0.13 * len(pairs))
        self.eng_free[e] = t1
        self.ev_t[("e_pe", self.cnt[e] + 1)] = t1
        self._do_waits(e, need)
        n = len(pairs)
        inst = None
        for i, (l, r) in enumerate(pairs):
            inst = self.nc.tensor.matmul(out_ap, l, r, start=(i == 0), stop=(i == n - 1))
        self.cnt[e] += 1
        inst.then_inc(self.semh["e_pe"], 1)
        self._record(("e_pe", self.cnt[e]), reads, [out_buf])

    def dma(self, q, out_ap, in_ap, reads=(), writes=()):
        b = writes[0] if writes else reads[0]
        need = {}
        for rb in reads:
            if rb.last_w is not None:
                k, v = rb.last_w
                need[k] = max(need.get(k, 0), v)
        for wb in writes:
            if wb.last_w is not None:
                k, v = wb.last_w
                need[k] = max(need.get(k, 0), v)
            for k, v in wb.readers.items():
                need[k] = max(need.get(k, 0), v)
        self._do_waits(q, need)
        if b.dsem is None:
            b.dsem = "d_" + b.name
            self.semh[b.dsem] = self.nc.alloc_semaphore(b.dsem)
        b.dcnt += 16
        self.eng[q].dma_start(out=out_ap, in_=in_ap).then_inc(self.semh[b.dsem], 16)
        self._record((b.dsem, b.dcnt), reads, writes)

    def alias_barrier(self, from_bufs, to_bufs):
        evs = {}
        for b in from_bufs:
            if b.last_w is not None:
                k, v = b.last_w
                evs[k] = max(evs.get(k, 0), v)
            for k, v in b.readers.items():
                evs[k] = max(evs.get(k, 0), v)
        for b in to_bufs:
            for k, v in evs.items():
                if b.readers.get(k, 0) < v:
                    b.readers[k] = v

    def final_wait(self, q, bufs):
        need = {}
        for b in bufs:
            if b.last_w is not None:
                k, v = b.last_w
                need[k] = max(need.get(k, 0), v)
            for k, v in b.readers.items():
                need[k] = max(need.get(k, 0), v)
        self._do_waits(q, need)


def run_interleaved(gens):
    gens = list(gens)
    while gens:
        nxt = []
        for g in gens:
            try:
                next(g)
                nxt.append(g)
            except StopIteration:
                pass
        gens = nxt


def run_scheduled(S, gens):
    gens = list(gens)
    pend = [[] for _ in gens]
    alive = [True] * len(gens)
    blocked = [None] * len(gens)
    while True:
        progress = False
        for i, g in enumerate(gens):
            while alive[i] and not pend[i]:
                if blocked[i] is not None:
                    if S.flag_ok(blocked[i]):
                        blocked[i] = None
                    else:
                        break
                S.defer, S.pend = True, []
                try:
                    r = next(g)
                except StopIteration:
                    alive[i] = False
                    r = None
                finally:
                    S.defer = False
                pend[i].extend(S.pend)
                S.pend = []
                if isinstance(r, tuple) and r and r[0] == "wait":
                    blocked[i] = r[1]
                    if pend[i]:
                        break
        best, bt = None, None
        for i in range(len(gens)):
            if pend[i]:
                t = S.est_start(pend[i][0])
                if bt is None or t < bt:
                    best, bt = i, t
        if best is None:
            if any(alive):
                if all((not alive[i]) or (blocked[i] is not None and not S.flag_ok(blocked[i])) for i in range(len(gens))):
                    raise RuntimeError("scheduler deadlock on flags: %s" % [b for b in blocked if b])
                continue
            break
        S.commit(pend[best].pop(0))


class SubT:
    def __init__(self, slot, off, n, is_pre, first_real, dram0):
        self.slot, self.off, self.n, self.is_pre, self.first_real, self.dram0 = slot, off, n, is_pre, first_real, dram0


def build_program(LAYERS):
    nc = bass.Bass("TRN2", target_bir_lowering=False)
    S = Sched(nc)
    NLY = len(LAYERS)
    NPP = DEPTH * NL

    def din(name, shape):
        return nc.dram_tensor(name, shape, F32, kind="ExternalInput").ap()

    xT = din("xT", [8, 128, HALO + TPC])
    memT = din("memT", [8, 128, 256])
    pp_d = din("pp", [128, NPP])
    rowp_d = din("rowp", [DEPTH, 3, 384])
    sgb_d = din("sgb", [DEPTH, 512])
    sgwT_d = din("sgwT", [DEPTH, 4, 128, 128])
    poolw_d = din("poolw", [DEPTH, 4, 64, 64])
    ident_d = din("ident", [128, 128])
    triu_d = din("triu", [128, 128])
    icnt_d = din("icnt", [128, 2, 16])
    mask_d = din("mask", [128, 1])
    w_in_d = din("w_in", [DEPTH, D, D_IN])
    w_out_d = din("w_out", [DEPTH, D, D])
    wq_d = din("wq", [DEPTH, D, D])
    wk_d = din("wk", [DEPTH, D, D])
    wv_d = din("wv", [DEPTH, D, D])
    wo_d = din("wo", [DEPTH, D, D])
    w1_d = din("w_ff1", [DEPTH, D, D_FF])
    w2_d = din("w_ff2", [DEPTH, D_FF, D])
    outT = nc.dram_tensor("outT", [8, 128, TPC], F32, kind="ExternalOutput").ap()

    def sb(name, shape, dt):
        return nc.alloc_sbuf_tensor("s_" + name, shape, dt)

    xres = sb("xres", [128, 8, RES_T], F32)
    hT = sb("hT", [128, 8, 2 * SUB], BF16)
    sq = sb("sq", [128, 8, SUB], BF16)
    st = sb("st", [128, 4, SUB], F32)
    kT = sb("kT", [128, NLY, 8, 256], BF16)
    vv = sb("vv", [128, NLY, 2, 1024], BF16)
    pp = sb("pp", [128, NPP], F32)
    ppn = sb("ppn", [128, 9 * DEPTH], F32)
    rowb = sb("rowb", [128, 3, 384], F32)
    sgw = sb("sgw", [128, NLY, 4, 128], BF16)
    bd = sb("bd", [128, NLY, 2, 128], BF16)
    ones = sb("ones", [128, 128], BF16)
    ident = sb("ident", [128, 128], F32)
    sgbr = sb("sgbr", [1, NLY * 512], BF16)
    icnt = sb("icnt", [128, 2, 16], F32)
    mask = sb("mask", [128, 1], F32)
    sm = sb("sm", [128, 2, 16], F32)
    stZ = sb("stZ", [128, NLY, 2, 16], F32)
    stH = sb("stH", [128, NLY, 3, 32], BF16)
    uT = sb("uT", [128, 8, SUB], BF16)
    rl = sb("rl", [128, 2, SUB], F32)
    gs = uT[:, :, :].rearrange("p c t -> p (c t)").bitcast(F32).rearrange("p (c t) -> p c t", c=4)
    ring = [sb(f"ring{i}", [128, 8, 1024], BF16) for i in range(4)]
    w9 = sb("w9", [128, 1024], BF16)
    MR_F = 8 * RES_T
    mr_words = 0
    carve = {}

    def carve_f32(name, nwords):
        nonlocal mr_words
        carve[name] = (mr_words, nwords)
        mr_words += nwords

    carve_f32("zA", 2 * (16 + SUB))
    carve_f32("T1", 16 + SUB)
    carve_f32("T2", 16 + SUB)
    carve_f32("tmp", 4 * 384)
    carve_f32("cc", 3 * SUB)
    carve_f32("yT", 8 * SUB)
    carve_f32("pb", 2 * SUB // 2)
    carve_f32("ya", 2 * SUB // 2)
    carve_f32("ub", 4 * SUB // 2)
    carve_f32("vln", 2 * 384 // 2)
    carve_f32("yb", 4 * SUB // 2)
    carve_f32("hbuf", 3 * (32 + SUB) // 2)
    carve_f32("yc", 3 * SUB // 2)
    carve_f32("dg", 93 * 128 // 2)
    HF_W = 8 * RES_T // 2
    mr_total = max(mr_words, MR_F + HF_W)
    MR = sb("MR", [128, mr_total], F32)

    def mrv(name, dt, pattern=None, **kw):
        o, nw = carve[name]
        v = MR[:, o:o + nw]
        if dt == BF16:
            v = v.bitcast(BF16)
        if pattern:
            v = v.rearrange(pattern, **kw)
        return v

    zA = mrv("zA", F32, "p (c t) -> p c t", c=2)
    T1 = mrv("T1", F32)
    T2 = mrv("T2", F32)
    tmp = mrv("tmp", F32, "p (c t) -> p c t", c=4)
    cc = mrv("cc", F32, "p (c t) -> p c t", c=3)
    yT = mrv("yT", F32, "p (c t) -> p c t", c=8)
    pb = mrv("pb", BF16, "p (c t) -> p c t", c=2)
    ya = mrv("ya", BF16, "p (c t) -> p c t", c=2)
    ub = mrv("ub", BF16, "p (c t) -> p c t", c=4)
    vln = mrv("vln", BF16, "p (c t) -> p c t", c=2)
    yb = mrv("yb", BF16, "p (c t) -> p c t", c=4)
    hbuf = mrv("hbuf", BF16, "p (c t) -> p c t", c=3)
    yc = mrv("yc", BF16, "p (c t) -> p c t", c=3)
    dg = mrv("dg", BF16, "p (k m) -> p k m", k=93)
    hTf = MR[:, MR_F:MR_F + HF_W].bitcast(BF16).rearrange("p (c t) -> p c t", c=8)
    triu = MR[:, carve["yT"][0]:carve["yT"][0] + 128]
    yacc = MR[:, 0:MR_F].rearrange("p (c t) -> p c t", c=8)
    a0 = carve["zA"][0]
    qT = MR[:, a0:a0 + 1024].bitcast(BF16).rearrange("p (c t) -> p c t", c=8)
    oT = MR[:, a0 + 1024:a0 + 2048].bitcast(BF16).rearrange("p (c t) -> p c t", c=8)
    ET = MR[:, a0 + 2048:a0 + 2560].bitcast(BF16).rearrange("p (a b t) -> p a b t", a=2, b=2)
    rec = MR[:, a0 + 2560:a0 + 3072].rearrange("p (a t) -> p a t", a=2)
    assert a0 + 3072 <= carve["yT"][0], "attention scratch overlaps yT"

    banks = [(S.buf(f"bank{i}"), nc.alloc_psum_tensor(f"bank{i}", [128, 512], F32)) for i in range(8)]
    for bB, _ in banks:
        bB.excl = True
    free_banks = list(range(8))
    bank_idx = {}
    for i_, (bB_, _t) in enumerate(banks):
        bank_idx[bB_.name] = i_

    def acquire():
        while not free_banks:
            yield ("wait", "__bank__")
        i = free_banks.pop(0)
        return banks[i]

    def bank():
        assert free_banks, "no free PSUM bank in immediate mode"
        return banks[free_banks.pop(0)]

    def rel(bB):
        i = bank_idx[bB.name]
        if S.defer:
            S.pend.append(("rel", i))
        else:
            free_banks.append(i)

    locks = {"norm": True}

    def lock_acquire(name):
        while not locks[name]:
            yield ("wait", "__lock__" + name)
        locks[name] = False

    def lock_release(name):
        if S.defer:
            S.pend.append(("rel", "L:" + name))
        else:
            locks[name] = True

    def _on_rel(i):
        if isinstance(i, str):
            locks[i[2:]] = True
        else:
            free_banks.append(i)

    S.on_rel = _on_rel
    S.flag_ok = lambda name: (bool(free_banks) if name == "__bank__" else
                              (locks[name[8:]] if name.startswith("__lock__") else name in S.flags))

    def drive(gen):
        try:
            while True:
                r = next(gen)
                assert not (isinstance(r, tuple) and r and r[0] == "wait"), "blocking wait in immediate mode"
        except StopIteration as e:
            return e.value

    def grouped(items, emit_mm, emit_evac):
        for i in range(0, len(items), 2):
            bB, t = bank()
            grp = items[i:i + 2]
            for j, it in enumerate(grp):
                emit_mm(it, bB, t[:, 256 * j:256 * j + 256])
            for j, it in enumerate(grp):
                emit_evac(it, bB, t[:, 256 * j:256 * j + 256], i // 2)
            rel(bB)

    def grouped2(items, emit_mm, emit_evac_bank, n):
        for i in range(0, len(items), 2):
            bB, t = bank()
            for j in range(2):
                emit_mm(items[i + j], bB, t[:, 256 * j:256 * j + 256])
            emit_evac_bank(items[i], bB, t[:, :].rearrange("p (a t) -> p a t", a=2)[:, :, 0:n], i // 2)
            rel(bB)

    NSLOT = NSUBP + 1
    xB = [S.buf(f"x{i}") for i in range(NSLOT)]
    hB = [S.buf("h0"), S.buf("h1")]
    hfB = [S.buf(f"hf{i}") for i in range(NSLOT)]
    yaccB = [S.buf(f"yacc{i}") for i in range(NSLOT)]
    sqB = S.buf("sq")
    rstdB = [S.buf("rstd0"), S.buf("rstd1")]
    stmB, stvB = S.buf("stm"), S.buf("stv")
    kTB, vvB = S.buf("kT"), S.buf("vv")
    ppB, ppnB, rowbB, sgwB, bdB, onesB = S.buf("pp"), S.buf("ppn"), S.buf("rowb"), S.buf("sgw"), S.buf("bd"), S.buf("ones")
    identB, sgbrB, icntB, maskB = S.buf("ident"), S.buf("sgbr"), S.buf("icnt"), S.buf("mask")
    smB = [S.buf("sm0"), S.buf("sm1")]
    stZB, stHB = S.buf("stZ"), S.buf("stH")
    uTB, rlB = S.buf("uT"), [S.buf("rl0"), S.buf("rl1")]
    gsB = [S.buf(f"gs{i}") for i in range(3)]
    ringB = [S.buf(f"ring{i}") for i in range(4)]
    w9B = S.buf("w9")
    zAB = [S.buf("zA0"), S.buf("zA1")]
    T1B, T2B = S.buf("T1"), S.buf("T2")
    tmpB = [S.buf(f"tmp{i}") for i in range(4)]
    ccB = [S.buf(f"cc{i}") for i in range(3)]
    yTB = S.buf("yT")
    yTcB = [S.buf(f"yTc{i}") for i in range(8)]
    qTcB = [S.buf(f"qTc{i}") for i in range(4)]
    pbB = [S.buf("pb0"), S.buf("pb1")]
    yaB = [S.buf("ya0"), S.buf("ya1")]
    ubB = [S.buf(f"ub{i}") for i in range(4)]
    vlnB = [S.buf("vln0"), S.buf("vln1")]
    ybB = [S.buf(f"yb{i}") for i in range(4)]
    hbufB = [S.buf(f"hbuf{i}") for i in range(3)]
    ycB = [S.buf(f"yc{i}") for i in range(3)]
    dgB = [S.buf(f"dg{i}") for i in range(3)]
    qTB, oTB, ETB, recB = S.buf("qT"), S.buf("oT"), [S.buf("ET0"), S.buf("ET1")], [S.buf("rec0"), S.buf("rec1")]
    mixer_bufs = zAB + [T1B, T2B] + tmpB + ccB + [yTB] + yTcB + pbB + yaB + ubB + vlnB + ybB + hbufB + ycB + dgB
    q2cB = [S.buf(f"q2c{i}") for i in range(4)]
    attn_bufs = [qTB, oTB] + qTcB + ETB + recB
    attn_alias_src = zAB + [T1B, T2B] + tmpB + ccB

    loads = []
    state = {"emitted": 0}

    def w_rows(src2d, p=128):
        return src2d.rearrange("(k p) n -> p k n", p=p)

    def add_load(kind, l, q=None):
        j = len(loads)
        slot = j % 4

        def emit():
            R, RB = ring[slot], ringB[slot]
            if kind == "inA":
                S.dma("pool", R[:, :, :], w_rows(w_in_d[l, :, 0:1024]), writes=[RB])
            elif kind == "inB":
                S.dma("pool", R[:, :, 0:768], w_rows(w_in_d[l, :, 1024:1792]), writes=[RB])
            elif kind == "out":
                S.dma("pool", R[:, 0:2, :], w_rows(w_out_d[l, 0:256, :]), writes=[RB])
                S.dma("pool", R[0:96, 2:6, :], w_rows(w_out_d[l, 256:640, :], 96), writes=[RB])
                S.dma("pool", R[:, 6:8, :], w_rows(w_out_d[l, 640:896, :]), writes=[RB])
                S.dma("pool", w9[:, :], w_out_d[l, 896:1024, :], writes=[w9B])
            elif kind in ("wq", "wk", "wv", "wo"):
                src = {"wq": wq_d, "wk": wk_d, "wv": wv_d, "wo": wo_d}[kind]
                S.dma("pool", R[:, :, :], w_rows(src[l, :, :]), writes=[RB])
            elif kind == "w1":
                S.dma("pool", R[:, :, :], w_rows(w1_d[l, :, q * 1024:(q + 1) * 1024]), writes=[RB])
            elif kind == "w2":
                S.dma("pool", R[:, :, :], w_rows(w2_d[l, q * 1024:(q + 1) * 1024, :]), writes=[RB])

        loads.append(emit)
        return j

    def pump(upto):
        while state["emitted"] <= upto and state["emitted"] < len(loads):
            loads[state["emitted"]]()
            state["emitted"] += 1

    def done(j):
        pump(j + 4)

    kv_loads = {}
    pl_loads = {}
    for p in range(NPASS):
        for li, l in enumerate(LAYERS):
            d = {}
            d["inA"] = add_load("inA", l)
            d["inB"] = add_load("inB", l)
            d["out"] = add_load("out", l)
            if p == 0:
                kv_loads[li] = (add_load("wk", l), add_load("wv", l))
            d["wq"] = add_load("wq", l)
            d["wo"] = add_load("wo", l)
            for q in range(4):
                d[("w1", q)] = add_load("w1", l, q)
                d[("w2", q)] = add_load("w2", l, q)
            pl_loads[(p, li)] = d

    S.dma("sp", pp[:, :], pp_d, writes=[ppB])
    S.dma("sp", ident[:, :], ident_d, writes=[identB])
    S.dma("sp", icnt[:, :, :], icnt_d, writes=[icntB])
    S.dma("sp", mask[:, :], mask_d, writes=[maskB])
    pump(3)
    for l_ in range(DEPTH):
        S.op("dve", lambda e, l_=l_: e.tensor_scalar(out=ppn[:, 9 * l_:9 * l_ + 3], in0=pp[:, BCG_(l_, 0):BCG_(l_, 0) + 3], scalar1=-1.0, scalar2=None, op0=ALU.mult),
             reads=[ppB], writes=[ppnB])
        S.op("dve", lambda e, l_=l_: e.tensor_scalar(out=ppn[:, 9 * l_ + 3:9 * l_ + 9], in0=pp[:, CLG_(l_, 0):CLG_(l_, 0) + 6], scalar1=-1.0, scalar2=None, op0=ALU.mult),
             reads=[ppB], writes=[ppnB])
    S.op("dve", lambda e: e.memset(ones[:, :], 1.0), writes=[onesB])
    S.op("dve", lambda e: e.memset(bd[:, :, :, :], 0.0), writes=[bdB])
    S.op("dve", lambda e: e.memset(MR[:, 0:mr_total], 0.0), writes=mixer_bufs)
    S.dma("sp", triu, triu_d, writes=[yTB])
    for li, l in enumerate(LAYERS):
        for c in range(2):
            for hh in range(2):
                g = 2 * c + hh
                S.dma("pool", bd[64 * hh:64 * hh + 64, li, c, 64 * hh:64 * hh + 64], poolw_d[l, g, :, :], writes=[bdB])
        S.dma("pool", sgbr[0:1, li * 512:(li + 1) * 512], sgb_d[l:l + 1, :], writes=[sgbrB])
        for h in range(4):
            tb = tmpB[h % 4]
            S.dma("sp", tmp[:, h % 4, 0:128], sgwT_d[l, h, :, :], writes=[tb])
            S.op("dve", lambda e, h=h, li=li: e.tensor_tensor(out=sgw[:, li, h, :], in0=tmp[:, h % 4, 0:128], in1=triu, op=ALU.mult),
                 reads=[tb, yTB], writes=[sgwB])

    def norm_stats(src3d, src_bufs, n, par):
        S.op("act", lambda e: e.activation(out=sq[:, :, 0:n], in_=src3d, func=AF.Square), reads=src_bufs, writes=[sqB], dur=2.0)
        return stats_from_sq(n, par)

    def stats_gen(n, par):
        bB, t = yield from acquire()
        S.mm(bB, t[:, 0:n], [(ones[:, :], sq[:, c, 0:n]) for c in range(8)], reads=[sqB, onesB])
        r = st[:, par, 0:n]
        S.op("act", lambda e: e.activation(out=r, in_=t[:, 0:n], func=AF.Ln, scale=1.0 / D, bias=EPS),
             reads=[bB], writes=[rstdB[par]])
        rel(bB)
        S.op("act", lambda e: e.activation(out=r, in_=r, func=AF.Exp, scale=-0.5),
             reads=[rstdB[par]], writes=[rstdB[par]])
        return r

    def stats_from_sq(n, par):
        return drive(stats_gen(n, par))

    def hview(sub, ffn=False):
        if ffn:
            return (lambda k: hTf[:, k, sub.off:sub.off + sub.n]), hfB[sub.slot]
        hs = sub.slot % 2
        return (lambda k: hT[:, k, hs * SUB:hs * SUB + sub.n]), hB[hs]

    def pre_norm(l, gi, sub, par, ffn=False):
        return drive(pre_norm_gen(l, gi, sub, par, ffn))

    def pre_norm_gen(l, gi, sub, par, ffn=False):
        n, off, s = sub.n, sub.off, sub.slot
        hv, hb = hview(sub, ffn)
        yield from lock_acquire("norm")
        S.op("act", lambda e: e.activation(out=sq[:, :, 0:n], in_=xres[:, :, off:off + n], func=AF.Square), reads=[xB[s]], writes=[sqB], dur=2.0)
        yield
        r = yield from stats_gen(n, par)
        yield
        for c in range(8):
            S.op("dve", lambda e, c=c: e.scalar_tensor_tensor(
                out=hv(c), in0=xres[:, c, off:off + n], scalar=pp[:, G_(l, gi, c):G_(l, gi, c) + 1],
                in1=r, op0=ALU.mult, op1=ALU.mult), reads=[xB[s], rstdB[par], ppB], writes=[hb])
            if c == 7:
                lock_release("norm")
            yield

    def post_norm_residual(l, gi, sub, par, y_ap_fn, yBuf, y3d):
        return drive(post_norm_gen(l, gi, sub, par, y_ap_fn, yBuf, y3d))

    def post_norm_gen(l, gi, sub, par, y_ap_fn, yBuf, y3d):
        n, off, s = sub.n, sub.off, sub.slot
        allb = list(dict.fromkeys(yBuf(c) for c in range(8)))
        yield from lock_acquire("norm")
        S.op("act", lambda e: e.activation(out=sq[:, :, 0:n], in_=y3d, func=AF.Square), reads=allb, writes=[sqB], dur=2.0)
        yield
        r = yield from stats_gen(n, par)
        yield
        for c in range(8):
            S.op("dve", lambda e, c=c: e.scalar_tensor_tensor(
                out=y_ap_fn(c), in0=y_ap_fn(c), scalar=pp[:, G_(l, gi, c):G_(l, gi, c) + 1],
                in1=r, op0=ALU.mult, op1=ALU.mult), reads=[yBuf(c), rstdB[par], ppB], writes=[yBuf(c)])
        lock_release("norm")
        S.op("dve", lambda e: e.tensor_tensor(out=xres[:, :, off:off + n], in0=xres[:, :, off:off + n], in1=y3d, op=ALU.add),
             reads=allb + [xB[s]], writes=[xB[s]], dur=2.3)

    def gelu_chain(z, w, out, zb, wb, outb, out_wait=None):
        S.op("act", lambda e: e.activation(out=w, in_=z, func=AF.Square), reads=[zb], writes=[wb]); yield
        S.op("dve", lambda e: e.scalar_tensor_tensor(out=w, in0=w, scalar=1.0 / 0.044715, in1=z, op0=ALU.add, op1=ALU.mult),
             reads=[wb, zb], writes=[wb]); yield
        S.op("act", lambda e: e.activation(out=w, in_=w, func=AF.Exp, scale=-2.0 * GELU_C * 0.044715), reads=[wb], writes=[wb]); yield
        S.op("act", lambda e: e.activation(out=w, in_=w, func=AF.Ln, bias=1.0), reads=[wb], writes=[wb]); yield
        S.op("act", lambda e: e.activation(out=w, in_=w, func=AF.Exp, scale=-1.0), reads=[wb], writes=[wb]); yield
        if out_wait:
            yield ("wait", out_wait)
        S.op("dve", lambda e: e.tensor_tensor(out=out, in0=z, in1=w, op=ALU.mult), reads=[zb, wb], writes=[outb]); yield

    def kv_prologue(li, l):
        jk, jv = kv_loads[li]
        hkv = [hB[0]]
        S.dma("sp", yT[:, :, :], memT.rearrange("c p t -> p c t"), writes=[yTB])
        r = norm_stats(yT[:, :, :], [yTB], 256, 0)
        for c in range(8):
            S.op("dve", lambda e, c=c: e.scalar_tensor_tensor(
                out=hT[:, c, 0:256], in0=yT[:, c, :], scalar=pp[:, G_(l, 6, c):G_(l, 6, c) + 1], in1=r,
                op0=ALU.mult, op1=ALU.mult), reads=[yTB, rstdB[0], ppB], writes=hkv)
        Rk, RkB = ring[jk % 4], ringB[jk % 4]
        grouped(list(range(8)),
                lambda m, bB, ap: S.mm(bB, ap, [(Rk[:, k, m * 128:(m + 1) * 128], hT[:, k, 0:256]) for k in range(8)], reads=[RkB] + hkv),
                lambda m, bB, ap, gi: S.op("act", lambda e: e.activation(out=kT[:, li, m, :], in_=ap, func=AF.Copy), reads=[bB], writes=[kTB]))
        done(jk)
        Rv, RvB = ring[jv % 4], ringB[jv % 4]
        for mc in range(2):
            for hf in range(2):
                bB, t = bank()
                S.mm(bB, t[:, :], [(hT[:, k, mc * 128:(mc + 1) * 128], Rv[:, k, hf * 512:(hf + 1) * 512]) for k in range(8)],
                     reads=[RvB] + hkv)
                S.op("dve", lambda e, mc=mc, hf=hf, t=t: e.tensor_copy(out=vv[:, li, mc, hf * 512:(hf + 1) * 512], in_=t[:, :]),
                     reads=[bB], writes=[vvB])
                rel(bB)
        done(jv)

    def interleave_gen(gens):
        gens = list(gens)
        while gens:
            nxt = []
            for g in gens:
                try:
                    r = next(g)
                    nxt.append(g)
                    yield r
                except StopIteration:
                    pass
            gens = nxt

    def mixer_body(p, li, l, sub, par, last_layer, prev_key):
        key = f"{p}_{li}_{sub.slot}"
        n, off, s = sub.n, sub.off, sub.slot
        ld = pl_loads[(p, li)]
        RA, RAB = ring[ld["inA"] % 4], ringB[ld["inA"] % 4]
        RB_, RBB = ring[ld["inB"] % 4], ringB[ld["inB"] % 4]
        RO, ROB = ring[ld["out"] % 4], ringB[ld["out"] % 4]
        hrhs, hb_ = hview(sub)
        W = n + 16
        nblk = n // 128

        def pool_s1():
            bB, t = yield from acquire()
            for c in range(2):
                S.mm(bB, t[:, 256 * c:256 * c + n], [(RA[:, k, 128 * c:128 * c + 128], hrhs(k)) for k in range(8)], reads=[RAB, hb_])
            yield
            for c in range(2):
                S.op("act", lambda e, c=c: e.activation(out=zA[:, c, 16:16 + n], in_=t[:, 256 * c:256 * c + n], func=AF.Identity,
                                                       bias=pp[:, BA_(l, c):BA_(l, c) + 1]), reads=[bB, ppB], writes=[zAB[c]])
                if c == 1:
                    rel(bB)
                yield
            for c in range(2):
                S.op("dve", lambda e, c=c: e.tensor_tensor(out=T1[:, 1:W], in0=zA[:, c, 1:W], in1=zA[:, c, 0:W - 1], op=ALU.add),
                     reads=[zAB[c]], writes=[T1B]); yield
                S.op("dve", lambda e: e.tensor_tensor(out=T2[:, 3:W], in0=T1[:, 3:W], in1=T1[:, 1:W - 2], op=ALU.add),
                     reads=[T1B], writes=[T2B]); yield
                if c == 1:
                    S.op("dve", lambda e: e.tensor_tensor(out=T1[:, 7:W], in0=T2[:, 7:W], in1=T2[:, 3:W - 4], op=ALU.add),
                         reads=[T2B], writes=[T1B]); yield
                    S.op("dve", lambda e: e.tensor_tensor(out=T2[:, 15:W], in0=T1[:, 15:W], in1=T1[:, 7:W - 8], op=ALU.add),
                         reads=[T1B], writes=[T2B]); yield
                for (lo, srcT, srcB) in ((0, T1, T1B), (64, T2, T2B)):
                    if prev_key and c == 0 and lo == 0:
                        yield ("wait", prev_key + "_pool")
                    S.op("dve", lambda e, lo=lo, srcT=srcT, c=c: e.scalar_tensor_tensor(
                        out=pb[lo:lo + 64, c, 0:n], in0=srcT[lo:lo + 64, 16:16 + n],
                        scalar=pp[lo:lo + 64, IW_(l, c):IW_(l, c) + 1], in1=zA[lo:lo + 64, c, 16:16 + n],
                        op0=ALU.mult, op1=ALU.subtract), reads=[srcB, zAB[c], ppB], writes=[pbB[c]]); yield
                    if sub.first_real:
                        S.op("dve", lambda e, lo=lo, srcT=srcT, c=c: e.tensor_tensor(
                            out=rl[lo:lo + 64, 0, 0:16], in0=srcT[lo:lo + 64, 16:32], in1=icnt[lo:lo + 64, c, :], op=ALU.mult),
                            reads=[srcB, icntB], writes=[rlB[0]]); yield
                        S.op("dve", lambda e, lo=lo, c=c: e.tensor_tensor(
                            out=pb[lo:lo + 64, c, 0:16], in0=rl[lo:lo + 64, 0, 0:16], in1=zA[lo:lo + 64, c, 16:32], op=ALU.subtract),
                            reads=[rlB[0], zAB[c]], writes=[pbB[c]]); yield
                if sub.is_pre:
                    S.op("dve", lambda e, c=c: e.tensor_scalar(out=zA[:, c, 0:16], in0=zA[:, c, n:n + 16], scalar1=mask[:, 0:1],
                                                              scalar2=None, op0=ALU.mult), reads=[zAB[c], maskB], writes=[zAB[c]])
                else:
                    S.op("dve", lambda e, c=c: e.tensor_copy(out=zA[:, c, 0:16], in_=zA[:, c, n:n + 16]), reads=[zAB[c]], writes=[zAB[c]])
                yield

        def pool_s2():
            bB2, t2 = yield from acquire()
            for c in range(2):
                S.mm(bB2, t2[:, 256 * c:256 * c + n], [(bd[:, li, c, :], pb[:, c, 0:n])], reads=[bdB, pbB[c]])
            S.set_flag(key + "_pool")
            yield
            for c in range(2):
                S.op("act", lambda e, c=c: e.activation(out=ya[:, c, 0:n], in_=t2[:, 256 * c:256 * c + n], func=AF.Identity,
                                                       scale=pp[:, PSC_(l, c):PSC_(l, c) + 1]), reads=[bB2, ppB], writes=[yaB[c]])
                if c == 1:
                    rel(bB2)
                yield

        def u_pair(h0):
            bB, t = yield from acquire()
            for j in range(2):
                h = h0 + j
                S.mm(bB, t[0:96, 256 * j:256 * j + n], [(RA[:, k, 256 + 96 * h:256 + 96 * h + 96], hrhs(k)) for k in range(8)], reads=[RAB, hb_])
            yield

            for j in range(2):
                S.op("act", lambda e, j=j: e.activation(out=tmp[0:96, 2 * j, 0:n], in_=t[0:96, 256 * j:256 * j + n], func=AF.Identity,
                                                       bias=pp[0:96, BU_(l, h0 + j):BU_(l, h0 + j) + 1]),
                     reads=[bB, ppB], writes=[tmpB[2 * j]])
            rel(bB)
            yield

            def chain(j):
                h = h0 + j
                ts = 2 * j
                z = tmp[0:96, ts, 0:n]
                w = tmp[0:96, ts + 1, 0:n]
                yield from gelu_chain(z, w, ub[0:96, h, 0:n], tmpB[ts], tmpB[ts + 1], ubB[h], out_wait=(prev_key + "_sg") if prev_key else None)

            yield from interleave_gen([chain(0), chain(1)])

        def v_chain(b, ts):
            bB, t = yield from acquire()
            S.mm(bB, t[:, 0:384], [(hrhs(k)[:, 128 * b:128 * b + 128], RA[:, k, 640:1024]) for k in range(8)],
                 reads=[RAB, hb_])
            yield
            z = tmp[:, ts, :]
            w = tmp[:, ts + 1, :]
            S.op("dve", lambda e: e.tensor_tensor(out=z, in0=t[:, 0:384], in1=rowb[:, 0, :], op=ALU.add),
                 reads=[bB, rowbB], writes=[tmpB[ts]])
            rel(bB)
            yield
            yield from gelu_chain(z, w, z, tmpB[ts], tmpB[ts + 1], tmpB[ts])
            smb = smB[b % 2]
            smt = sm[:, b % 2, :]
            S.op("dve", lambda e: e.bn_stats(out=smt[:, 0:6], in_=z), reads=[tmpB[ts]], writes=[smb]); yield
            S.op("dve", lambda e: e.bn_aggr(out=smt[:, 8:10], in_=smt[:, 0:6]), reads=[smb], writes=[smb]); yield
            S.op("act", lambda e: e.activation(out=smt[:, 10:11], in_=smt[:, 9:10], func=AF.Ln, bias=EPS), reads=[smb], writes=[smb]); yield
            S.op("act", lambda e: e.activation(out=smt[:, 10:11], in_=smt[:, 10:11], func=AF.Exp, scale=-0.5), reads=[smb], writes=[smb]); yield
            S.op("dve", lambda e: e.tensor_scalar(out=z, in0=z, scalar1=smt[:, 8:9], scalar2=smt[:, 10:11], op0=ALU.subtract, op1=ALU.mult),
                 reads=[tmpB[ts], smb], writes=[tmpB[ts]]); yield
            S.op("dve", lambda e: e.tensor_tensor(out=z, in0=z, in1=rowb[:, 1, :], op=ALU.mult), reads=[tmpB[ts], rowbB], writes=[tmpB[ts]]); yield
            if prev_key:
                yield ("wait", prev_key + "_sgmm")
            S.op("dve", lambda e: e.tensor_tensor(out=vln[:, b, :], in0=z, in1=rowb[:, 2, :], op=ALU.add), reads=[tmpB[ts], rowbB], writes=[vlnB[b]]); yield

        def sg_s1():
            yield from u_pair(0)
            yield from u_pair(2)
            yield from interleave_gen([v_chain(b, 2 * b) for b in range(nblk)])

        def sg_s2():
            for h0 in (0, 2):
                bB, t = yield from acquire()
                for j in range(2):
                    h = h0 + j
                    for b in range(nblk):
                        S.mm(bB, t[0:96, 256 * j + 128 * b:256 * j + 128 * b + 128],
                             [(vln[:, b, 96 * h:96 * h + 96], sgw[:, li, h, :]),
                              (ones[0:1, 0:96], sgbr[0:1, li * 512 + 128 * h:li * 512 + 128 * h + 128])],
                             reads=[vlnB[b], sgwB, onesB, sgbrB])
                if h0 == 2:
                    S.set_flag(key + "_sgmm")
                yield
                for j in range(2):
                    h = h0 + j
                    S.op("dve", lambda e, h=h, j=j: e.tensor_tensor(out=yb[0:96, h, 0:n], in0=t[0:96, 256 * j:256 * j + n], in1=ub[0:96, h, 0:n], op=ALU.mult),
                         reads=[bB, ubB[h]], writes=[ybB[h]])
                    if h == 3:
                        S.set_flag(key + "_sg")
                    if j == 1:
                        rel(bB)
                    yield

        def glu_chain(c):
            bB, t = yield from acquire()
            pa, pg = t[:, 0:256], t[:, 256:512]
            S.mm(bB, pa[:, 0:n], [(RB_[:, k, 128 * c:128 * c + 128], hrhs(k)) for k in range(8)], reads=[RBB, hb_])
            S.mm(bB, pg[:, 0:n], [(RB_[:, k, 384 + 128 * c:384 + 128 * c + 128], hrhs(k)) for k in range(8)], reads=[RBB, hb_])
            yield
            w = gs[:, c, 0:n]
            S.op("act", lambda e: e.activation(out=w, in_=pg[:, 0:n], func=AF.Exp, scale=-1.0, bias=ppn[:, 9 * l + c:9 * l + c + 1]),
                 reads=[bB, ppnB], writes=[gsB[c]]); yield
            S.op("act", lambda e: e.activation(out=w, in_=w, func=AF.Ln, bias=1.0), reads=[gsB[c]], writes=[gsB[c]]); yield
            S.op("act", lambda e: e.activation(out=w, in_=w, func=AF.Exp, scale=-1.0), reads=[gsB[c]], writes=[gsB[c]]); yield
            if prev_key:
                yield ("wait", prev_key + "_conv")
            S.op("dve", lambda e: e.scalar_tensor_tensor(out=hbuf[:, c, 32:32 + n], in0=pa[:, 0:n],
                                                        scalar=pp[:, BCA_(l, c):BCA_(l, c) + 1], in1=w, op0=ALU.add, op1=ALU.mult),
                 reads=[bB, gsB[c], ppB], writes=[hbufB[c]])
            rel(bB)
            yield

        def conv_mm(c):
            bB, t = yield from acquire()
            pc = t[:, 0:n]
            S.mm(bB, pc, [(dg[:, 31 * c + k, :], hbuf[:, c, 2 + k:2 + k + n]) for k in range(CONV_K)], reads=[dgB[c], hbufB[c]])
            if sub.is_pre:
                S.op("dve", lambda e: e.tensor_scalar(out=hbuf[:, c, 0:32], in0=hbuf[:, c, n:n + 32], scalar1=mask[:, 0:1],
                                                     scalar2=None, op0=ALU.mult), reads=[hbufB[c], maskB], writes=[hbufB[c]])
            else:
                S.op("dve", lambda e: e.tensor_copy(out=hbuf[:, c, 0:32], in_=hbuf[:, c, n:n + 32]), reads=[hbufB[c]], writes=[hbufB[c]])
            return bB, pc

        def conv_chunk(c, bB, pc):
            cbcol = pp[:, CB_(l, c):CB_(l, c) + 1]
            S.op("act", lambda e: e.activation(out=cc[:, c, 0:n], in_=pc, func=AF.Identity, bias=cbcol),
                 reads=[bB, ppB], writes=[ccB[c]])
            rel(bB)
            yield
            S.op("act", lambda e: e.activation(out=sq[:, c, 0:n], in_=cc[:, c, 0:n], func=AF.Copy), reads=[ccB[c]], writes=[sqB]); yield
            S.op("act", lambda e: e.activation(out=sq[:, 3 + c, 0:n], in_=cc[:, c, 0:n], func=AF.Square),
                 reads=[ccB[c]], writes=[sqB]); yield

        mt = st[:, 2, 0:n]
        vt = st[:, 3, 0:n]

        def silu_chain(c):
            t_ = cc[:, c, 0:n]
            w = yT[:, c, 0:n]
            S.op("dve", lambda e: e.tensor_tensor(out=t_, in0=t_, in1=mt, op=ALU.subtract), reads=[ccB[c], stmB], writes=[ccB[c]]); yield
            S.op("dve", lambda e: e.tensor_tensor(out=t_, in0=t_, in1=vt, op=ALU.mult), reads=[ccB[c], stvB], writes=[ccB[c]]); yield
            S.op("act", lambda e: e.activation(out=w, in_=t_, func=AF.Exp, scale=ppn[:, 9 * l + 3 + c:9 * l + 4 + c],
                                               bias=ppn[:, 9 * l + 6 + c:9 * l + 7 + c]), reads=[ccB[c], ppnB], writes=[yTcB[c]]); yield
            S.op("dve", lambda e: e.tensor_scalar(out=t_, in0=t_, scalar1=pp[:, CLG_(l, c):CLG_(l, c) + 1],
                                                 scalar2=pp[:, CLB_(l, c):CLB_(l, c) + 1], op0=ALU.mult, op1=ALU.add),
                 reads=[ccB[c], ppB], writes=[ccB[c]]); yield
            S.op("act", lambda e: e.activation(out=w, in_=w, func=AF.Ln, bias=1.0), reads=[yTcB[c]], writes=[yTcB[c]]); yield
            S.op("act", lambda e: e.activation(out=w, in_=w, func=AF.Exp, scale=-1.0), reads=[yTcB[c]], writes=[yTcB[c]]); yield
            S.op("dve", lambda e: e.tensor_tensor(out=yc[:, c, 0:n], in0=t_, in1=w, op=ALU.mult), reads=[ccB[c], yTcB[c]], writes=[ycB[c]]); yield

        def conv_s1():
            yield from interleave_gen([glu_chain(c) for c in range(3)])

        def conv_s2():
            cm = []
            for c in range(3):
                cm.append((yield from conv_mm(c)))
            S.set_flag(key + "_conv")
            yield
            yield from lock_acquire("norm")
            yield from interleave_gen([conv_chunk(c, cm[c][0], cm[c][1]) for c in range(3)])
            bS, tS = yield from acquire()
            ps_, pq = tS[:, 0:n], tS[:, 256:256 + n]
            S.mm(bS, ps_, [(ones[:, :], sq[:, c, 0:n]) for c in range(3)], reads=[sqB, onesB])
            S.mm(bS, pq, [(ones[:, :], sq[:, 3 + c, 0:n]) for c in range(3)], reads=[sqB, onesB])
            lock_release("norm")
            yield
            S.op("act", lambda e: e.activation(out=mt, in_=ps_, func=AF.Identity, scale=1.0 / W_C), reads=[bS], writes=[stmB]); yield
            S.op("dve", lambda e: e.tensor_tensor(out=vt, in0=mt, in1=mt, op=ALU.mult), reads=[stmB], writes=[stvB]); yield
            S.op("dve", lambda e: e.scalar_tensor_tensor(out=vt, in0=pq, scalar=1.0 / W_C, in1=vt, op0=ALU.mult, op1=ALU.subtract),
                 reads=[bS, stvB], writes=[stvB])
            rel(bS)
            yield
            S.op("act", lambda e: e.activation(out=vt, in_=vt, func=AF.Ln, bias=EPS), reads=[stvB], writes=[stvB]); yield
            S.op("act", lambda e: e.activation(out=vt, in_=vt, func=AF.Exp, scale=-0.5), reads=[stvB], writes=[stvB]); yield
            yield from interleave_gen([silu_chain(c) for c in range(3)])

        def wout_mm(m, bB, ap):
            ms = slice(m * 128, (m + 1) * 128)
            pairs = [(RO[:, 0, ms], ya[:, 0, 0:n]), (RO[:, 1, ms], ya[:, 1, 0:n])]
            pairs += [(RO[0:96, 2 + h, ms], yb[0:96, h, 0:n]) for h in range(4)]
            pairs += [(RO[:, 6, ms], yc[:, 0, 0:n]), (RO[:, 7, ms], yc[:, 1, 0:n]), (w9[:, ms], yc[:, 2, 0:n])]
            S.mm(bB, ap[:, 0:n], pairs, reads=[ROB, w9B] + yaB + ybB + ycB)

        def s3():
            for m0 in range(0, 8, 2):
                bB, t = yield from acquire()
                for j in range(2):
                    wout_mm(m0 + j, bB, t[:, 256 * j:256 * j + 256])
                yield
                if (m0 // 2) % 2 == 0:
                    S.op("act", lambda e: e.activation(out=yT[:, m0:m0 + 2, 0:n], in_=t[:, :].rearrange("p (a t) -> p a t", a=2)[:, :, 0:n], func=AF.Copy),
                         reads=[bB], writes=[yTcB[m0], yTcB[m0 + 1]])
                else:
                    S.op("dve", lambda e: e.tensor_copy(out=yT[:, m0:m0 + 2, 0:n], in_=t[:, :].rearrange("p (a t) -> p a t", a=2)[:, :, 0:n]),
                         reads=[bB], writes=[yTcB[m0], yTcB[m0 + 1]])
                rel(bB)
                yield
            yield from post_norm_gen(l, 1, sub, par, lambda c: yT[:, c, 0:n], lambda c: yTcB[c], yT[:, :, 0:n])
            yield

        def with_done(gen, flag):
            yield from gen
            S.set_flag(flag)
            yield

        def s3_stream():
            for f_ in ("_dconv", "_dsg", "_dpool"):
                yield ("wait", key + f_)
            yield from s3()

        s1_streams = [glu_chain(c) for c in range(3)] + [sg_s1(), pool_s1()]
        s23_streams = [with_done(conv_s2(), key + "_dconv"), with_done(sg_s2(), key + "_dsg"), with_done(pool_s2(), key + "_dpool")]
        if not (sub.is_pre and last_layer):
            s23_streams.append(s3_stream())
        return s1_streams, s23_streams

    def delayed(gen, k):
        for _ in range(k):
            yield
        yield from gen

    qbufs = [(qT, qTcB), (uT, q2cB)]

    def attn_q_gen(p, li, l, sub, qsel):
        n = sub.n
        ld = pl_loads[(p, li)]
        RQ, RQB = ring[ld["wq"] % 4], ringB[ld["wq"] % 4]
        hv, hb_ = hview(sub)
        qT, qTcB = qbufs[qsel]
        for m in range(0, 8, 2):
            bB, t = yield from acquire()
            for j in range(2):
                S.mm(bB, t[:, 256 * j:256 * j + n], [(RQ[:, k, (m + j) * 128:(m + j + 1) * 128], hv(k)) for k in range(8)], reads=[RQB, hb_])
            yield
            ap3 = t[:, :].rearrange("p (a t) -> p a t", a=2)[:, :, 0:n]
            if (m // 2) % 2 == 0:
                S.op("act", lambda e, m=m, ap3=ap3: e.activation(out=qT[:, m:m + 2, 0:n], in_=ap3, func=AF.Copy), reads=[bB], writes=[qTcB[m // 2]], dur=0.6)
            else:
                S.op("dve", lambda e, m=m, ap3=ap3: e.tensor_copy(out=qT[:, m:m + 2, 0:n], in_=ap3), reads=[bB], writes=[qTcB[m // 2]], dur=0.6)
            rel(bB)
            yield

    def attn_main_gen(p, li, l, sub, par, qsel):
        n, off, s = sub.n, sub.off, sub.slot
        ld = pl_loads[(p, li)]
        RO, ROB = ring[ld["wo"] % 4], ringB[ld["wo"] % 4]
        qT, qTcB = qbufs[qsel]

        def head_gen(h):
            hp = h % 2
            bB, t = yield from acquire()
            for mc in range(2):
                S.mm(bB, t[:, 256 * mc:256 * mc + n], [(kT[:, li, 2 * h + dc, mc * 128:(mc + 1) * 128], qT[:, 2 * h + dc, 0:n]) for dc in range(2)],
                     reads=[kTB, qTcB[h]])
            yield
            S.op("act", lambda e: e.activation(out=ET[:, hp, :, 0:n], in_=t[:, :].rearrange("p (a t) -> p a t", a=2)[:, :, 0:n],
                                               func=AF.Exp, scale=1.0 / 16.0), reads=[bB], writes=[ETB[hp]])
            rel(bB)
            yield
            bS, tS = yield from acquire()
            S.mm(bS, tS[:, 0:n], [(ones[:, :], ET[:, hp, 0, 0:n]), (ones[:, :], ET[:, hp, 1, 0:n])], reads=[onesB, ETB[hp]])
            bO, tO = yield from acquire()
            for dc in range(2):
                S.mm(bO, tO[:, 256 * dc:256 * dc + n], [(vv[:, li, mc, (2 * h + dc) * 128:(2 * h + dc + 1) * 128], ET[:, hp, mc, 0:n]) for mc in range(2)],
                     reads=[vvB, ETB[hp]])
            yield
            S.op("act", lambda e: e.activation(out=rec[:, hp, 0:n], in_=tS[:, 0:n], func=AF.Ln), reads=[bS], writes=[recB[hp]])
            rel(bS)
            yield
            S.op("act", lambda e: e.activation(out=rec[:, hp, 0:n], in_=rec[:, hp, 0:n], func=AF.Exp, scale=-1.0), reads=[recB[hp]], writes=[recB[hp]]); yield
            for dc in range(2):
                S.op("dve", lambda e, dc=dc: e.tensor_tensor(out=oT[:, 2 * h + dc, 0:n], in0=tO[:, 256 * dc:256 * dc + n], in1=rec[:, hp, 0:n], op=ALU.mult),
                     reads=[bO, recB[hp]], writes=[oTB])
                if dc == 1:
                    rel(bO)
                yield

        yield from interleave_gen([head_gen(0), head_gen(1)])
        yield from interleave_gen([head_gen(2), head_gen(3)])
        for m in range(0, 8, 2):
            bB, t = yield from acquire()
            for j in range(2):
                S.mm(bB, t[:, 256 * j:256 * j + n], [(RO[:, k, (m + j) * 128:(m + j + 1) * 128], oT[:, k, 0:n]) for k in range(8)], reads=[ROB, oTB])
            yield
            ap3 = t[:, :].rearrange("p (a t) -> p a t", a=2)[:, :, 0:n]
            if (m // 2) % 2 == 0:
                S.op("act", lambda e, m=m, ap3=ap3: e.activation(out=yT[:, m:m + 2, 0:n], in_=ap3, func=AF.Copy), reads=[bB], writes=[yTcB[m], yTcB[m + 1]], dur=0.6)
            else:
                S.op("dve", lambda e, m=m, ap3=ap3: e.tensor_copy(out=yT[:, m:m + 2, 0:n], in_=ap3), reads=[bB], writes=[yTcB[m], yTcB[m + 1]], dur=0.6)
            rel(bB)
            yield
        yield from post_norm_gen(l, 3, sub, par, lambda c: yT[:, c, 0:n], lambda c: yTcB[c], yT[:, :, 0:n])
        yield


    def ffn_w1(p, li, l, q, sub, ubuf):
        n, off, s = sub.n, sub.off, sub.slot
        ld = pl_loads[(p, li)]
        R1, R1B = ring[ld[("w1", q)] % 4], ringB[ld[("w1", q)] % 4]
        uT, uTB = ubuf

        def u_evac(f, bB, ap3, gi):
            S.op("act", lambda e: e.activation(out=rl[:, :, 0:n], in_=ap3, func=AF.Relu), reads=[bB], writes=rlB)
            S.op("dve", lambda e: e.tensor_tensor(out=uT[:, f:f + 2, 0:n], in0=rl[:, :, 0:n], in1=rl[:, :, 0:n], op=ALU.mult),
                 reads=rlB, writes=[uTB])

        grouped2(list(range(8)),
                 lambda f, bB, ap: S.mm(bB, ap[:, 0:n], [(R1[:, k, f * 128:(f + 1) * 128], hTf[:, k, off:off + n]) for k in range(8)], reads=[R1B, hfB[s]]),
                 u_evac, n)

    def ffn_w2(p, li, l, q, sub, ubuf):
        n, off, s = sub.n, sub.off, sub.slot
        ld = pl_loads[(p, li)]
        R2, R2B = ring[ld[("w2", q)] % 4], ringB[ld[("w2", q)] % 4]
        uT, uTB = ubuf

        def y_evac(m, bB, ap3, gi):
            if q == 0:
                S.op("act", lambda e: e.activation(out=yacc[:, m:m + 2, off:off + n], in_=ap3, func=AF.Copy),
                     reads=[bB], writes=[yaccB[s]])
            else:
                S.op("dve", lambda e: e.tensor_tensor(out=yacc[:, m:m + 2, off:off + n], in0=ap3, in1=yacc[:, m:m + 2, off:off + n], op=ALU.add),
                     reads=[bB, yaccB[s]], writes=[yaccB[s]])

        grouped2(list(range(8)),
                 lambda m, bB, ap: S.mm(bB, ap[:, 0:n], [(R2[:, f, m * 128:(m + 1) * 128], uT[:, f, 0:n]) for f in range(8)], reads=[R2B, uTB]),
                 y_evac, n)


    import os as _os
    MAXPH = int(_os.environ.get("K_MAXPH", "100000"))
    phc = [0]

    def ph_ok():
        phc[0] += 1
        return phc[0] <= MAXPH


    S.alias_barrier([yTB], yTcB)
    stored = {}
    par_ctr = [0]

    def npar():
        par_ctr[0] += 1
        return par_ctr[0] % 2

    for p in range(NPASS):
        subs = []
        if p == 0:
            subs.append(SubT(0, 0, HALO, True, False, 0))
        for i in range(NSUBP):
            subs.append(SubT(1 + i, HALO + SUB * i, SUB, False, (p == 0 and i == 0), HALO + p * PASS_T + SUB * i))
        if p == 0:
            for sub in subs:
                S.dma("sp", xres[:, :, sub.off:sub.off + sub.n], xT[:, :, sub.dram0:sub.dram0 + sub.n].rearrange("c p t -> p c t"),
                      writes=[xB[sub.slot]])
        for li, l in enumerate(LAYERS):
            last_layer = (li == NLY - 1)
            ld = pl_loads[(p, li)]
            for r_ in range(3):
                S.dma("sp", rowb[:, r_, :], rowp_d[l, r_:r_ + 1, :].partition_broadcast(128), writes=[rowbB])
            if not ph_ok():
                continue
            S.alias_barrier(yaccB + hfB, mixer_bufs)
            if p == 0:
                S.op("dve", lambda e: e.memset(zA[:, :, 0:16], 0.0), writes=zAB)
                S.op("dve", lambda e: e.memset(hbuf[:, :, 0:32], 0.0), writes=hbufB)
            else:
                S.op("dve", lambda e: e.tensor_copy(out=zA[:, :, 0:16], in_=stZ[:, li, :, :]), reads=[stZB], writes=zAB)
                S.op("dve", lambda e: e.tensor_copy(out=hbuf[:, :, 0:32], in_=stH[:, li, :, :]), reads=[stHB], writes=hbufB)
            pars = [npar() for _ in subs]
            pre_norm(l, 0, subs[0], pars[0])
            for c_ in range(3):
                a_ = CW_(l, c_, 0)
                S.op("dve", lambda e, c_=c_, a_=a_: e.tensor_tensor(
                    out=dg[:, 31 * c_:31 * c_ + 31, :],
                    in0=ident[:, :].unsqueeze(1).broadcast_to([128, CONV_K, 128]),
                    in1=pp[:, a_:a_ + CONV_K].unsqueeze(2).broadcast_to([128, CONV_K, 128]), op=ALU.mult),
                    reads=[identB, ppB], writes=[dgB[c_]], dur=4.5)
            S.alias_barrier([uTB], gsB)
            if len(subs) > 1:
                pre_norm(l, 0, subs[1], pars[1])
            stages = [mixer_body(p, li, l, sub, pars[i], last_layer, (f"{p}_{li}_{subs[i - 1].slot}" if i > 0 else None))
                      for i, sub in enumerate(subs)]
            run_scheduled(S, stages[0][0])
            if len(subs) == 1:
                done(ld["inA"]); done(ld["inB"])
            for i, sub in enumerate(subs):
                g_ = list(stages[i][1])
                if i + 1 < len(subs):
                    g_ = stages[i + 1][0] + g_
                if i + 2 < len(subs):
                    g_.append(pre_norm_gen(l, 0, subs[i + 2], pars[i + 2]))
                run_scheduled(S, g_)
                if i + 1 == len(subs) - 1:
                    done(ld["inA"]); done(ld["inB"])
            S.op("dve", lambda e: e.tensor_copy(out=stZ[:, li, :, :], in_=zA[:, :, 0:16]), reads=zAB, writes=[stZB])
            S.op("dve", lambda e: e.tensor_copy(out=stH[:, li, :, :], in_=hbuf[:, :, 0:32]), reads=hbufB, writes=[stHB])
            done(ld["out"])
            asubs = [sb_ for sb_ in subs if not (sb_.is_pre and last_layer)]
            if not ph_ok():
                continue
            if p == 0:
                S.alias_barrier(yTcB, [yTB])
                kv_prologue(li, l)
                S.alias_barrier([yTB], yTcB)
            S.alias_barrier(attn_alias_src, attn_bufs)
            S.alias_barrier(mixer_bufs, hfB)
            S.alias_barrier([uTB] + gsB, q2cB)
            pars = [npar() for _ in asubs]
            pre_norm(l, 2, asubs[0], pars[0])
            if len(asubs) > 1:
                pre_norm(l, 2, asubs[1], pars[1])
            run_scheduled(S, [attn_q_gen(p, li, l, asubs[0], 0)])

            def chain2(a, b):
                yield from a
                yield from b

            for i, sub in enumerate(asubs):
                g_ = [attn_main_gen(p, li, l, sub, pars[i], i % 2)]
                if i + 1 < len(asubs):
                    qg = attn_q_gen(p, li, l, asubs[i + 1], (i + 1) % 2)
                    if i + 1 >= 2:
                        qg = chain2(pre_norm_gen(l, 2, asubs[i + 1], pars[i + 1]), qg)
                    g_.append(qg)
                if i >= 1:
                    g_.append(pre_norm_gen(l, 4, asubs[i - 1], npar(), True))
                run_scheduled(S, g_)
            done(ld["wq"]); done(ld["wo"])
            S.alias_barrier(attn_bufs, attn_alias_src)
            if not ph_ok():
                continue
            S.alias_barrier(mixer_bufs, yaccB + hfB)
            S.alias_barrier(gsB + q2cB, [uTB])
            pre_norm(l, 4, asubs[-1], npar(), ffn=True)
            items = [(q, sub) for q in range(4) for sub in asubs]
            ubufs = [(uT, uTB), (hT[:, :, 0:SUB], hB[0])]

            def ub_of(ix):
                return ubufs[ix % 2]

            ffn_w1(p, li, l, items[0][0], items[0][1], ub_of(0))
            for ix, (q, sub) in enumerate(items):
                last_of_q = (sub is asubs[-1])
                if last_of_q:
                    done(ld[("w1", q)])
                nxt = items[ix + 1] if ix + 1 < len(items) else None
                pipelined = nxt is not None
                if pipelined:
                    ffn_w1(p, li, l, nxt[0], nxt[1], ub_of(ix + 1))
                ffn_w2(p, li, l, q, sub, ub_of(ix))
                if last_of_q:
                    done(ld[("w2", q)])
                if True:
                    if q == 3:
                        post_norm_residual(l, 5, sub, npar(), lambda c, sub=sub: yacc[:, c, sub.off:sub.off + sub.n], lambda c, sub=sub: yaccB[sub.slot], yacc[:, :, sub.off:sub.off + sub.n])
                        if last_layer and not sub.is_pre and not stored.get((p, sub.slot)):
                            stored[(p, sub.slot)] = True
                            o0 = sub.dram0 - HALO
                            S.dma("sp", outT[:, :, o0:o0 + sub.n].rearrange("c p t -> p c t"), xres[:, :, sub.off:sub.off + sub.n], reads=[xB[sub.slot]])
                            if p + 1 < NPASS:
                                d1 = sub.dram0 + PASS_T
                                S.dma("sp", xres[:, :, sub.off:sub.off + sub.n], xT[:, :, d1:d1 + sub.n].rearrange("c p t -> p c t"),
                                      writes=[xB[sub.slot]])
                if nxt is not None and not pipelined:
                    ffn_w1(p, li, l, nxt[0], nxt[1], ub_of(ix + 1))
        for sub in subs:
            if sub.is_pre or stored.get((p, sub.slot)):
                continue
            o0 = sub.dram0 - HALO
            S.dma("sp", outT[:, :, o0:o0 + sub.n].rearrange("c p t -> p c t"), xres[:, :, sub.off:sub.off + sub.n], reads=[xB[sub.slot]])
    S.final_wait("sp", S.allb)
    return nc


def _pack_pp(inp):
    pp = np.zeros((128, DEPTH * NL), np.float32)

    def col(v):
        return np.ascontiguousarray(v.reshape(-1, 128).T)

    wins = (2, 4, 8, 16)
    for l in range(DEPTH):
        for i, name in enumerate(["pre_mix_g", "post_mix_g", "pre_x_g", "post_x_g", "pre_ff_g", "post_ff_g", "mem_g"]):
            pp[:, G_(l, i, 0):G_(l, i, 0) + 8] = col(inp[name][l])
        b = inp["b_in"][l]
        pp[:, BA_(l, 0):BA_(l, 0) + 2] = col(b[0:256])
        for h in range(4):
            pp[0:96, BU_(l, h)] = b[256 + 96 * h:256 + 96 * h + 96]
        pp[:, BCA_(l, 0):BCA_(l, 0) + 3] = col(b[1024:1408])
        pp[:, BCG_(l, 0):BCG_(l, 0) + 3] = col(b[1408:1792])
        pp[:, PSC_(l, 0):PSC_(l, 0) + 2] = col(inp["pool_scale"][l])
        for c in range(2):
            for hh in range(2):
                pp[64 * hh:64 * hh + 64, IW_(l, c)] = np.float32(1.0) / np.float32(wins[2 * c + hh])
        pp[:, CB_(l, 0):CB_(l, 0) + 3] = col(inp["conv_b"][l])
        pp[:, CLG_(l, 0):CLG_(l, 0) + 3] = col(inp["conv_ln_g"][l])
        pp[:, CLB_(l, 0):CLB_(l, 0) + 3] = col(inp["conv_ln_b"][l])
        cw = inp["conv_w"][l]
        for c in range(3):
            pp[:, CW_(l, c, 0):CW_(l, c, 0) + 31] = cw[:, 128 * c:128 * c + 128].T
    return pp


def _common_inputs(inp):
    f = lambda a: np.ascontiguousarray(np.asarray(a, dtype=np.float32))
    com = {}
    com["pp"] = _pack_pp({k: np.asarray(v, np.float32) for k, v in inp.items()})
    rowp = np.zeros((DEPTH, 3, 384), np.float32)
    for l in range(DEPTH):
        rowp[l, 0] = inp["b_in"][l][640:1024]
        rowp[l, 1] = inp["sg_ln_g"][l]
        rowp[l, 2] = inp["sg_ln_b"][l]
    com["rowp"] = rowp
    com["sgb"] = f(np.asarray(inp["sg_b"]).reshape(DEPTH, 512))
    com["sgwT"] = f(np.transpose(np.asarray(inp["sg_w"]), (0, 1, 3, 2)))
    com["poolw"] = f(inp["pool_w"])
    com["ident"] = np.eye(128, dtype=np.float32)
    com["triu"] = np.triu(np.ones((128, 128), np.float32))
    for k in ("w_in", "w_out", "wq", "wk", "wv", "wo", "w_ff1", "w_ff2"):
        com[k] = f(inp[k])
    return com


def _core_inputs(x, mem, c):
    b, j = c // 4, c % 4
    t0 = j * TPC
    own = x[b, t0:t0 + TPC]
    halo = x[b, t0 - HALO:t0] if j > 0 else np.zeros((HALO, D), np.float32)
    xf = np.concatenate([halo, own], axis=0)
    d = {}
    d["xT"] = np.ascontiguousarray(xf.T).reshape(8, 128, HALO + TPC)
    d["memT"] = np.ascontiguousarray(mem[b].T).reshape(8, 128, 256)
    d["mask"] = np.full((128, 1), 1.0 if j > 0 else 0.0, np.float32)
    wins = (2, 4, 8, 16)
    ic = np.zeros((128, 2, 16), np.float32)
    tpos = np.arange(16, dtype=np.float32) + 1.0
    for cc_ in range(2):
        for hh in range(2):
            w = np.float32(wins[2 * cc_ + hh])
            cnt = np.minimum(tpos, w) if j == 0 else np.full(16, w, np.float32)
            ic[64 * hh:64 * hh + 64, cc_, :] = (np.float32(1.0) / cnt)[None, :]
    d["icnt"] = ic
    return d


FUSED = True
_PROG = {}


def _get_prog(layers):
    key = tuple(layers)
    if key not in _PROG:
        _PROG[key] = build_program(list(layers))
    return _PROG[key]


def _run(layers, x, mem, com):
    nc = _get_prog(layers)
    in_maps = []
    for c in range(NCORE):
        d = dict(com)
        d.update(_core_inputs(x, mem, c))
        in_maps.append(d)
    res = run_bass_kernel_spmd(nc, in_maps, core_ids=list(range(NCORE)))
    out = np.empty((BATCH, SEQ, D), np.float32)
    for c in range(NCORE):
        b, j = c // 4, c % 4
        o = np.asarray(res.results[c]["outT"]).reshape(D, TPC)
        out[b, j * TPC:(j + 1) * TPC] = o.T
    return out


def kernel(**inputs):
    inp = {k: np.asarray(v) for k, v in inputs.items()}
    x = np.asarray(inp["x"], np.float32)
    mem = np.asarray(inp["mem"], np.float32)
    com = _common_inputs(inp)
    if FUSED:
        return _run(range(DEPTH), x, mem, com)
    for l in range(DEPTH):
        x = _run([l], x, mem, com)
    return x
```

```python
import numpy as np
import concourse.bass as bass
import concourse.mybir as mybir
from concourse.bass_utils import run_bass_kernel_spmd

F32 = mybir.dt.float32
BF16 = mybir.dt.bfloat16
AF = mybir.ActivationFunctionType
ALU = mybir.AluOpType

D = 1024
SEQ = 16384
BATCH = 2
NCORE = 8
TPC = 4096
HALO = 128
SUB = 256
NSUBP = 4
PASS_T = SUB * NSUBP
NPASS = TPC // PASS_T
RES_T = HALO + PASS_T
DEPTH = 2
EPS = 1e-6
W_A, W_B, W_C = 256, 384, 384
D_IN = 1792
D_FF = 4096
CONV_K = 31
GELU_C = 0.7978845608028654

NL = 174
def G_(l, i, c): return l * NL + 8 * i + c
def BA_(l, c): return l * NL + 56 + c
def BU_(l, h): return l * NL + 58 + h
def BCA_(l, c): return l * NL + 62 + c
def BCG_(l, c): return l * NL + 65 + c
def PSC_(l, c): return l * NL + 68 + c
def IW_(l, c): return l * NL + 70 + c
def CB_(l, c): return l * NL + 72 + c
def CLG_(l, c): return l * NL + 75 + c
def CLB_(l, c): return l * NL + 78 + c
def CW_(l, c, k): return l * NL + 81 + c * 31 + k


class Buf:
    __slots__ = ("name", "last_w", "readers", "dsem", "dcnt", "excl")

    def __init__(self, name):
        self.name = name
        self.last_w = None
        self.readers = {}
        self.dsem = None
        self.dcnt = 0
        self.excl = False


class Sched:
    def __init__(self, nc):
        self.nc = nc
        self.eng = {"pe": nc.tensor, "act": nc.scalar, "dve": nc.vector, "pool": nc.gpsimd, "sp": nc.sync}
        self.semh = {}
        self.cnt = {}
        self.waited = {k: {} for k in self.eng}
        for k in self.eng:
            self.semh["e_" + k] = nc.alloc_semaphore("e_" + k)
            self.cnt[k] = 0
        self.nbuf = 0
        self.allb = []
        self.defer = False
        self.pend = []
        self.flags = set()
        self.eng_free = {k: 0.0 for k in self.eng}
        self.ev_t = {}

    def buf(self, name):
        self.nbuf += 1
        b = Buf(f"{name}_{self.nbuf}")
        self.allb.append(b)
        return b

    def _needs(self, e, reads, writes):
        own = "e_" + e
        need = {}

        def add(ev, allow_own):
            if ev is None:
                return
            k, v = ev
            if k == own and not allow_own:
                return
            if need.get(k, 0) < v:
                need[k] = v

        for b in reads:
            add(b.last_w, True)
            if b.excl:
                for k, v in b.readers.items():
                    add((k, v), False)
        for b in writes:
            add(b.last_w, False)
            for k, v in b.readers.items():
                add((k, v), False)
        return need

    def _do_waits(self, e, need):
        w = self.waited[e]
        h = self.eng[e]
        for k, v in need.items():
            if w.get(k, 0) < v:
                h.wait_ge(self.semh[k], v)
                w[k] = v

    def _record(self, ev, reads, writes):
        k, v = ev
        for b in reads:
            if b.readers.get(k, 0) < v:
                b.readers[k] = v
        for b in writes:
            b.last_w = ev
            b.readers = {}

    def _est(self, e, need, dur):
        t = self.eng_free[e]
        for k, v in need.items():
            tv = self.ev_t.get((k, v))
            if tv is not None and tv + 0.25 > t:
                t = tv + 0.25
        return t, t + dur

    def est_start(self, d):
        if d[0] in ("flag", "rel"):
            return -1.0
        if d[0] == "op":
            _, e, fn, reads, writes, dur = d
            return self._est(e, self._needs(e, reads, writes), dur)[0]
        _, out_buf, out_ap, pairs, reads, dur = d
        return self._est("pe", self._needs("pe", reads, [out_buf]), dur)[0]

    def commit(self, d):
        if d[0] == "flag":
            self.flags.add(d[1])
        elif d[0] == "rel":
            self.on_rel(d[1])
        elif d[0] == "op":
            self.op(d[1], d[2], d[3], d[4], dur=d[5], force=True)
        else:
            self.mm(d[1], d[2], d[3], d[4], force=True)

    def set_flag(self, name):
        if self.defer:
            self.pend.append(("flag", name))
        else:
            self.flags.add(name)

    def op(self, e, fn, reads=(), writes=(), dur=None, force=False):
        if dur is None:
            dur = 0.36 if e == "act" else 0.43
        if self.defer and not force:
            self.pend.append(("op", e, fn, tuple(reads), tuple(writes), dur))
            return
        need = self._needs(e, reads, writes)
        t0, t1 = self._est(e, need, dur)
        self._do_waits(e, need)
        inst = fn(self.eng[e])
        self.cnt[e] += 1
        inst.then_inc(self.semh["e_" + e], 1)
        self.eng_free[e] = t1
        self.ev_t[("e_" + e, self.cnt[e])] = t1
        self._record(("e_" + e, self.cnt[e]), reads, writes)

    def mm(self, out_buf, out_ap, pairs, reads, force=False):
        e = "pe"
        if self.defer and not force:
            self.pend.append(("mm", out_buf, out_ap, list(pairs), tuple(reads), 0.13 * len(pairs)))
            return
        need = self._needs(e, reads, [out_buf])
        t0, t1 = self._est(e, need, 0.13 * len(pairs))
        self.eng_free[e] = t1
        self.ev_t[("e_pe", self.cnt[e] + 1)] = t1
        self._do_waits(e, need)
        n = len(pairs)
        inst = None
        for i, (l, r) in enumerate(pairs):
            inst = self.nc.tensor.matmul(out_ap, l, r, start=(i == 0), stop=(i == n - 1))
        self.cnt[e] += 1
        inst.then_inc(self.semh["e_pe"], 1)
        self._record(("e_pe", self.cnt[e]), reads, [out_buf])

    def dma(self, q, out_ap, in_ap, reads=(), writes=()):
        b = writes[0] if writes else reads[0]
        need = {}
        for rb in reads:
            if rb.last_w is not None:
                k, v = rb.last_w
                need[k] = max(need.get(k, 0), v)
        for wb in writes:
            if wb.last_w is not None:
                k, v = wb.last_w
                need[k] = max(need.get(k, 0), v)
            for k, v in wb.readers.items():
                need[k] = max(need.get(k, 0), v)
        self._do_waits(q, need)
        if b.dsem is None:
            b.dsem = "d_" + b.name
            self.semh[b.dsem] = self.nc.alloc_semaphore(b.dsem)
        b.dcnt += 16
        self.eng[q].dma_start(out=out_ap, in_=in_ap).then_inc(self.semh[b.dsem], 16)
        self._record((b.dsem, b.dcnt), reads, writes)

    def alias_barrier(self, from_bufs, to_bufs):
        evs = {}
        for b in from_bufs:
            if b.last_w is not None:
                k, v = b.last_w
                evs[k] = max(evs.get(k, 0), v)
            for k, v in b.readers.items():
                evs[k] = max(evs.get(k, 0), v)
        for b in to_bufs:
            for k, v in evs.items():
                if b.readers.get(k, 0) < v:
                    b.readers[k] = v

    def final_wait(self, q, bufs):
        need = {}
        for b in bufs:
            if b.last_w is not None:
                k, v = b.last_w
                need[k] = max(need.get(k, 0), v)
            for k, v in b.readers.items():
                need[k] = max(need.get(k, 0), v)
        self._do_waits(q, need)


def run_interleaved(gens):
    gens = list(gens)
    while gens:
        nxt = []
        for g in gens:
            try:
                next(g)
                nxt.append(g)
            except StopIteration:
                pass
        gens = nxt


def run_scheduled(S, gens):
    gens = list(gens)
    pend = [[] for _ in gens]
    alive = [True] * len(gens)
    blocked = [None] * len(gens)
    while True:
        progress = False
        for i, g in enumerate(gens):
            while alive[i] and not pend[i]:
                if blocked[i] is not None:
                    if S.flag_ok(blocked[i]):
                        blocked[i] = None
                    else:
                        break
                S.defer, S.pend = True, []
                try:
                    r = next(g)
                except StopIteration:
                    alive[i] = False
                    r = None
                finally:
                    S.defer = False
                pend[i].extend(S.pend)
                S.pend = []
                if isinstance(r, tuple) and r and r[0] == "wait":
                    blocked[i] = r[1]
                    if pend[i]:
                        break
        best, bt = None, None
        for i in range(len(gens)):
            if pend[i]:
                t = S.est_start(pend[i][0])
                if bt is None or t < bt:
                    best, bt = i, t
        if best is None:
            if any(alive):
                if all((not alive[i]) or (blocked[i] is not None and not S.flag_ok(blocked[i])) for i in range(len(gens))):
                    raise RuntimeError("scheduler deadlock on flags: %s" % [b for b in blocked if b])
                continue
            break
        S.commit(pend[best].pop(0))


class SubT:
    def __init__(self, slot, off, n, is_pre, first_real, dram0):
        self.slot, self.off, self.n, self.is_pre, self.first_real, self.dram0 = slot, off, n, is_pre, first_real, dram0


def build_program(LAYERS):
    nc = bass.Bass("TRN2", target_bir_lowering=False)
    S = Sched(nc)
    NLY = len(LAYERS)
    NPP = DEPTH * NL

    def din(name, shape):
        return nc.dram_tensor(name, shape, F32, kind="ExternalInput").ap()

    xT = din("xT", [8, 128, HALO + TPC])
    memT = din("memT", [8, 128, 256])
    pp_d = din("pp", [128, NPP])
    rowp_d = din("rowp", [DEPTH, 3, 384])
    sgb_d = din("sgb", [DEPTH, 512])
    sgwT_d = din("sgwT", [DEPTH, 4, 128, 128])
    poolw_d = din("poolw", [DEPTH, 4, 64, 64])
    ident_d = din("ident", [128, 128])
    triu_d = din("triu", [128, 128])
    icnt_d = din("icnt", [128, 2, 16])
    mask_d = din("mask", [128, 1])
    w_in_d = din("w_in", [DEPTH, D, D_IN])
    w_out_d = din("w_out", [DEPTH, D, D])
    wq_d = din("wq", [DEPTH, D, D])
    wk_d = din("wk", [DEPTH, D, D])
    wv_d = din("wv", [DEPTH, D, D])
    wo_d = din("wo", [DEPTH, D, D])
    w1_d = din("w_ff1", [DEPTH, D, D_FF])
    w2_d = din("w_ff2", [DEPTH, D_FF, D])
    outT = nc.dram_tensor("outT", [8, 128, TPC], F32, kind="ExternalOutput").ap()

    def sb(name, shape, dt):
        return nc.alloc_sbuf_tensor("s_" + name, shape, dt)

    xres = sb("xres", [128, 8, RES_T], F32)
    hT = sb("hT", [128, 8, 2 * SUB], BF16)
    sq = sb("sq", [128, 8, SUB], BF16)
    st = sb("st", [128, 4, SUB], F32)
    kT = sb("kT", [128, NLY, 8, 256], BF16)
    vv = sb("vv", [128, NLY, 2, 1024], BF16)
    pp = sb("pp", [128, NPP], F32)
    ppn = sb("ppn", [128, 9 * DEPTH], F32)
    rowb = sb("rowb", [128, 3, 384], F32)
    sgw = sb("sgw", [128, NLY, 4, 128], BF16)
    bd = sb("bd", [128, NLY, 2, 128], BF16)
    ones = sb("ones", [128, 128], BF16)
    ident = sb("ident", [128, 128], F32)
    sgbr = sb("sgbr", [1, NLY * 512], BF16)
    icnt = sb("icnt", [128, 2, 16], F32)
    mask = sb("mask", [128, 1], F32)
    sm = sb("sm", [128, 2, 16], F32)
    stZ = sb("stZ", [128, NLY, 2, 16], F32)
    stH = sb("stH", [128, NLY, 3, 32], BF16)
    uT = sb("uT", [128, 8, SUB], BF16)
    rl = sb("rl", [128, 2, SUB], F32)
    gs = uT[:, :, :].rearrange("p c t -> p (c t)").bitcast(F32).rearrange("p (c t) -> p c t", c=4)
    ring = [sb(f"ring{i}", [128, 8, 1024], BF16) for i in range(4)]
    w9 = sb("w9", [128, 1024], BF16)
    MR_F = 8 * RES_T
    mr_words = 0
    carve = {}

    def carve_f32(name, nwords):
        nonlocal mr_words
        carve[name] = (mr_words, nwords)
        mr_words += nwords

    carve_f32("zA", 2 * (16 + SUB))
    carve_f32("T1", 16 + SUB)
    carve_f32("T2", 16 + SUB)
    carve_f32("tmp", 4 * 384)
    carve_f32("cc", 3 * SUB)
    carve_f32("yT", 8 * SUB)
    carve_f32("pb", 2 * SUB // 2)
    carve_f32("ya", 2 * SUB // 2)
    carve_f32("ub", 4 * SUB // 2)
    carve_f32("vln", 2 * 384 // 2)
    carve_f32("yb", 4 * SUB // 2)
    carve_f32("hbuf", 3 * (32 + SUB) // 2)
    carve_f32("yc", 3 * SUB // 2)
    carve_f32("dg", 93 * 128 // 2)
    HF_W = 8 * RES_T // 2
    mr_total = max(mr_words, MR_F + HF_W)
    MR = sb("MR", [128, mr_total], F32)

    def mrv(name, dt, pattern=None, **kw):
        o, nw = carve[name]
        v = MR[:, o:o + nw]
        if dt == BF16:
            v = v.bitcast(BF16)
        if pattern:
            v = v.rearrange(pattern, **kw)
        return v

    zA = mrv("zA", F32, "p (c t) -> p c t", c=2)
    T1 = mrv("T1", F32)
    T2 = mrv("T2", F32)
    tmp = mrv("tmp", F32, "p (c t) -> p c t", c=4)
    cc = mrv("cc", F32, "p (c t) -> p c t", c=3)
    yT = mrv("yT", F32, "p (c t) -> p c t", c=8)
    pb = mrv("pb", BF16, "p (c t) -> p c t", c=2)
    ya = mrv("ya", BF16, "p (c t) -> p c t", c=2)
    ub = mrv("ub", BF16, "p (c t) -> p c t", c=4)
    vln = mrv("vln", BF16, "p (c t) -> p c t", c=2)
    yb = mrv("yb", BF16, "p (c t) -> p c t", c=4)
    hbuf = mrv("hbuf", BF16, "p (c t) -> p c t", c=3)
    yc = mrv("yc", BF16, "p (c t) -> p c t", c=3)
    dg = mrv("dg", BF16, "p (k m) -> p k m", k=93)
    hTf = MR[:, MR_F:MR_F + HF_W].bitcast(BF16).rearrange("p (c t) -> p c t", c=8)
    triu = MR[:, carve["yT"][0]:carve["yT"][0] + 128]
    yacc = MR[:, 0:MR_F].rearrange("p (c t) -> p c t", c=8)
    a0 = carve["zA"][0]
    qT = MR[:, a0:a0 + 1024].bitcast(BF16).rearrange("p (c t) -> p c t", c=8)
    oT = MR[:, a0 + 1024:a0 + 2048].bitcast(BF16).rearrange("p (c t) -> p c t", c=8)
    ET = MR[:, a0 + 2048:a0 + 2560].bitcast(BF16).rearrange("p (a b t) -> p a b t", a=2, b=2)
    rec = MR[:, a0 + 2560:a0 + 3072].rearrange("p (a t) -> p a t", a=2)
    assert a0 + 3072 <= carve["yT"][0], "attention scratch overlaps yT"

    banks = [(S.buf(f"bank{i}"), nc.alloc_psum_tensor(f"bank{i}", [128, 512], F32)) for i in range(8)]
    for bB, _ in banks:
        bB.excl = True
    free_banks = list(range(8))
    bank_idx = {}
    for i_, (bB_, _t) in enumerate(banks):
        bank_idx[bB_.name] = i_

    def acquire():
        while not free_banks:
            yield ("wait", "__bank__")
        i = free_banks.pop(0)
        return banks[i]

    def bank():
        assert free_banks, "no free PSUM bank in immediate mode"
        return banks[free_banks.pop(0)]

    def rel(bB):
        i = bank_idx[bB.name]
        if S.defer:
            S.pend.append(("rel", i))
        else:
            free_banks.append(i)

    locks = {"norm": True}

    def lock_acquire(name):
        while not locks[name]:
            yield ("wait", "__lock__" + name)
        locks[name] = False

    def lock_release(name):
        if S.defer:
            S.pend.append(("rel", "L:" + name))
        else:
            locks[name] = True

    def _on_rel(i):
        if isinstance(i, str):
            locks[i[2:]] = True
        else:
            free_banks.append(i)

    S.on_rel = _on_rel
    S.flag_ok = lambda name: (bool(free_banks) if name == "__bank__" else
                              (locks[name[8:]] if name.startswith("__lock__") else name in S.flags))

    def drive(gen):
        try:
            while True:
                r = next(gen)
                assert not (isinstance(r, tuple) and r and r[0] == "wait"), "blocking wait in immediate mode"
        except StopIteration as e:
            return e.value

    def grouped(items, emit_mm, emit_evac):
        for i in range(0, len(items), 2):
            bB, t = bank()
            grp = items[i:i + 2]
            for j, it in enumerate(grp):
                emit_mm(it, bB, t[:, 256 * j:256 * j + 256])
            for j, it in enumerate(grp):
                emit_evac(it, bB, t[:, 256 * j:256 * j + 256], i // 2)
            rel(bB)

    def grouped2(items, emit_mm, emit_evac_bank, n):
        for i in range(0, len(items), 2):
            bB, t = bank()
            for j in range(2):
                emit_mm(items[i + j], bB, t[:, 256 * j:256 * j + 256])
            emit_evac_bank(items[i], bB, t[:, :].rearrange("p (a t) -> p a t", a=2)[:, :, 0:n], i // 2)
            rel(bB)

    NSLOT = NSUBP + 1
    xB = [S.buf(f"x{i}") for i in range(NSLOT)]
    hB = [S.buf("h0"), S.buf("h1")]
    hfB = [S.buf(f"hf{i}") for i in range(NSLOT)]
    yaccB = [S.buf(f"yacc{i}") for i in range(NSLOT)]
    sqB = S.buf("sq")
    rstdB = [S.buf("rstd0"), S.buf("rstd1")]
    stmB, stvB = S.buf("stm"), S.buf("stv")
    kTB, vvB = S.buf("kT"), S.buf("vv")
    ppB, ppnB, rowbB, sgwB, bdB, onesB = S.buf("pp"), S.buf("ppn"), S.buf("rowb"), S.buf("sgw"), S.buf("bd"), S.buf("ones")
    identB, sgbrB, icntB, maskB = S.buf("ident"), S.buf("sgbr"), S.buf("icnt"), S.buf("mask")
    smB = [S.buf("sm0"), S.buf("sm1")]
    stZB, stHB = S.buf("stZ"), S.buf("stH")
    uTB, rlB = S.buf("uT"), [S.buf("rl0"), S.buf("rl1")]
    gsB = [S.buf(f"gs{i}") for i in range(3)]
    ringB = [S.buf(f"ring{i}") for i in range(4)]
    w9B = S.buf("w9")
    zAB = [S.buf("zA0"), S.buf("zA1")]
    T1B, T2B = S.buf("T1"), S.buf("T2")
    tmpB = [S.buf(f"tmp{i}") for i in range(4)]
    ccB = [S.buf(f"cc{i}") for i in range(3)]
    yTB = S.buf("yT")
    yTcB = [S.buf(f"yTc{i}") for i in range(8)]
    qTcB = [S.buf(f"qTc{i}") for i in range(4)]
    pbB = [S.buf("pb0"), S.buf("pb1")]
    yaB = [S.buf("ya0"), S.buf("ya1")]
    ubB = [S.buf(f"ub{i}") for i in range(4)]
    vlnB = [S.buf("vln0"), S.buf("vln1")]
    ybB = [S.buf(f"yb{i}") for i in range(4)]
    hbufB = [S.buf(f"hbuf{i}") for i in range(3)]
    ycB = [S.buf(f"yc{i}") for i in range(3)]
    dgB = [S.buf(f"dg{i}") for i in range(3)]
    qTB, oTB, ETB, recB = S.buf("qT"), S.buf("oT"), [S.buf("ET0"), S.buf("ET1")], [S.buf("rec0"), S.buf("rec1")]
    mixer_bufs = zAB + [T1B, T2B] + tmpB + ccB + [yTB] + yTcB + pbB + yaB + ubB + vlnB + ybB + hbufB + ycB + dgB
    q2cB = [S.buf(f"q2c{i}") for i in range(4)]
    attn_bufs = [qTB, oTB] + qTcB + ETB + recB
    attn_alias_src = zAB + [T1B, T2B] + tmpB + ccB

    loads = []
    state = {"emitted": 0}

    def w_rows(src2d, p=128):
        return src2d.rearrange("(k p) n -> p k n", p=p)

    def add_load(kind, l, q=None):
        j = len(loads)
        slot = j % 4

        def emit():
            R, RB = ring[slot], ringB[slot]
            if kind == "inA":
                S.dma("pool", R[:, :, :], w_rows(w_in_d[l, :, 0:1024]), writes=[RB])
            elif kind == "inB":
                S.dma("pool", R[:, :, 0:768], w_rows(w_in_d[l, :, 1024:1792]), writes=[RB])
            elif kind == "out":
                S.dma("pool", R[:, 0:2, :], w_rows(w_out_d[l, 0:256, :]), writes=[RB])
                S.dma("pool", R[0:96, 2:6, :], w_rows(w_out_d[l, 256:640, :], 96), writes=[RB])
                S.dma("pool", R[:, 6:8, :], w_rows(w_out_d[l, 640:896, :]), writes=[RB])
                S.dma("pool", w9[:, :], w_out_d[l, 896:1024, :], writes=[w9B])
            elif kind in ("wq", "wk", "wv", "wo"):
                src = {"wq": wq_d, "wk": wk_d, "wv": wv_d, "wo": wo_d}[kind]
                S.dma("pool", R[:, :, :], w_rows(src[l, :, :]), writes=[RB])
            elif kind == "w1":
                S.dma("pool", R[:, :, :], w_rows(w1_d[l, :, q * 1024:(q + 1) * 1024]), writes=[RB])
            elif kind == "w2":
                S.dma("pool", R[:, :, :], w_rows(w2_d[l, q * 1024:(q + 1) * 1024, :]), writes=[RB])

        loads.append(emit)
        return j

    def pump(upto):
        while state["emitted"] <= upto and state["emitted"] < len(loads):
            loads[state["emitted"]]()
            state["emitted"] += 1

    def done(j):
        pump(j + 4)

    kv_loads = {}
    pl_loads = {}
    for p in range(NPASS):
        for li, l in enumerate(LAYERS):
            d = {}
            d["inA"] = add_load("inA", l)
            d["inB"] = add_load("inB", l)
            d["out"] = add_load("out", l)
            if p == 0:
                kv_loads[li] = (add_load("wk", l), add_load("wv", l))
            d["wq"] = add_load("wq", l)
            d["wo"] = add_load("wo", l)
            for q in range(4):
                d[("w1", q)] = add_load("w1", l, q)
                d[("w2", q)] = add_load("w2", l, q)
            pl_loads[(p, li)] = d

    S.dma("sp", pp[:, :], pp_d, writes=[ppB])
    S.dma("sp", ident[:, :], ident_d, writes=[identB])
    S.dma("sp", icnt[:, :, :], icnt_d, writes=[icntB])
    S.dma("sp", mask[:, :], mask_d, writes=[maskB])
    pump(3)
    for l_ in range(DEPTH):
        S.op("dve", lambda e, l_=l_: e.tensor_scalar(out=ppn[:, 9 * l_:9 * l_ + 3], in0=pp[:, BCG_(l_, 0):BCG_(l_, 0) + 3], scalar1=-1.0, scalar2=None, op0=ALU.mult),
             reads=[ppB], writes=[ppnB])
        S.op("dve", lambda e, l_=l_: e.tensor_scalar(out=ppn[:, 9 * l_ + 3:9 * l_ + 9], in0=pp[:, CLG_(l_, 0):CLG_(l_, 0) + 6], scalar1=-1.0, scalar2=None, op0=ALU.mult),
             reads=[ppB], writes=[ppnB])
    S.op("dve", lambda e: e.memset(ones[:, :], 1.0), writes=[onesB])
    S.op("dve", lambda e: e.memset(bd[:, :, :, :], 0.0), writes=[bdB])
    S.op("dve", lambda e: e.memset(MR[:, 0:mr_total], 0.0), writes=mixer_bufs)
    S.dma("sp", triu, triu_d, writes=[yTB])
    for li, l in enumerate(LAYERS):
        for c in range(2):
            for hh in range(2):
                g = 2 * c + hh
                S.dma("pool", bd[64 * hh:64 * hh + 64, li, c, 64 * hh:64 * hh + 64], poolw_d[l, g, :, :], writes=[bdB])
        S.dma("pool", sgbr[0:1, li * 512:(li + 1) * 512], sgb_d[l:l + 1, :], writes=[sgbrB])
        for h in range(4):
            tb = tmpB[h % 4]
            S.dma("sp", tmp[:, h % 4, 0:128], sgwT_d[l, h, :, :], writes=[tb])
            S.op("dve", lambda e, h=h, li=li: e.tensor_tensor(out=sgw[:, li, h, :], in0=tmp[:, h % 4, 0:128], in1=triu, op=ALU.mult),
                 reads=[tb, yTB], writes=[sgwB])

    def norm_stats(src3d, src_bufs, n, par):
        S.op("act", lambda e: e.activation(out=sq[:, :, 0:n], in_=src3d, func=AF.Square), reads=src_bufs, writes=[sqB], dur=2.0)
        return stats_from_sq(n, par)

    def stats_gen(n, par):
        bB, t = yield from acquire()
        S.mm(bB, t[:, 0:n], [(ones[:, :], sq[:, c, 0:n]) for c in range(8)], reads=[sqB, onesB])
        r = st[:, par, 0:n]
        S.op("act", lambda e: e.activation(out=r, in_=t[:, 0:n], func=AF.Ln, scale=1.0 / D, bias=EPS),
             reads=[bB], writes=[rstdB[par]])
        rel(bB)
        S.op("act", lambda e: e.activation(out=r, in_=r, func=AF.Exp, scale=-0.5),
             reads=[rstdB[par]], writes=[rstdB[par]])
        return r

    def stats_from_sq(n, par):
        return drive(stats_gen(n, par))

    def hview(sub, ffn=False):
        if ffn:
            return (lambda k: hTf[:, k, sub.off:sub.off + sub.n]), hfB[sub.slot]
        hs = sub.slot % 2
        return (lambda k: hT[:, k, hs * SUB:hs * SUB + sub.n]), hB[hs]

    def pre_norm(l, gi, sub, par, ffn=False):
        return drive(pre_norm_gen(l, gi, sub, par, ffn))

    def pre_norm_gen(l, gi, sub, par, ffn=False):
        n, off, s = sub.n, sub.off, sub.slot
        hv, hb = hview(sub, ffn)
        yield from lock_acquire("norm")
        S.op("act", lambda e: e.activation(out=sq[:, :, 0:n], in_=xres[:, :, off:off + n], func=AF.Square), reads=[xB[s]], writes=[sqB], dur=2.0)
        yield
        r = yield from stats_gen(n, par)
        yield
        for c in range(8):
            S.op("dve", lambda e, c=c: e.scalar_tensor_tensor(
                out=hv(c), in0=xres[:, c, off:off + n], scalar=pp[:, G_(l, gi, c):G_(l, gi, c) + 1],
                in1=r, op0=ALU.mult, op1=ALU.mult), reads=[xB[s], rstdB[par], ppB], writes=[hb])
            if c == 7:
                lock_release("norm")
            yield

    def post_norm_residual(l, gi, sub, par, y_ap_fn, yBuf, y3d):
        return drive(post_norm_gen(l, gi, sub, par, y_ap_fn, yBuf, y3d))

    def post_norm_gen(l, gi, sub, par, y_ap_fn, yBuf, y3d):
        n, off, s = sub.n, sub.off, sub.slot
        allb = list(dict.fromkeys(yBuf(c) for c in range(8)))
        yield from lock_acquire("norm")
        S.op("act", lambda e: e.activation(out=sq[:, :, 0:n], in_=y3d, func=AF.Square), reads=allb, writes=[sqB], dur=2.0)
        yield
        r = yield from stats_gen(n, par)
        yield
        for c in range(8):
            S.op("dve", lambda e, c=c: e.scalar_tensor_tensor(
                out=y_ap_fn(c), in0=y_ap_fn(c), scalar=pp[:, G_(l, gi, c):G_(l, gi, c) + 1],
                in1=r, op0=ALU.mult, op1=ALU.mult), reads=[yBuf(c), rstdB[par], ppB], writes=[yBuf(c)])
        lock_release("norm")
        S.op("dve", lambda e: e.tensor_tensor(out=xres[:, :, off:off + n], in0=xres[:, :, off:off + n], in1=y3d, op=ALU.add),
             reads=allb + [xB[s]], writes=[xB[s]], dur=2.3)

    def gelu_chain(z, w, out, zb, wb, outb, out_wait=None):
        S.op("act", lambda e: e.activation(out=w, in_=z, func=AF.Square), reads=[zb], writes=[wb]); yield
        S.op("dve", lambda e: e.scalar_tensor_tensor(out=w, in0=w, scalar=1.0 / 0.044715, in1=z, op0=ALU.add, op1=ALU.mult),
             reads=[wb, zb], writes=[wb]); yield
        S.op("act", lambda e: e.activation(out=w, in_=w, func=AF.Exp, scale=-2.0 * GELU_C * 0.044715), reads=[wb], writes=[wb]); yield
        S.op("act", lambda e: e.activation(out=w, in_=w, func=AF.Ln, bias=1.0), reads=[wb], writes=[wb]); yield
        S.op("act", lambda e: e.activation(out=w, in_=w, func=AF.Exp, scale=-1.0), reads=[wb], writes=[wb]); yield
        if out_wait:
            yield ("wait", out_wait)
        S.op("dve", lambda e: e.tensor_tensor(out=out, in0=z, in1=w, op=ALU.mult), reads=[zb, wb], writes=[outb]); yield

    def kv_prologue(li, l):
        jk, jv = kv_loads[li]
        hkv = [hB[0]]
        S.dma("sp", yT[:, :, :], memT.rearrange("c p t -> p c t"), writes=[yTB])
        r = norm_stats(yT[:, :, :], [yTB], 256, 0)
        for c in range(8):
            S.op("dve", lambda e, c=c: e.scalar_tensor_tensor(
                out=hT[:, c, 0:256], in0=yT[:, c, :], scalar=pp[:, G_(l, 6, c):G_(l, 6, c) + 1], in1=r,
                op0=ALU.mult, op1=ALU.mult), reads=[yTB, rstdB[0], ppB], writes=hkv)
        Rk, RkB = ring[jk % 4], ringB[jk % 4]
        grouped(list(range(8)),
                lambda m, bB, ap: S.mm(bB, ap, [(Rk[:, k, m * 128:(m + 1) * 128], hT[:, k, 0:256]) for k in range(8)], reads=[RkB] + hkv),
                lambda m, bB, ap, gi: S.op("act", lambda e: e.activation(out=kT[:, li, m, :], in_=ap, func=AF.Copy), reads=[bB], writes=[kTB]))
        done(jk)
        Rv, RvB = ring[jv % 4], ringB[jv % 4]
        for mc in range(2):
            for hf in range(2):
                bB, t = bank()
                S.mm(bB, t[:, :], [(hT[:, k, mc * 128:(mc + 1) * 128], Rv[:, k, hf * 512:(hf + 1) * 512]) for k in range(8)],
                     reads=[RvB] + hkv)
                S.op("dve", lambda e, mc=mc, hf=hf, t=t: e.tensor_copy(out=vv[:, li, mc, hf * 512:(hf + 1) * 512], in_=t[:, :]),
                     reads=[bB], writes=[vvB])
                rel(bB)
        done(jv)

    def interleave_gen(gens):
        gens = list(gens)
        while gens:
            nxt = []
            for g in gens:
                try:
                    r = next(g)
                    nxt.append(g)
                    yield r
                except StopIteration:
                    pass
            gens = nxt

    def mixer_body(p, li, l, sub, par, last_layer, prev_key):
        key = f"{p}_{li}_{sub.slot}"
        n, off, s = sub.n, sub.off, sub.slot
        ld = pl_loads[(p, li)]
        RA, RAB = ring[ld["inA"] % 4], ringB[ld["inA"] % 4]
        RB_, RBB = ring[ld["inB"] % 4], ringB[ld["inB"] % 4]
        RO, ROB = ring[ld["out"] % 4], ringB[ld["out"] % 4]
        hrhs, hb_ = hview(sub)
        W = n + 16
        nblk = n // 128

        def pool_s1():
            bB, t = yield from acquire()
            for c in range(2):
                S.mm(bB, t[:, 256 * c:256 * c + n], [(RA[:, k, 128 * c:128 * c + 128], hrhs(k)) for k in range(8)], reads=[RAB, hb_])
            yield
            for c in range(2):
                S.op("act", lambda e, c=c: e.activation(out=zA[:, c, 16:16 + n], in_=t[:, 256 * c:256 * c + n], func=AF.Identity,
                                                       bias=pp[:, BA_(l, c):BA_(l, c) + 1]), reads=[bB, ppB], writes=[zAB[c]])
                if c == 1:
                    rel(bB)
                yield
            for c in range(2):
                S.op("dve", lambda e, c=c: e.tensor_tensor(out=T1[:, 1:W], in0=zA[:, c, 1:W], in1=zA[:, c, 0:W - 1], op=ALU.add),
                     reads=[zAB[c]], writes=[T1B]); yield
                S.op("dve", lambda e: e.tensor_tensor(out=T2[:, 3:W], in0=T1[:, 3:W], in1=T1[:, 1:W - 2], op=ALU.add),
                     reads=[T1B], writes=[T2B]); yield
                if c == 1:
                    S.op("dve", lambda e: e.tensor_tensor(out=T1[:, 7:W], in0=T2[:, 7:W], in1=T2[:, 3:W - 4], op=ALU.add),
                         reads=[T2B], writes=[T1B]); yield
                    S.op("dve", lambda e: e.tensor_tensor(out=T2[:, 15:W], in0=T1[:, 15:W], in1=T1[:, 7:W - 8], op=ALU.add),
                         reads=[T1B], writes=[T2B]); yield
                for (lo, srcT, srcB) in ((0, T1, T1B), (64, T2, T2B)):
                    if prev_key and c == 0 and lo == 0:
                        yield ("wait", prev_key + "_pool")
                    S.op("dve", lambda e, lo=lo, srcT=srcT, c=c: e.scalar_tensor_tensor(
                        out=pb[lo:lo + 64, c, 0:n], in0=srcT[lo:lo + 64, 16:16 + n],
                        scalar=pp[lo:lo + 64, IW_(l, c):IW_(l, c) + 1], in1=zA[lo:lo + 64, c, 16:16 + n],
                        op0=ALU.mult, op1=ALU.subtract), reads=[srcB, zAB[c], ppB], writes=[pbB[c]]); yield
                    if sub.first_real:
                        S.op("dve", lambda e, lo=lo, srcT=srcT, c=c: e.tensor_tensor(
                            out=rl[lo:lo + 64, 0, 0:16], in0=srcT[lo:lo + 64, 16:32], in1=icnt[lo:lo + 64, c, :], op=ALU.mult),
                            reads=[srcB, icntB], writes=[rlB[0]]); yield
                        S.op("dve", lambda e, lo=lo, c=c: e.tensor_tensor(
                            out=pb[lo:lo + 64, c, 0:16], in0=rl[lo:lo + 64, 0, 0:16], in1=zA[lo:lo + 64, c, 16:32], op=ALU.subtract),
                            reads=[rlB[0], zAB[c]], writes=[pbB[c]]); yield
                if sub.is_pre:
                    S.op("dve", lambda e, c=c: e.tensor_scalar(out=zA[:, c, 0:16], in0=zA[:, c, n:n + 16], scalar1=mask[:, 0:1],
                                                              scalar2=None, op0=ALU.mult), reads=[zAB[c], maskB], writes=[zAB[c]])
                else:
                    S.op("dve", lambda e, c=c: e.tensor_copy(out=zA[:, c, 0:16], in_=zA[:, c, n:n + 16]), reads=[zAB[c]], writes=[zAB[c]])
                yield

        def pool_s2():
            bB2, t2 = yield from acquire()
            for c in range(2):
                S.mm(bB2, t2[:, 256 * c:256 * c + n], [(bd[:, li, c, :], pb[:, c, 0:n])], reads=[bdB, pbB[c]])
            S.set_flag(key + "_pool")
            yield
            for c in range(2):
                S.op("act", lambda e, c=c: e.activation(out=ya[:, c, 0:n], in_=t2[:, 256 * c:256 * c + n], func=AF.Identity,
                                                       scale=pp[:, PSC_(l, c):PSC_(l, c) + 1]), reads=[bB2, ppB], writes=[yaB[c]])
                if c == 1:
                    rel(bB2)
                yield

        def u_pair(h0):
            bB, t = yield from acquire()
            for j in range(2):
                h = h0 + j
                S.mm(bB, t[0:96, 256 * j:256 * j + n], [(RA[:, k, 256 + 96 * h:256 + 96 * h + 96], hrhs(k)) for k in range(8)], reads=[RAB, hb_])
            yield

            for j in range(2):
                S.op("act", lambda e, j=j: e.activation(out=tmp[0:96, 2 * j, 0:n], in_=t[0:96, 256 * j:256 * j + n], func=AF.Identity,
                                                       bias=pp[0:96, BU_(l, h0 + j):BU_(l, h0 + j) + 1]),
                     reads=[bB, ppB], writes=[tmpB[2 * j]])
            rel(bB)
            yield

            def chain(j):
                h = h0 + j
                ts = 2 * j
                z = tmp[0:96, ts, 0:n]
                w = tmp[0:96, ts + 1, 0:n]
                yield from gelu_chain(z, w, ub[0:96, h, 0:n], tmpB[ts], tmpB[ts + 1], ubB[h], out_wait=(prev_key + "_sg") if prev_key else None)

            yield from interleave_gen([chain(0), chain(1)])

        def v_chain(b, ts):
            bB, t = yield from acquire()
            S.mm(bB, t[:, 0:384], [(hrhs(k)[:, 128 * b:128 * b + 128], RA[:, k, 640:1024]) for k in range(8)],
                 reads=[RAB, hb_])
            yield
            z = tmp[:, ts, :]
            w = tmp[:, ts + 1, :]
            S.op("dve", lambda e: e.tensor_tensor(out=z, in0=t[:, 0:384], in1=rowb[:, 0, :], op=ALU.add),
                 reads=[bB, rowbB], writes=[tmpB[ts]])
            rel(bB)
            yield
            yield from gelu_chain(z, w, z, tmpB[ts], tmpB[ts + 1], tmpB[ts])
            smb = smB[b % 2]
            smt = sm[:, b % 2, :]
            S.op("dve", lambda e: e.bn_stats(out=smt[:, 0:6], in_=z), reads=[tmpB[ts]], writes=[smb]); yield
            S.op("dve", lambda e: e.bn_aggr(out=smt[:, 8:10], in_=smt[:, 0:6]), reads=[smb], writes=[smb]); yield
            S.op("act", lambda e: e.activation(out=smt[:, 10:11], in_=smt[:, 9:10], func=AF.Ln, bias=EPS), reads=[smb], writes=[smb]); yield
            S.op("act", lambda e: e.activation(out=smt[:, 10:11], in_=smt[:, 10:11], func=AF.Exp, scale=-0.5), reads=[smb], writes=[smb]); yield
            S.op("dve", lambda e: e.tensor_scalar(out=z, in0=z, scalar1=smt[:, 8:9], scalar2=smt[:, 10:11], op0=ALU.subtract, op1=ALU.mult),
                 reads=[tmpB[ts], smb], writes=[tmpB[ts]]); yield
            S.op("dve", lambda e: e.tensor_tensor(out=z, in0=z, in1=rowb[:, 1, :], op=ALU.mult), reads=[tmpB[ts], rowbB], writes=[tmpB[ts]]); yield
            if prev_key:
                yield ("wait", prev_key + "_sgmm")
            S.op("dve", lambda e: e.tensor_tensor(out=vln[:, b, :], in0=z, in1=rowb[:, 2, :], op=ALU.add), reads=[tmpB[ts], rowbB], writes=[vlnB[b]]); yield

        def sg_s1():
            yield from u_pair(0)
            yield from u_pair(2)
            yield from interleave_gen([v_chain(b, 2 * b) for b in range(nblk)])

        def sg_s2():
            for h0 in (0, 2):
                bB, t = yield from acquire()
                for j in range(2):
                    h = h0 + j
                    for b in range(nblk):
                        S.mm(bB, t[0:96, 256 * j + 128 * b:256 * j + 128 * b + 128],
                             [(vln[:, b, 96 * h:96 * h + 96], sgw[:, li, h, :]),
                              (ones[0:1, 0:96], sgbr[0:1, li * 512 + 128 * h:li * 512 + 128 * h + 128])],
                             reads=[vlnB[b], sgwB, onesB, sgbrB])
                if h0 == 2:
                    S.set_flag(key + "_sgmm")
                yield
                for j in range(2):
                    h = h0 + j
                    S.op("dve", lambda e, h=h, j=j: e.tensor_tensor(out=yb[0:96, h, 0:n], in0=t[0:96, 256 * j:256 * j + n], in1=ub[0:96, h, 0:n], op=ALU.mult),
                         reads=[bB, ubB[h]], writes=[ybB[h]])
                    if h == 3:
                        S.set_flag(key + "_sg")
                    if j == 1:
                        rel(bB)
                    yield

        def glu_chain(c):
            bB, t = yield from acquire()
            pa, pg = t[:, 0:256], t[:, 256:512]
            S.mm(bB, pa[:, 0:n], [(RB_[:, k, 128 * c:128 * c + 128], hrhs(k)) for k in range(8)], reads=[RBB, hb_])
            S.mm(bB, pg[:, 0:n], [(RB_[:, k, 384 + 128 * c:384 + 128 * c + 128], hrhs(k)) for k in range(8)], reads=[RBB, hb_])
            yield
            w = gs[:, c, 0:n]
            S.op("act", lambda e: e.activation(out=w, in_=pg[:, 0:n], func=AF.Exp, scale=-1.0, bias=ppn[:, 9 * l + c:9 * l + c + 1]),
                 reads=[bB, ppnB], writes=[gsB[c]]); yield
            S.op("act", lambda e: e.activation(out=w, in_=w, func=AF.Ln, bias=1.0), reads=[gsB[c]], writes=[gsB[c]]); yield
            S.op("act", lambda e: e.activation(out=w, in_=w, func=AF.Exp, scale=-1.0), reads=[gsB[c]], writes=[gsB[c]]); yield
            if prev_key:
                yield ("wait", prev_key + "_conv")
            S.op("dve", lambda e: e.scalar_tensor_tensor(out=hbuf[:, c, 32:32 + n], in0=pa[:, 0:n],
                                                        scalar=pp[:, BCA_(l, c):BCA_(l, c) + 1], in1=w, op0=ALU.add, op1=ALU.mult),
                 reads=[bB, gsB[c], ppB], writes=[hbufB[c]])
            rel(bB)
            yield

        def conv_mm(c):
            bB, t = yield from acquire()
            pc = t[:, 0:n]
            S.mm(bB, pc, [(dg[:, 31 * c + k, :], hbuf[:, c, 2 + k:2 + k + n]) for k in range(CONV_K)], reads=[dgB[c], hbufB[c]])
            if sub.is_pre:
                S.op("dve", lambda e: e.tensor_scalar(out=hbuf[:, c, 0:32], in0=hbuf[:, c, n:n + 32], scalar1=mask[:, 0:1],
                                                     scalar2=None, op0=ALU.mult), reads=[hbufB[c], maskB], writes=[hbufB[c]])
            else:
                S.op("dve", lambda e: e.tensor_copy(out=hbuf[:, c, 0:32], in_=hbuf[:, c, n:n + 32]), reads=[hbufB[c]], writes=[hbufB[c]])
            return bB, pc

        def conv_chunk(c, bB, pc):
            cbcol = pp[:, CB_(l, c):CB_(l, c) + 1]
            S.op("act", lambda e: e.activation(out=cc[:, c, 0:n], in_=pc, func=AF.Identity, bias=cbcol),
                 reads=[bB, ppB], writes=[ccB[c]])
            rel(bB)
            yield
            S.op("act", lambda e: e.activation(out=sq[:, c, 0:n], in_=cc[:, c, 0:n], func=AF.Copy), reads=[ccB[c]], writes=[sqB]); yield
            S.op("act", lambda e: e.activation(out=sq[:, 3 + c, 0:n], in_=cc[:, c, 0:n], func=AF.Square),
                 reads=[ccB[c]], writes=[sqB]); yield

        mt = st[:, 2, 0:n]
        vt = st[:, 3, 0:n]

        def silu_chain(c):
            t_ = cc[:, c, 0:n]
            w = yT[:, c, 0:n]
            S.op("dve", lambda e: e.tensor_tensor(out=t_, in0=t_, in1=mt, op=ALU.subtract), reads=[ccB[c], stmB], writes=[ccB[c]]); yield
            S.op("dve", lambda e: e.tensor_tensor(out=t_, in0=t_, in1=vt, op=ALU.mult), reads=[ccB[c], stvB], writes=[ccB[c]]); yield
            S.op("act", lambda e: e.activation(out=w, in_=t_, func=AF.Exp, scale=ppn[:, 9 * l + 3 + c:9 * l + 4 + c],
                                               bias=ppn[:, 9 * l + 6 + c:9 * l + 7 + c]), reads=[ccB[c], ppnB], writes=[yTcB[c]]); yield
            S.op("dve", lambda e: e.tensor_scalar(out=t_, in0=t_, scalar1=pp[:, CLG_(l, c):CLG_(l, c) + 1],
                                                 scalar2=pp[:, CLB_(l, c):CLB_(l, c) + 1], op0=ALU.mult, op1=ALU.add),
                 reads=[ccB[c], ppB], writes=[ccB[c]]); yield
            S.op("act", lambda e: e.activation(out=w, in_=w, func=AF.Ln, bias=1.0), reads=[yTcB[c]], writes=[yTcB[c]]); yield
            S.op("act", lambda e: e.activation(out=w, in_=w, func=AF.Exp, scale=-1.0), reads=[yTcB[c]], writes=[yTcB[c]]); yield
            S.op("dve", lambda e: e.tensor_tensor(out=yc[:, c, 0:n], in0=t_, in1=w, op=ALU.mult), reads=[ccB[c], yTcB[c]], writes=[ycB[c]]); yield

        def conv_s1():
            yield from interleave_gen([glu_chain(c) for c in range(3)])

        def conv_s2():
            cm = []
            for c in range(3):
                cm.append((yield from conv_mm(c)))
            S.set_flag(key + "_conv")
            yield
            yield from interleave_gen([conv_chunk(c, cm[c][0], cm[c][1]) for c in range(3)])
            bS, tS = yield from acquire()
            ps_, pq = tS[:, 0:n], tS[:, 256:256 + n]
            S.mm(bS, ps_, [(ones[:, :], sq[:, c, 0:n]) for c in range(3)], reads=[sqB, onesB])
            S.mm(bS, pq, [(ones[:, :], sq[:, 3 + c, 0:n]) for c in range(3)], reads=[sqB, onesB])
            yield
            S.op("act", lambda e: e.activation(out=mt, in_=ps_, func=AF.Identity, scale=1.0 / W_C), reads=[bS], writes=[stmB]); yield
            S.op("dve", lambda e: e.tensor_tensor(out=vt, in0=mt, in1=mt, op=ALU.mult), reads=[stmB], writes=[stvB]); yield
            S.op("dve", lambda e: e.scalar_tensor_tensor(out=vt, in0=pq, scalar=1.0 / W_C, in1=vt, op0=ALU.mult, op1=ALU.subtract),
                 reads=[bS, stvB], writes=[stvB])
            rel(bS)
            yield
            S.op("act", lambda e: e.activation(out=vt, in_=vt, func=AF.Ln, bias=EPS), reads=[stvB], writes=[stvB]); yield
            S.op("act", lambda e: e.activation(out=vt, in_=vt, func=AF.Exp, scale=-0.5), reads=[stvB], writes=[stvB]); yield
            yield from interleave_gen([silu_chain(c) for c in range(3)])

        def wout_mm(m, bB, ap):
            ms = slice(m * 128, (m + 1) * 128)
            pairs = [(RO[:, 0, ms], ya[:, 0, 0:n]), (RO[:, 1, ms], ya[:, 1, 0:n])]
            pairs += [(RO[0:96, 2 + h, ms], yb[0:96, h, 0:n]) for h in range(4)]
            pairs += [(RO[:, 6, ms], yc[:, 0, 0:n]), (RO[:, 7, ms], yc[:, 1, 0:n]), (w9[:, ms], yc[:, 2, 0:n])]
            S.mm(bB, ap[:, 0:n], pairs, reads=[ROB, w9B] + yaB + ybB + ycB)

        def s3():
            for m0 in range(0, 8, 2):
                bB, t = yield from acquire()
                for j in range(2):
                    wout_mm(m0 + j, bB, t[:, 256 * j:256 * j + 256])
                yield
                if (m0 // 2) % 2 == 0:
                    S.op("act", lambda e: e.activation(out=yT[:, m0:m0 + 2, 0:n], in_=t[:, :].rearrange("p (a t) -> p a t", a=2)[:, :, 0:n], func=AF.Copy),
                         reads=[bB], writes=[yTcB[m0], yTcB[m0 + 1]])
                else:
                    S.op("dve", lambda e: e.tensor_copy(out=yT[:, m0:m0 + 2, 0:n], in_=t[:, :].rearrange("p (a t) -> p a t", a=2)[:, :, 0:n]),
                         reads=[bB], writes=[yTcB[m0], yTcB[m0 + 1]])
                rel(bB)
                yield
            yield from post_norm_gen(l, 1, sub, par, lambda c: yT[:, c, 0:n], lambda c: yTcB[c], yT[:, :, 0:n])
            yield

        def with_done(gen, flag):
            yield from gen
            S.set_flag(flag)
            yield

        def s3_stream():
            for f_ in ("_dconv", "_dsg", "_dpool"):
                yield ("wait", key + f_)
            yield from s3()

        s1_streams = [glu_chain(c) for c in range(3)] + [sg_s1(), pool_s1()]
        s23_streams = [with_done(conv_s2(), key + "_dconv"), with_done(sg_s2(), key + "_dsg"), with_done(pool_s2(), key + "_dpool")]
        if not (sub.is_pre and last_layer):
            s23_streams.append(s3_stream())
        return s1_streams, s23_streams

    def delayed(gen, k):
        for _ in range(k):
            yield
        yield from gen

    qbufs = [(qT, qTcB), (uT, q2cB)]

    def attn_q_gen(p, li, l, sub, qsel):
        n = sub.n
        ld = pl_loads[(p, li)]
        RQ, RQB = ring[ld["wq"] % 4], ringB[ld["wq"] % 4]
        hv, hb_ = hview(sub)
        qT, qTcB = qbufs[qsel]
        for m in range(0, 8, 2):
            bB, t = yield from acquire()
            for j in range(2):
                S.mm(bB, t[:, 256 * j:256 * j + n], [(RQ[:, k, (m + j) * 128:(m + j + 1) * 128], hv(k)) for k in range(8)], reads=[RQB, hb_])
            yield
            ap3 = t[:, :].rearrange("p (a t) -> p a t", a=2)[:, :, 0:n]
            if (m // 2) % 2 == 0:
                S.op("act", lambda e, m=m, ap3=ap3: e.activation(out=qT[:, m:m + 2, 0:n], in_=ap3, func=AF.Copy), reads=[bB], writes=[qTcB[m // 2]], dur=0.6)
            else:
                S.op("dve", lambda e, m=m, ap3=ap3: e.tensor_copy(out=qT[:, m:m + 2, 0:n], in_=ap3), reads=[bB], writes=[qTcB[m // 2]], dur=0.6)
            rel(bB)
            yield

    def attn_main_gen(p, li, l, sub, par, qsel):
        n, off, s = sub.n, sub.off, sub.slot
        ld = pl_loads[(p, li)]
        RO, ROB = ring[ld["wo"] % 4], ringB[ld["wo"] % 4]
        qT, qTcB = qbufs[qsel]

        def head_gen(h):
            hp = h % 2
            bB, t = yield from acquire()
            for mc in range(2):
                S.mm(bB, t[:, 256 * mc:256 * mc + n], [(kT[:, li, 2 * h + dc, mc * 128:(mc + 1) * 128], qT[:, 2 * h + dc, 0:n]) for dc in range(2)],
                     reads=[kTB, qTcB[h]])
            yield
            S.op("act", lambda e: e.activation(out=ET[:, hp, :, 0:n], in_=t[:, :].rearrange("p (a t) -> p a t", a=2)[:, :, 0:n],
                                               func=AF.Exp, scale=1.0 / 16.0), reads=[bB], writes=[ETB[hp]])
            rel(bB)
            yield
            bS, tS = yield from acquire()
            S.mm(bS, tS[:, 0:n], [(ones[:, :], ET[:, hp, 0, 0:n]), (ones[:, :], ET[:, hp, 1, 0:n])], reads=[onesB, ETB[hp]])
            bO, tO = yield from acquire()
            for dc in range(2):
                S.mm(bO, tO[:, 256 * dc:256 * dc + n], [(vv[:, li, mc, (2 * h + dc) * 128:(2 * h + dc + 1) * 128], ET[:, hp, mc, 0:n]) for mc in range(2)],
                     reads=[vvB, ETB[hp]])
            yield
            S.op("act", lambda e: e.activation(out=rec[:, hp, 0:n], in_=tS[:, 0:n], func=AF.Ln), reads=[bS], writes=[recB[hp]])
            rel(bS)
            yield
            S.op("act", lambda e: e.activation(out=rec[:, hp, 0:n], in_=rec[:, hp, 0:n], func=AF.Exp, scale=-1.0), reads=[recB[hp]], writes=[recB[hp]]); yield
            for dc in range(2):
                S.op("dve", lambda e, dc=dc: e.tensor_tensor(out=oT[:, 2 * h + dc, 0:n], in0=tO[:, 256 * dc:256 * dc + n], in1=rec[:, hp, 0:n], op=ALU.mult),
                     reads=[bO, recB[hp]], writes=[oTB])
                if dc == 1:
                    rel(bO)
                yield

        yield from interleave_gen([head_gen(0), head_gen(1)])
        yield from interleave_gen([head_gen(2), head_gen(3)])
        for m in range(0, 8, 2):
            bB, t = yield from acquire()
            for j in range(2):
                S.mm(bB, t[:, 256 * j:256 * j + n], [(RO[:, k, (m + j) * 128:(m + j + 1) * 128], oT[:, k, 0:n]) for k in range(8)], reads=[ROB, oTB])
            yield
            ap3 = t[:, :].rearrange("p (a t) -> p a t", a=2)[:, :, 0:n]
            if (m // 2) % 2 == 0:
                S.op("act", lambda e, m=m, ap3=ap3: e.activation(out=yT[:, m:m + 2, 0:n], in_=ap3, func=AF.Copy), reads=[bB], writes=[yTcB[m], yTcB[m + 1]], dur=0.6)
            else:
                S.op("dve", lambda e, m=m, ap3=ap3: e.tensor_copy(out=yT[:, m:m + 2, 0:n], in_=ap3), reads=[bB], writes=[yTcB[m], yTcB[m + 1]], dur=0.6)
            rel(bB)
            yield
        yield from post_norm_gen(l, 3, sub, par, lambda c: yT[:, c, 0:n], lambda c: yTcB[c], yT[:, :, 0:n])
        yield


    def ffn_w1(p, li, l, q, sub, ubuf):
        n, off, s = sub.n, sub.off, sub.slot
        ld = pl_loads[(p, li)]
        R1, R1B = ring[ld[("w1", q)] % 4], ringB[ld[("w1", q)] % 4]
        uT, uTB = ubuf

        def u_evac(f, bB, ap3, gi):
            S.op("act", lambda e: e.activation(out=rl[:, :, 0:n], in_=ap3, func=AF.Relu), reads=[bB], writes=rlB)
            S.op("dve", lambda e: e.tensor_tensor(out=uT[:, f:f + 2, 0:n], in0=rl[:, :, 0:n], in1=rl[:, :, 0:n], op=ALU.mult),
                 reads=rlB, writes=[uTB])

        grouped2(list(range(8)),
                 lambda f, bB, ap: S.mm(bB, ap[:, 0:n], [(R1[:, k, f * 128:(f + 1) * 128], hTf[:, k, off:off + n]) for k in range(8)], reads=[R1B, hfB[s]]),
                 u_evac, n)

    def ffn_w2(p, li, l, q, sub, ubuf):
        n, off, s = sub.n, sub.off, sub.slot
        ld = pl_loads[(p, li)]
        R2, R2B = ring[ld[("w2", q)] % 4], ringB[ld[("w2", q)] % 4]
        uT, uTB = ubuf

        def y_evac(m, bB, ap3, gi):
            if q == 0:
                S.op("act", lambda e: e.activation(out=yacc[:, m:m + 2, off:off + n], in_=ap3, func=AF.Copy),
                     reads=[bB], writes=[yaccB[s]])
            else:
                S.op("dve", lambda e: e.tensor_tensor(out=yacc[:, m:m + 2, off:off + n], in0=ap3, in1=yacc[:, m:m + 2, off:off + n], op=ALU.add),
                     reads=[bB, yaccB[s]], writes=[yaccB[s]])

        grouped2(list(range(8)),
                 lambda m, bB, ap: S.mm(bB, ap[:, 0:n], [(R2[:, f, m * 128:(m + 1) * 128], uT[:, f, 0:n]) for f in range(8)], reads=[R2B, uTB]),
                 y_evac, n)


    import os as _os
    MAXPH = int(_os.environ.get("K_MAXPH", "100000"))
    phc = [0]

    def ph_ok():
        phc[0] += 1
        return phc[0] <= MAXPH


    S.alias_barrier([yTB], yTcB)
    stored = {}
    par_ctr = [0]

    def npar():
        par_ctr[0] += 1
        return par_ctr[0] % 2

    for p in range(NPASS):
        subs = []
        if p == 0:
            subs.append(SubT(0, 0, HALO, True, False, 0))
        for i in range(NSUBP):
            subs.append(SubT(1 + i, HALO + SUB * i, SUB, False, (p == 0 and i == 0), HALO + p * PASS_T + SUB * i))
        if p == 0:
            for sub in subs:
                S.dma("sp", xres[:, :, sub.off:sub.off + sub.n], xT[:, :, sub.dram0:sub.dram0 + sub.n].rearrange("c p t -> p c t"),
                      writes=[xB[sub.slot]])
        for li, l in enumerate(LAYERS):
            last_layer = (li == NLY - 1)
            ld = pl_loads[(p, li)]
            for r_ in range(3):
                S.dma("sp", rowb[:, r_, :], rowp_d[l, r_:r_ + 1, :].partition_broadcast(128), writes=[rowbB])
            if not ph_ok():
                continue
            S.alias_barrier(yaccB + hfB, mixer_bufs)
            if p == 0:
                S.op("dve", lambda e: e.memset(zA[:, :, 0:16], 0.0), writes=zAB)
                S.op("dve", lambda e: e.memset(hbuf[:, :, 0:32], 0.0), writes=hbufB)
            else:
                S.op("dve", lambda e: e.tensor_copy(out=zA[:, :, 0:16], in_=stZ[:, li, :, :]), reads=[stZB], writes=zAB)
                S.op("dve", lambda e: e.tensor_copy(out=hbuf[:, :, 0:32], in_=stH[:, li, :, :]), reads=[stHB], writes=hbufB)
            pars = [npar() for _ in subs]
            pre_norm(l, 0, subs[0], pars[0])
            for c_ in range(3):
                a_ = CW_(l, c_, 0)
                S.op("dve", lambda e, c_=c_, a_=a_: e.tensor_tensor(
                    out=dg[:, 31 * c_:31 * c_ + 31, :],
                    in0=ident[:, :].unsqueeze(1).broadcast_to([128, CONV_K, 128]),
                    in1=pp[:, a_:a_ + CONV_K].unsqueeze(2).broadcast_to([128, CONV_K, 128]), op=ALU.mult),
                    reads=[identB, ppB], writes=[dgB[c_]], dur=4.5)
            S.alias_barrier([uTB], gsB)
            if len(subs) > 1:
                pre_norm(l, 0, subs[1], pars[1])
            stages = [mixer_body(p, li, l, sub, pars[i], last_layer, (f"{p}_{li}_{subs[i - 1].slot}" if i > 0 else None))
                      for i, sub in enumerate(subs)]
            run_scheduled(S, stages[0][0])
            if len(subs) == 1:
                done(ld["inA"]); done(ld["inB"])
            for i, sub in enumerate(subs):
                g_ = list(stages[i][1])
                if i + 1 < len(subs):
                    g_ = stages[i + 1][0] + g_
                run_scheduled(S, g_)
                if i + 1 == len(subs) - 1:
                    done(ld["inA"]); done(ld["inB"])
                if i + 2 < len(subs):
                    pre_norm(l, 0, subs[i + 2], pars[i + 2])
            S.op("dve", lambda e: e.tensor_copy(out=stZ[:, li, :, :], in_=zA[:, :, 0:16]), reads=zAB, writes=[stZB])
            S.op("dve", lambda e: e.tensor_copy(out=stH[:, li, :, :], in_=hbuf[:, :, 0:32]), reads=hbufB, writes=[stHB])
            done(ld["out"])
            asubs = [sb_ for sb_ in subs if not (sb_.is_pre and last_layer)]
            if not ph_ok():
                continue
            if p == 0:
                S.alias_barrier(yTcB, [yTB])
                kv_prologue(li, l)
                S.alias_barrier([yTB], yTcB)
            S.alias_barrier(attn_alias_src, attn_bufs)
            S.alias_barrier(mixer_bufs, hfB)
            S.alias_barrier([uTB] + gsB, q2cB)
            pars = [npar() for _ in asubs]
            pre_norm(l, 2, asubs[0], pars[0])
            if len(asubs) > 1:
                pre_norm(l, 2, asubs[1], pars[1])
            run_scheduled(S, [attn_q_gen(p, li, l, asubs[0], 0)])

            def chain2(a, b):
                yield from a
                yield from b

            for i, sub in enumerate(asubs):
                g_ = [attn_main_gen(p, li, l, sub, pars[i], i % 2)]
                if i + 1 < len(asubs):
                    qg = attn_q_gen(p, li, l, asubs[i + 1], (i + 1) % 2)
                    if i + 1 >= 2:
                        qg = chain2(pre_norm_gen(l, 2, asubs[i + 1], pars[i + 1]), qg)
                    g_.append(qg)
                if i >= 1:
                    g_.append(pre_norm_gen(l, 4, asubs[i - 1], npar(), True))
                run_scheduled(S, g_)
            done(ld["wq"]); done(ld["wo"])
            S.alias_barrier(attn_bufs, attn_alias_src)
            if not ph_ok():
                continue
            S.alias_barrier(mixer_bufs, yaccB + hfB)
            S.alias_barrier(gsB + q2cB, [uTB])
            pre_norm(l, 4, asubs[-1], npar(), ffn=True)
            items = [(q, sub) for q in range(4) for sub in asubs]
            ubufs = [(uT, uTB), (hT[:, :, 0:SUB], hB[0])]

            def ub_of(ix):
                return ubufs[ix % 2]

            ffn_w1(p, li, l, items[0][0], items[0][1], ub_of(0))
            for ix, (q, sub) in enumerate(items):
                last_of_q = (sub is asubs[-1])
                if last_of_q:
                    done(ld[("w1", q)])
                nxt = items[ix + 1] if ix + 1 < len(items) else None
                pipelined = nxt is not None
                if pipelined:
                    ffn_w1(p, li, l, nxt[0], nxt[1], ub_of(ix + 1))
                ffn_w2(p, li, l, q, sub, ub_of(ix))
                if last_of_q:
                    done(ld[("w2", q)])
                if True:
                    if q == 3:
                        post_norm_residual(l, 5, sub, npar(), lambda c, sub=sub: yacc[:, c, sub.off:sub.off + sub.n], lambda c, sub=sub: yaccB[sub.slot], yacc[:, :, sub.off:sub.off + sub.n])
                        if last_layer and not sub.is_pre and not stored.get((p, sub.slot)):
                            stored[(p, sub.slot)] = True
                            o0 = sub.dram0 - HALO
                            S.dma("sp", outT[:, :, o0:o0 + sub.n].rearrange("c p t -> p c t"), xres[:, :, sub.off:sub.off + sub.n], reads=[xB[sub.slot]])
                            if p + 1 < NPASS:
                                d1 = sub.dram0 + PASS_T
                                S.dma("sp", xres[:, :, sub.off:sub.off + sub.n], xT[:, :, d1:d1 + sub.n].rearrange("c p t -> p c t"),
                                      writes=[xB[sub.slot]])
                if nxt is not None and not pipelined:
                    ffn_w1(p, li, l, nxt[0], nxt[1], ub_of(ix + 1))
        for sub in subs:
            if sub.is_pre or stored.get((p, sub.slot)):
                continue
            o0 = sub.dram0 - HALO
            S.dma("sp", outT[:, :, o0:o0 + sub.n].rearrange("c p t -> p c t"), xres[:, :, sub.off:sub.off + sub.n], reads=[xB[sub.slot]])
    S.final_wait("sp", S.allb)
    return nc


def _pack_pp(inp):
    pp = np.zeros((128, DEPTH * NL), np.float32)

    def col(v):
        return np.ascontiguousarray(v.reshape(-1, 128).T)

    wins = (2, 4, 8, 16)
    for l in range(DEPTH):
        for i, name in enumerate(["pre_mix_g", "post_mix_g", "pre_x_g", "post_x_g", "pre_ff_g", "post_ff_g", "mem_g"]):
            pp[:, G_(l, i, 0):G_(l, i, 0) + 8] = col(inp[name][l])
        b = inp["b_in"][l]
        pp[:, BA_(l, 0):BA_(l, 0) + 2] = col(b[0:256])
        for h in range(4):
            pp[0:96, BU_(l, h)] = b[256 + 96 * h:256 + 96 * h + 96]
        pp[:, BCA_(l, 0):BCA_(l, 0) + 3] = col(b[1024:1408])
        pp[:, BCG_(l, 0):BCG_(l, 0) + 3] = col(b[1408:1792])
        pp[:, PSC_(l, 0):PSC_(l, 0) + 2] = col(inp["pool_scale"][l])
        for c in range(2):
            for hh in range(2):
                pp[64 * hh:64 * hh + 64, IW_(l, c)] = np.float32(1.0) / np.float32(wins[2 * c + hh])
        pp[:, CB_(l, 0):CB_(l, 0) + 3] = col(inp["conv_b"][l])
        pp[:, CLG_(l, 0):CLG_(l, 0) + 3] = col(inp["conv_ln_g"][l])
        pp[:, CLB_(l, 0):CLB_(l, 0) + 3] = col(inp["conv_ln_b"][l])
        cw = inp["conv_w"][l]
        for c in range(3):
            pp[:, CW_(l, c, 0):CW_(l, c, 0) + 31] = cw[:, 128 * c:128 * c + 128].T
    return pp


def _common_inputs(inp):
    f = lambda a: np.ascontiguousarray(np.asarray(a, dtype=np.float32))
    com = {}
    com["pp"] = _pack_pp({k: np.asarray(v, np.float32) for k, v in inp.items()})
    rowp = np.zeros((DEPTH, 3, 384), np.float32)
    for l in range(DEPTH):
        rowp[l, 0] = inp["b_in"][l][640:1024]
        rowp[l, 1] = inp["sg_ln_g"][l]
        rowp[l, 2] = inp["sg_ln_b"][l]
    com["rowp"] = rowp
    com["sgb"] = f(np.asarray(inp["sg_b"]).reshape(DEPTH, 512))
    com["sgwT"] = f(np.transpose(np.asarray(inp["sg_w"]), (0, 1, 3, 2)))
    com["poolw"] = f(inp["pool_w"])
    com["ident"] = np.eye(128, dtype=np.float32)
    com["triu"] = np.triu(np.ones((128, 128), np.float32))
    for k in ("w_in", "w_out", "wq", "wk", "wv", "wo", "w_ff1", "w_ff2"):
        com[k] = f(inp[k])
    return com


def _core_inputs(x, mem, c):
    b, j = c // 4, c % 4
    t0 = j * TPC
    own = x[b, t0:t0 + TPC]
    halo = x[b, t0 - HALO:t0] if j > 0 else np.zeros((HALO, D), np.float32)
    xf = np.concatenate([halo, own], axis=0)
    d = {}
    d["xT"] = np.ascontiguousarray(xf.T).reshape(8, 128, HALO + TPC)
    d["memT"] = np.ascontiguousarray(mem[b].T).reshape(8, 128, 256)
    d["mask"] = np.full((128, 1), 1.0 if j > 0 else 0.0, np.float32)
    wins = (2, 4, 8, 16)
    ic = np.zeros((128, 2, 16), np.float32)
    tpos = np.arange(16, dtype=np.float32) + 1.0
    for cc_ in range(2):
        for hh in range(2):
            w = np.float32(wins[2 * cc_ + hh])
            cnt = np.minimum(tpos, w) if j == 0 else np.full(16, w, np.float32)
            ic[64 * hh:64 * hh + 64, cc_, :] = (np.float32(1.0) / cnt)[None, :]
    d["icnt"] = ic
    return d


FUSED = True
_PROG = {}


def _get_prog(layers):
    key = tuple(layers)
    if key not in _PROG:
        _PROG[key] = build_program(list(layers))
    return _PROG[key]


def _run(layers, x, mem, com):
    nc = _get_prog(layers)
    in_maps = []
    for c in range(NCORE):
        d = dict(com)
        d.update(_core_inputs(x, mem, c))
        in_maps.append(d)
    res = run_bass_kernel_spmd(nc, in_maps, core_ids=list(range(NCORE)))
    out = np.empty((BATCH, SEQ, D), np.float32)
    for c in range(NCORE):
        b, j = c // 4, c % 4
        o = np.asarray(res.results[c]["outT"]).reshape(D, TPC)
        out[b, j * TPC:(j + 1) * TPC] = o.T
    return out


def kernel(**inputs):
    inp = {k: np.asarray(v) for k, v in inputs.items()}
    x = np.asarray(inp["x"], np.float32)
    mem = np.asarray(inp["mem"], np.float32)
    com = _common_inputs(inp)
    if FUSED:
        return _run(range(DEPTH), x, mem, com)
    for l in range(DEPTH):
        x = _run([l], x, mem, com)
    return x
```

```python
import numpy as np
import concourse.bass as bass
import concourse.mybir as mybir
from concourse.bass_utils import run_bass_kernel_spmd

F32 = mybir.dt.float32
BF16 = mybir.dt.bfloat16
AF = mybir.ActivationFunctionType
ALU = mybir.AluOpType

D = 1024
SEQ = 16384
BATCH = 2
NCORE = 8
TPC = 4096
HALO = 128
SUB = 256
NSUBP = 4
PASS_T = SUB * NSUBP
NPASS = TPC // PASS_T
RES_T = HALO + PASS_T
DEPTH = 2
EPS = 1e-6
W_A, W_B, W_C = 256, 384, 384
D_IN = 1792
D_FF = 4096
CONV_K = 31
GELU_C = 0.7978845608028654

NL = 174
def G_(l, i, c): return l * NL + 8 * i + c
def BA_(l, c): return l * NL + 56 + c
def BU_(l, h): return l * NL + 58 + h
def BCA_(l, c): return l * NL + 62 + c
def BCG_(l, c): return l * NL + 65 + c
def PSC_(l, c): return l * NL + 68 + c
def IW_(l, c): return l * NL + 70 + c
def CB_(l, c): return l * NL + 72 + c
def CLG_(l, c): return l * NL + 75 + c
def CLB_(l, c): return l * NL + 78 + c
def CW_(l, c, k): return l * NL + 81 + c * 31 + k


class Buf:
    __slots__ = ("name", "last_w", "readers", "dsem", "dcnt", "excl")

    def __init__(self, name):
        self.name = name
        self.last_w = None
        self.readers = {}
        self.dsem = None
        self.dcnt = 0
        self.excl = False


class Sched:
    def __init__(self, nc):
        self.nc = nc
        self.eng = {"pe": nc.tensor, "act": nc.scalar, "dve": nc.vector, "pool": nc.gpsimd, "sp": nc.sync}
        self.semh = {}
        self.cnt = {}
        self.waited = {k: {} for k in self.eng}
        for k in self.eng:
            self.semh["e_" + k] = nc.alloc_semaphore("e_" + k)
            self.cnt[k] = 0
        self.nbuf = 0
        self.allb = []
        self.defer = False
        self.pend = []
        self.flags = set()
        self.eng_free = {k: 0.0 for k in self.eng}
        self.ev_t = {}

    def buf(self, name):
        self.nbuf += 1
        b = Buf(f"{name}_{self.nbuf}")
        self.allb.append(b)
        return b

    def _needs(self, e, reads, writes):
        own = "e_" + e
        need = {}

        def add(ev, allow_own):
            if ev is None:
                return
            k, v = ev
            if k == own and not allow_own:
                return
            if need.get(k, 0) < v:
                need[k] = v

        for b in reads:
            add(b.last_w, True)
            if b.excl:
                for k, v in b.readers.items():
                    add((k, v), False)
        for b in writes:
            add(b.last_w, False)
            for k, v in b.readers.items():
                add((k, v), False)
        return need

    def _do_waits(self, e, need):
        w = self.waited[e]
        h = self.eng[e]
        for k, v in need.items():
            if w.get(k, 0) < v:
                h.wait_ge(self.semh[k], v)
                w[k] = v

    def _record(self, ev, reads, writes):
        k, v = ev
        for b in reads:
            if b.readers.get(k, 0) < v:
                b.readers[k] = v
        for b in writes:
            b.last_w = ev
            b.readers = {}

    def _est(self, e, need, dur):
        t = self.eng_free[e]
        for k, v in need.items():
            tv = self.ev_t.get((k, v))
            if tv is not None and tv + 0.25 > t:
                t = tv + 0.25
        return t, t + dur

    def est_start(self, d):
        if d[0] in ("flag", "rel"):
            return -1.0
        if d[0] == "op":
            _, e, fn, reads, writes, dur = d
            return self._est(e, self._needs(e, reads, writes), dur)[0]
        _, out_buf, out_ap, pairs, reads, dur = d
        return self._est("pe", self._needs("pe", reads, [out_buf]), dur)[0]

    def commit(self, d):
        if d[0] == "flag":
            self.flags.add(d[1])
        elif d[0] == "rel":
            self.on_rel(d[1])
        elif d[0] == "op":
            self.op(d[1], d[2], d[3], d[4], dur=d[5], force=True)
        else:
            self.mm(d[1], d[2], d[3], d[4], force=True)

    def set_flag(self, name):
        if self.defer:
            self.pend.append(("flag", name))
        else:
            self.flags.add(name)

    def op(self, e, fn, reads=(), writes=(), dur=None, force=False):
        if dur is None:
            dur = 0.36 if e == "act" else 0.43
        if self.defer and not force:
            self.pend.append(("op", e, fn, tuple(reads), tuple(writes), dur))
            return
        need = self._needs(e, reads, writes)
        t0, t1 = self._est(e, need, dur)
        self._do_waits(e, need)
        inst = fn(self.eng[e])
        self.cnt[e] += 1
        inst.then_inc(self.semh["e_" + e], 1)
        self.eng_free[e] = t1
        self.ev_t[("e_" + e, self.cnt[e])] = t1
        self._record(("e_" + e, self.cnt[e]), reads, writes)

    def mm(self, out_buf, out_ap, pairs, reads, force=False):
        e = "pe"
        if self.defer and not force:
            self.pend.append(("mm", out_buf, out_ap, list(pairs), tuple(reads), 0.13 * len(pairs)))
            return
        need = self._needs(e, reads, [out_buf])
        t0, t1 = self._est(e, need, 0.13 * len(pairs))
        self.eng_free[e] = t1
        self.ev_t[("e_pe", self.cnt[e] + 1)] = t1
        self._do_waits(e, need)
        n = len(pairs)
        inst = None
        for i, (l, r) in enumerate(pairs):
            inst = self.nc.tensor.matmul(out_ap, l, r, start=(i == 0), stop=(i == n - 1))
        self.cnt[e] += 1
        inst.then_inc(self.semh["e_pe"], 1)
        self._record(("e_pe", self.cnt[e]), reads, [out_buf])

    def dma(self, q, out_ap, in_ap, reads=(), writes=()):
        b = writes[0] if writes else reads[0]
        need = {}
        for rb in reads:
            if rb.last_w is not None:
                k, v = rb.last_w
                need[k] = max(need.get(k, 0), v)
        for wb in writes:
            if wb.last_w is not None:
                k, v = wb.last_w
                need[k] = max(need.get(k, 0), v)
            for k, v in wb.readers.items():
                need[k] = max(need.get(k, 0), v)
        self._do_waits(q, need)
        if b.dsem is None:
            b.dsem = "d_" + b.name
            self.semh[b.dsem] = self.nc.alloc_semaphore(b.dsem)
        b.dcnt += 16
        self.eng[q].dma_start(out=out_ap, in_=in_ap).then_inc(self.semh[b.dsem], 16)
        self._record((b.dsem, b.dcnt), reads, writes)

    def alias_barrier(self, from_bufs, to_bufs):
        evs = {}
        for b in from_bufs:
            if b.last_w is not None:
                k, v = b.last_w
                evs[k] = max(evs.get(k, 0), v)
            for k, v in b.readers.items():
                evs[k] = max(evs.get(k, 0), v)
        for b in to_bufs:
            for k, v in evs.items():
                if b.readers.get(k, 0) < v:
                    b.readers[k] = v

    def final_wait(self, q, bufs):
        need = {}
        for b in bufs:
            if b.last_w is not None:
                k, v = b.last_w
                need[k] = max(need.get(k, 0), v)
            for k, v in b.readers.items():
                need[k] = max(need.get(k, 0), v)
        self._do_waits(q, need)


def run_interleaved(gens):
    gens = list(gens)
    while gens:
        nxt = []
        for g in gens:
            try:
                next(g)
                nxt.append(g)
            except StopIteration:
                pass
        gens = nxt


def run_scheduled(S, gens):
    gens = list(gens)
    pend = [[] for _ in gens]
    alive = [True] * len(gens)
    blocked = [None] * len(gens)
    while True:
        progress = False
        for i, g in enumerate(gens):
            while alive[i] and not pend[i]:
                if blocked[i] is not None:
                    if S.flag_ok(blocked[i]):
                        blocked[i] = None
                    else:
                        break
                S.defer, S.pend = True, []
                try:
                    r = next(g)
                except StopIteration:
                    alive[i] = False
                    r = None
                finally:
                    S.defer = False
                pend[i].extend(S.pend)
                S.pend = []
                if isinstance(r, tuple) and r and r[0] == "wait":
                    blocked[i] = r[1]
                    if pend[i]:
                        break
        best, bt = None, None
        for i in range(len(gens)):
            if pend[i]:
                t = S.est_start(pend[i][0])
                if bt is None or t < bt:
                    best, bt = i, t
        if best is None:
            if any(alive):
                if all((not alive[i]) or (blocked[i] is not None and not S.flag_ok(blocked[i])) for i in range(len(gens))):
                    raise RuntimeError("scheduler deadlock on flags: %s" % [b for b in blocked if b])
                continue
            break
        S.commit(pend[best].pop(0))


class SubT:
    def __init__(self, slot, off, n, is_pre, first_real, dram0):
        self.slot, self.off, self.n, self.is_pre, self.first_real, self.dram0 = slot, off, n, is_pre, first_real, dram0


def build_program(LAYERS):
    nc = bass.Bass("TRN2", target_bir_lowering=False)
    S = Sched(nc)
    NLY = len(LAYERS)
    NPP = DEPTH * NL

    def din(name, shape):
        return nc.dram_tensor(name, shape, F32, kind="ExternalInput").ap()

    xT = din("xT", [8, 128, HALO + TPC])
    memT = din("memT", [8, 128, 256])
    pp_d = din("pp", [128, NPP])
    rowp_d = din("rowp", [DEPTH, 3, 384])
    sgb_d = din("sgb", [DEPTH, 512])
    sgwT_d = din("sgwT", [DEPTH, 4, 128, 128])
    poolw_d = din("poolw", [DEPTH, 4, 64, 64])
    ident_d = din("ident", [128, 128])
    triu_d = din("triu", [128, 128])
    icnt_d = din("icnt", [128, 2, 16])
    mask_d = din("mask", [128, 1])
    w_in_d = din("w_in", [DEPTH, D, D_IN])
    w_out_d = din("w_out", [DEPTH, D, D])
    wq_d = din("wq", [DEPTH, D, D])
    wk_d = din("wk", [DEPTH, D, D])
    wv_d = din("wv", [DEPTH, D, D])
    wo_d = din("wo", [DEPTH, D, D])
    w1_d = din("w_ff1", [DEPTH, D, D_FF])
    w2_d = din("w_ff2", [DEPTH, D_FF, D])
    outT = nc.dram_tensor("outT", [8, 128, TPC], F32, kind="ExternalOutput").ap()

    def sb(name, shape, dt):
        return nc.alloc_sbuf_tensor("s_" + name, shape, dt)

    xres = sb("xres", [128, 8, RES_T], F32)
    hT = sb("hT", [128, 8, 2 * SUB], BF16)
    sq = sb("sq", [128, 8, SUB], BF16)
    st = sb("st", [128, 4, SUB], F32)
    kT = sb("kT", [128, NLY, 8, 256], BF16)
    vv = sb("vv", [128, NLY, 2, 1024], BF16)
    pp = sb("pp", [128, NPP], F32)
    ppn = sb("ppn", [128, 9 * DEPTH], F32)
    rowb = sb("rowb", [128, 3, 384], F32)
    sgw = sb("sgw", [128, NLY, 4, 128], BF16)
    bd = sb("bd", [128, NLY, 2, 128], BF16)
    ones = sb("ones", [128, 128], BF16)
    ident = sb("ident", [128, 128], F32)
    sgbr = sb("sgbr", [1, NLY * 512], BF16)
    icnt = sb("icnt", [128, 2, 16], F32)
    mask = sb("mask", [128, 1], F32)
    sm = sb("sm", [128, 2, 16], F32)
    stZ = sb("stZ", [128, NLY, 2, 16], F32)
    stH = sb("stH", [128, NLY, 3, 32], BF16)
    uT = sb("uT", [128, 8, SUB], BF16)
    rl = sb("rl", [128, 2, SUB], F32)
    gs = uT[:, :, :].rearrange("p c t -> p (c t)").bitcast(F32).rearrange("p (c t) -> p c t", c=4)
    ring = [sb(f"ring{i}", [128, 8, 1024], BF16) for i in range(4)]
    w9 = sb("w9", [128, 1024], BF16)
    MR_F = 8 * RES_T
    mr_words = 0
    carve = {}

    def carve_f32(name, nwords):
        nonlocal mr_words
        carve[name] = (mr_words, nwords)
        mr_words += nwords

    carve_f32("zA", 2 * (16 + SUB))
    carve_f32("T1", 16 + SUB)
    carve_f32("T2", 16 + SUB)
    carve_f32("tmp", 4 * 384)
    carve_f32("cc", 3 * SUB)
    carve_f32("yT", 8 * SUB)
    carve_f32("pb", 2 * SUB // 2)
    carve_f32("ya", 2 * SUB // 2)
    carve_f32("ub", 4 * SUB // 2)
    carve_f32("vln", 2 * 384 // 2)
    carve_f32("yb", 4 * SUB // 2)
    carve_f32("hbuf", 3 * (32 + SUB) // 2)
    carve_f32("yc", 3 * SUB // 2)
    carve_f32("dg", 93 * 128 // 2)
    HF_W = 8 * RES_T // 2
    mr_total = max(mr_words, MR_F + HF_W)
    MR = sb("MR", [128, mr_total], F32)

    def mrv(name, dt, pattern=None, **kw):
        o, nw = carve[name]
        v = MR[:, o:o + nw]
        if dt == BF16:
            v = v.bitcast(BF16)
        if pattern:
            v = v.rearrange(pattern, **kw)
        return v

    zA = mrv("zA", F32, "p (c t) -> p c t", c=2)
    T1 = mrv("T1", F32)
    T2 = mrv("T2", F32)
    tmp = mrv("tmp", F32, "p (c t) -> p c t", c=4)
    cc = mrv("cc", F32, "p (c t) -> p c t", c=3)
    yT = mrv("yT", F32, "p (c t) -> p c t", c=8)
    pb = mrv("pb", BF16, "p (c t) -> p c t", c=2)
    ya = mrv("ya", BF16, "p (c t) -> p c t", c=2)
    ub = mrv("ub", BF16, "p (c t) -> p c t", c=4)
    vln = mrv("vln", BF16, "p (c t) -> p c t", c=2)
    yb = mrv("yb", BF16, "p (c t) -> p c t", c=4)
    hbuf = mrv("hbuf", BF16, "p (c t) -> p c t", c=3)
    yc = mrv("yc", BF16, "p (c t) -> p c t", c=3)
    dg = mrv("dg", BF16, "p (k m) -> p k m", k=93)
    hTf = MR[:, MR_F:MR_F + HF_W].bitcast(BF16).rearrange("p (c t) -> p c t", c=8)
    triu = MR[:, carve["yT"][0]:carve["yT"][0] + 128]
    yacc = MR[:, 0:MR_F].rearrange("p (c t) -> p c t", c=8)
    a0 = carve["zA"][0]
    qT = MR[:, a0:a0 + 1024].bitcast(BF16).rearrange("p (c t) -> p c t", c=8)
    oT = MR[:, a0 + 1024:a0 + 2048].bitcast(BF16).rearrange("p (c t) -> p c t", c=8)
    ET = MR[:, a0 + 2048:a0 + 2560].bitcast(BF16).rearrange("p (a b t) -> p a b t", a=2, b=2)
    rec = MR[:, a0 + 2560:a0 + 3072].rearrange("p (a t) -> p a t", a=2)
    assert a0 + 3072 <= carve["yT"][0], "attention scratch overlaps yT"

    banks = [(S.buf(f"bank{i}"), nc.alloc_psum_tensor(f"bank{i}", [128, 512], F32)) for i in range(8)]
    for bB, _ in banks:
        bB.excl = True
    free_banks = list(range(8))
    bank_idx = {}
    for i_, (bB_, _t) in enumerate(banks):
        bank_idx[bB_.name] = i_

    def acquire():
        while not free_banks:
            yield ("wait", "__bank__")
        i = free_banks.pop(0)
        return banks[i]

    def bank():
        assert free_banks, "no free PSUM bank in immediate mode"
        return banks[free_banks.pop(0)]

    def rel(bB):
        i = bank_idx[bB.name]
        if S.defer:
            S.pend.append(("rel", i))
        else:
            free_banks.append(i)

    locks = {"norm": True}

    def lock_acquire(name):
        while not locks[name]:
            yield ("wait", "__lock__" + name)
        locks[name] = False

    def lock_release(name):
        if S.defer:
            S.pend.append(("rel", "L:" + name))
        else:
            locks[name] = True

    def _on_rel(i):
        if isinstance(i, str):
            locks[i[2:]] = True
        else:
            free_banks.append(i)

    S.on_rel = _on_rel
    S.flag_ok = lambda name: (bool(free_banks) if name == "__bank__" else
                              (locks[name[8:]] if name.startswith("__lock__") else name in S.flags))

    def drive(gen):
        try:
            while True:
                r = next(gen)
                assert not (isinstance(r, tuple) and r and r[0] == "wait"), "blocking wait in immediate mode"
        except StopIteration as e:
            return e.value

    def grouped(items, emit_mm, emit_evac):
        for i in range(0, len(items), 2):
            bB, t = bank()
            grp = items[i:i + 2]
            for j, it in enumerate(grp):
                emit_mm(it, bB, t[:, 256 * j:256 * j + 256])
            for j, it in enumerate(grp):
                emit_evac(it, bB, t[:, 256 * j:256 * j + 256], i // 2)
            rel(bB)

    def grouped2(items, emit_mm, emit_evac_bank, n):
        for i in range(0, len(items), 2):
            bB, t = bank()
            for j in range(2):
                emit_mm(items[i + j], bB, t[:, 256 * j:256 * j + 256])
            emit_evac_bank(items[i], bB, t[:, :].rearrange("p (a t) -> p a t", a=2)[:, :, 0:n], i // 2)
            rel(bB)

    NSLOT = NSUBP + 1
    xB = [S.buf(f"x{i}") for i in range(NSLOT)]
    hB = [S.buf("h0"), S.buf("h1")]
    hfB = [S.buf(f"hf{i}") for i in range(NSLOT)]
    yaccB = [S.buf(f"yacc{i}") for i in range(NSLOT)]
    sqB = S.buf("sq")
    rstdB = [S.buf("rstd0"), S.buf("rstd1")]
    stmB, stvB = S.buf("stm"), S.buf("stv")
    kTB, vvB = S.buf("kT"), S.buf("vv")
    ppB, ppnB, rowbB, sgwB, bdB, onesB = S.buf("pp"), S.buf("ppn"), S.buf("rowb"), S.buf("sgw"), S.buf("bd"), S.buf("ones")
    identB, sgbrB, icntB, maskB = S.buf("ident"), S.buf("sgbr"), S.buf("icnt"), S.buf("mask")
    smB = [S.buf("sm0"), S.buf("sm1")]
    stZB, stHB = S.buf("stZ"), S.buf("stH")
    uTB, rlB = S.buf("uT"), [S.buf("rl0"), S.buf("rl1")]
    gsB = [S.buf(f"gs{i}") for i in range(3)]
    ringB = [S.buf(f"ring{i}") for i in range(4)]
    w9B = S.buf("w9")
    zAB = [S.buf("zA0"), S.buf("zA1")]
    T1B, T2B = S.buf("T1"), S.buf("T2")
    tmpB = [S.buf(f"tmp{i}") for i in range(4)]
    ccB = [S.buf(f"cc{i}") for i in range(3)]
    yTB = S.buf("yT")
    yTcB = [S.buf(f"yTc{i}") for i in range(8)]
    qTcB = [S.buf(f"qTc{i}") for i in range(4)]
    pbB = [S.buf("pb0"), S.buf("pb1")]
    yaB = [S.buf("ya0"), S.buf("ya1")]
    ubB = [S.buf(f"ub{i}") for i in range(4)]
    vlnB = [S.buf("vln0"), S.buf("vln1")]
    ybB = [S.buf(f"yb{i}") for i in range(4)]
    hbufB = [S.buf(f"hbuf{i}") for i in range(3)]
    ycB = [S.buf(f"yc{i}") for i in range(3)]
    dgB = [S.buf(f"dg{i}") for i in range(3)]
    qTB, oTB, ETB, recB = S.buf("qT"), S.buf("oT"), [S.buf("ET0"), S.buf("ET1")], [S.buf("rec0"), S.buf("rec1")]
    mixer_bufs = zAB + [T1B, T2B] + tmpB + ccB + [yTB] + yTcB + pbB + yaB + ubB + vlnB + ybB + hbufB + ycB + dgB
    q2cB = [S.buf(f"q2c{i}") for i in range(4)]
    attn_bufs = [qTB, oTB] + qTcB + ETB + recB
    attn_alias_src = zAB + [T1B, T2B] + tmpB + ccB

    loads = []
    state = {"emitted": 0}

    def w_rows(src2d, p=128):
        return src2d.rearrange("(k p) n -> p k n", p=p)

    def add_load(kind, l, q=None):
        j = len(loads)
        slot = j % 4

        def emit():
            R, RB = ring[slot], ringB[slot]
            if kind == "inA":
                S.dma("pool", R[:, :, :], w_rows(w_in_d[l, :, 0:1024]), writes=[RB])
            elif kind == "inB":
                S.dma("pool", R[:, :, 0:768], w_rows(w_in_d[l, :, 1024:1792]), writes=[RB])
            elif kind == "out":
                S.dma("pool", R[:, 0:2, :], w_rows(w_out_d[l, 0:256, :]), writes=[RB])
                S.dma("pool", R[0:96, 2:6, :], w_rows(w_out_d[l, 256:640, :], 96), writes=[RB])
                S.dma("pool", R[:, 6:8, :], w_rows(w_out_d[l, 640:896, :]), writes=[RB])
                S.dma("pool", w9[:, :], w_out_d[l, 896:1024, :], writes=[w9B])
            elif kind in ("wq", "wk", "wv", "wo"):
                src = {"wq": wq_d, "wk": wk_d, "wv": wv_d, "wo": wo_d}[kind]
                S.dma("pool", R[:, :, :], w_rows(src[l, :, :]), writes=[RB])
            elif kind == "w1":
                S.dma("pool", R[:, :, :], w_rows(w1_d[l, :, q * 1024:(q + 1) * 1024]), writes=[RB])
            elif kind == "w2":
                S.dma("pool", R[:, :, :], w_rows(w2_d[l, q * 1024:(q + 1) * 1024, :]), writes=[RB])

        loads.append(emit)
        return j

    def pump(upto):
        while state["emitted"] <= upto and state["emitted"] < len(loads):
            loads[state["emitted"]]()
            state["emitted"] += 1

    def done(j):
        pump(j + 4)

    kv_loads = {}
    pl_loads = {}
    for p in range(NPASS):
        for li, l in enumerate(LAYERS):
            d = {}
            d["inA"] = add_load("inA", l)
            d["inB"] = add_load("inB", l)
            d["out"] = add_load("out", l)
            if p == 0:
                kv_loads[li] = (add_load("wk", l), add_load("wv", l))
            d["wq"] = add_load("wq", l)
            d["wo"] = add_load("wo", l)
            for q in range(4):
                d[("w1", q)] = add_load("w1", l, q)
                d[("w2", q)] = add_load("w2", l, q)
            pl_loads[(p, li)] = d

    S.dma("sp", pp[:, :], pp_d, writes=[ppB])
    S.dma("sp", ident[:, :], ident_d, writes=[identB])
    S.dma("sp", icnt[:, :, :], icnt_d, writes=[icntB])
    S.dma("sp", mask[:, :], mask_d, writes=[maskB])
    pump(3)
    for l_ in range(DEPTH):
        S.op("dve", lambda e, l_=l_: e.tensor_scalar(out=ppn[:, 9 * l_:9 * l_ + 3], in0=pp[:, BCG_(l_, 0):BCG_(l_, 0) + 3], scalar1=-1.0, scalar2=None, op0=ALU.mult),
             reads=[ppB], writes=[ppnB])
        S.op("dve", lambda e, l_=l_: e.tensor_scalar(out=ppn[:, 9 * l_ + 3:9 * l_ + 9], in0=pp[:, CLG_(l_, 0):CLG_(l_, 0) + 6], scalar1=-1.0, scalar2=None, op0=ALU.mult),
             reads=[ppB], writes=[ppnB])
    S.op("dve", lambda e: e.memset(ones[:, :], 1.0), writes=[onesB])
    S.op("dve", lambda e: e.memset(bd[:, :, :, :], 0.0), writes=[bdB])
    S.op("dve", lambda e: e.memset(MR[:, 0:mr_total], 0.0), writes=mixer_bufs)
    S.dma("sp", triu, triu_d, writes=[yTB])
    for li, l in enumerate(LAYERS):
        for c in range(2):
            for hh in range(2):
                g = 2 * c + hh
                S.dma("pool", bd[64 * hh:64 * hh + 64, li, c, 64 * hh:64 * hh + 64], poolw_d[l, g, :, :], writes=[bdB])
        S.dma("pool", sgbr[0:1, li * 512:(li + 1) * 512], sgb_d[l:l + 1, :], writes=[sgbrB])
        for h in range(4):
            tb = tmpB[h % 4]
            S.dma("sp", tmp[:, h % 4, 0:128], sgwT_d[l, h, :, :], writes=[tb])
            S.op("dve", lambda e, h=h, li=li: e.tensor_tensor(out=sgw[:, li, h, :], in0=tmp[:, h % 4, 0:128], in1=triu, op=ALU.mult),
                 reads=[tb, yTB], writes=[sgwB])

    def norm_stats(src3d, src_bufs, n, par):
        S.op("act", lambda e: e.activation(out=sq[:, :, 0:n], in_=src3d, func=AF.Square), reads=src_bufs, writes=[sqB], dur=2.0)
        return stats_from_sq(n, par)

    def stats_gen(n, par):
        bB, t = yield from acquire()
        S.mm(bB, t[:, 0:n], [(ones[:, :], sq[:, c, 0:n]) for c in range(8)], reads=[sqB, onesB])
        r = st[:, par, 0:n]
        S.op("act", lambda e: e.activation(out=r, in_=t[:, 0:n], func=AF.Ln, scale=1.0 / D, bias=EPS),
             reads=[bB], writes=[rstdB[par]])
        rel(bB)
        S.op("act", lambda e: e.activation(out=r, in_=r, func=AF.Exp, scale=-0.5),
             reads=[rstdB[par]], writes=[rstdB[par]])
        return r

    def stats_from_sq(n, par):
        return drive(stats_gen(n, par))

    def hview(sub, ffn=False):
        if ffn:
            return (lambda k: hTf[:, k, sub.off:sub.off + sub.n]), hfB[sub.slot]
        hs = sub.slot % 2
        return (lambda k: hT[:, k, hs * SUB:hs * SUB + sub.n]), hB[hs]

    def pre_norm(l, gi, sub, par, ffn=False):
        return drive(pre_norm_gen(l, gi, sub, par, ffn))

    def pre_norm_gen(l, gi, sub, par, ffn=False):
        n, off, s = sub.n, sub.off, sub.slot
        hv, hb = hview(sub, ffn)
        yield from lock_acquire("norm")
        S.op("act", lambda e: e.activation(out=sq[:, :, 0:n], in_=xres[:, :, off:off + n], func=AF.Square), reads=[xB[s]], writes=[sqB], dur=2.0)
        yield
        r = yield from stats_gen(n, par)
        yield
        for c in range(8):
            S.op("dve", lambda e, c=c: e.scalar_tensor_tensor(
                out=hv(c), in0=xres[:, c, off:off + n], scalar=pp[:, G_(l, gi, c):G_(l, gi, c) + 1],
                in1=r, op0=ALU.mult, op1=ALU.mult), reads=[xB[s], rstdB[par], ppB], writes=[hb])
            if c == 7:
                lock_release("norm")
            yield

    def post_norm_residual(l, gi, sub, par, y_ap_fn, yBuf, y3d):
        return drive(post_norm_gen(l, gi, sub, par, y_ap_fn, yBuf, y3d))

    def post_norm_gen(l, gi, sub, par, y_ap_fn, yBuf, y3d):
        n, off, s = sub.n, sub.off, sub.slot
        allb = list(dict.fromkeys(yBuf(c) for c in range(8)))
        yield from lock_acquire("norm")
        S.op("act", lambda e: e.activation(out=sq[:, :, 0:n], in_=y3d, func=AF.Square), reads=allb, writes=[sqB], dur=2.0)
        yield
        r = yield from stats_gen(n, par)
        yield
        for c in range(8):
            S.op("dve", lambda e, c=c: e.scalar_tensor_tensor(
                out=y_ap_fn(c), in0=y_ap_fn(c), scalar=pp[:, G_(l, gi, c):G_(l, gi, c) + 1],
                in1=r, op0=ALU.mult, op1=ALU.mult), reads=[yBuf(c), rstdB[par], ppB], writes=[yBuf(c)])
        lock_release("norm")
        S.op("dve", lambda e: e.tensor_tensor(out=xres[:, :, off:off + n], in0=xres[:, :, off:off + n], in1=y3d, op=ALU.add),
             reads=allb + [xB[s]], writes=[xB[s]], dur=2.3)

    def gelu_chain(z, w, out, zb, wb, outb, out_wait=None):
        S.op("act", lambda e: e.activation(out=w, in_=z, func=AF.Square), reads=[zb], writes=[wb]); yield
        S.op("dve", lambda e: e.scalar_tensor_tensor(out=w, in0=w, scalar=1.0 / 0.044715, in1=z, op0=ALU.add, op1=ALU.mult),
             reads=[wb, zb], writes=[wb]); yield
        S.op("act", lambda e: e.activation(out=w, in_=w, func=AF.Exp, scale=-2.0 * GELU_C * 0.044715), reads=[wb], writes=[wb]); yield
        S.op("act", lambda e: e.activation(out=w, in_=w, func=AF.Ln, bias=1.0), reads=[wb], writes=[wb]); yield
        S.op("act", lambda e: e.activation(out=w, in_=w, func=AF.Exp, scale=-1.0), reads=[wb], writes=[wb]); yield
        if out_wait:
            yield ("wait", out_wait)
        S.op("dve", lambda e: e.tensor_tensor(out=out, in0=z, in1=w, op=ALU.mult), reads=[zb, wb], writes=[outb]); yield

    def kv_prologue(li, l):
        jk, jv = kv_loads[li]
        hkv = [hB[0]]
        S.dma("sp", yT[:, :, :], memT.rearrange("c p t -> p c t"), writes=[yTB])
        r = norm_stats(yT[:, :, :], [yTB], 256, 0)
        for c in range(8):
            S.op("dve", lambda e, c=c: e.scalar_tensor_tensor(
                out=hT[:, c, 0:256], in0=yT[:, c, :], scalar=pp[:, G_(l, 6, c):G_(l, 6, c) + 1], in1=r,
                op0=ALU.mult, op1=ALU.mult), reads=[yTB, rstdB[0], ppB], writes=hkv)
        Rk, RkB = ring[jk % 4], ringB[jk % 4]
        grouped(list(range(8)),
                lambda m, bB, ap: S.mm(bB, ap, [(Rk[:, k, m * 128:(m + 1) * 128], hT[:, k, 0:256]) for k in range(8)], reads=[RkB] + hkv),
                lambda m, bB, ap, gi: S.op("act", lambda e: e.activation(out=kT[:, li, m, :], in_=ap, func=AF.Copy), reads=[bB], writes=[kTB]))
        done(jk)
        Rv, RvB = ring[jv % 4], ringB[jv % 4]
        for mc in range(2):
            for hf in range(2):
                bB, t = bank()
                S.mm(bB, t[:, :], [(hT[:, k, mc * 128:(mc + 1) * 128], Rv[:, k, hf * 512:(hf + 1) * 512]) for k in range(8)],
                     reads=[RvB] + hkv)
                S.op("dve", lambda e, mc=mc, hf=hf, t=t: e.tensor_copy(out=vv[:, li, mc, hf * 512:(hf + 1) * 512], in_=t[:, :]),
                     reads=[bB], writes=[vvB])
                rel(bB)
        done(jv)

    def interleave_gen(gens):
        gens = list(gens)
        while gens:
            nxt = []
            for g in gens:
                try:
                    r = next(g)
                    nxt.append(g)
                    yield r
                except StopIteration:
                    pass
            gens = nxt

    def mixer_body(p, li, l, sub, par, last_layer, prev_key):
        key = f"{p}_{li}_{sub.slot}"
        n, off, s = sub.n, sub.off, sub.slot
        ld = pl_loads[(p, li)]
        RA, RAB = ring[ld["inA"] % 4], ringB[ld["inA"] % 4]
        RB_, RBB = ring[ld["inB"] % 4], ringB[ld["inB"] % 4]
        RO, ROB = ring[ld["out"] % 4], ringB[ld["out"] % 4]
        hrhs, hb_ = hview(sub)
        W = n + 16
        nblk = n // 128

        def pool_s1():
            bB, t = yield from acquire()
            for c in range(2):
                S.mm(bB, t[:, 256 * c:256 * c + n], [(RA[:, k, 128 * c:128 * c + 128], hrhs(k)) for k in range(8)], reads=[RAB, hb_])
            yield
            for c in range(2):
                S.op("act", lambda e, c=c: e.activation(out=zA[:, c, 16:16 + n], in_=t[:, 256 * c:256 * c + n], func=AF.Identity,
                                                       bias=pp[:, BA_(l, c):BA_(l, c) + 1]), reads=[bB, ppB], writes=[zAB[c]])
                if c == 1:
                    rel(bB)
                yield
            for c in range(2):
                S.op("dve", lambda e, c=c: e.tensor_tensor(out=T1[:, 1:W], in0=zA[:, c, 1:W], in1=zA[:, c, 0:W - 1], op=ALU.add),
                     reads=[zAB[c]], writes=[T1B]); yield
                S.op("dve", lambda e: e.tensor_tensor(out=T2[:, 3:W], in0=T1[:, 3:W], in1=T1[:, 1:W - 2], op=ALU.add),
                     reads=[T1B], writes=[T2B]); yield
                if c == 1:
                    S.op("dve", lambda e: e.tensor_tensor(out=T1[:, 7:W], in0=T2[:, 7:W], in1=T2[:, 3:W - 4], op=ALU.add),
                         reads=[T2B], writes=[T1B]); yield
                    S.op("dve", lambda e: e.tensor_tensor(out=T2[:, 15:W], in0=T1[:, 15:W], in1=T1[:, 7:W - 8], op=ALU.add),
                         reads=[T1B], writes=[T2B]); yield
                for (lo, srcT, srcB) in ((0, T1, T1B), (64, T2, T2B)):
                    if prev_key and c == 0 and lo == 0:
                        yield ("wait", prev_key + "_pool")
                    S.op("dve", lambda e, lo=lo, srcT=srcT, c=c: e.scalar_tensor_tensor(
                        out=pb[lo:lo + 64, c, 0:n], in0=srcT[lo:lo + 64, 16:16 + n],
                        scalar=pp[lo:lo + 64, IW_(l, c):IW_(l, c) + 1], in1=zA[lo:lo + 64, c, 16:16 + n],
                        op0=ALU.mult, op1=ALU.subtract), reads=[srcB, zAB[c], ppB], writes=[pbB[c]]); yield
                    if sub.first_real:
                        S.op("dve", lambda e, lo=lo, srcT=srcT, c=c: e.tensor_tensor(
                            out=rl[lo:lo + 64, 0, 0:16], in0=srcT[lo:lo + 64, 16:32], in1=icnt[lo:lo + 64, c, :], op=ALU.mult),
                            reads=[srcB, icntB], writes=[rlB[0]]); yield
                        S.op("dve", lambda e, lo=lo, c=c: e.tensor_tensor(
                            out=pb[lo:lo + 64, c, 0:16], in0=rl[lo:lo + 64, 0, 0:16], in1=zA[lo:lo + 64, c, 16:32], op=ALU.subtract),
                            reads=[rlB[0], zAB[c]], writes=[pbB[c]]); yield
                if sub.is_pre:
                    S.op("dve", lambda e, c=c: e.tensor_scalar(out=zA[:, c, 0:16], in0=zA[:, c, n:n + 16], scalar1=mask[:, 0:1],
                                                              scalar2=None, op0=ALU.mult), reads=[zAB[c], maskB], writes=[zAB[c]])
                else:
                    S.op("dve", lambda e, c=c: e.tensor_copy(out=zA[:, c, 0:16], in_=zA[:, c, n:n + 16]), reads=[zAB[c]], writes=[zAB[c]])
                yield

        def pool_s2():
            bB2, t2 = yield from acquire()
            for c in range(2):
                S.mm(bB2, t2[:, 256 * c:256 * c + n], [(bd[:, li, c, :], pb[:, c, 0:n])], reads=[bdB, pbB[c]])
            S.set_flag(key + "_pool")
            yield
            for c in range(2):
                S.op("act", lambda e, c=c: e.activation(out=ya[:, c, 0:n], in_=t2[:, 256 * c:256 * c + n], func=AF.Identity,
                                                       scale=pp[:, PSC_(l, c):PSC_(l, c) + 1]), reads=[bB2, ppB], writes=[yaB[c]])
                if c == 1:
                    rel(bB2)
                yield

        def u_pair(h0):
            bB, t = yield from acquire()
            for j in range(2):
                h = h0 + j
                S.mm(bB, t[0:96, 256 * j:256 * j + n], [(RA[:, k, 256 + 96 * h:256 + 96 * h + 96], hrhs(k)) for k in range(8)], reads=[RAB, hb_])
            yield

            for j in range(2):
                S.op("act", lambda e, j=j: e.activation(out=tmp[0:96, 2 * j, 0:n], in_=t[0:96, 256 * j:256 * j + n], func=AF.Identity,
                                                       bias=pp[0:96, BU_(l, h0 + j):BU_(l, h0 + j) + 1]),
                     reads=[bB, ppB], writes=[tmpB[2 * j]])
            rel(bB)
            yield

            def chain(j):
                h = h0 + j
                ts = 2 * j
                z = tmp[0:96, ts, 0:n]
                w = tmp[0:96, ts + 1, 0:n]
                yield from gelu_chain(z, w, ub[0:96, h, 0:n], tmpB[ts], tmpB[ts + 1], ubB[h], out_wait=(prev_key + "_sg") if prev_key else None)

            yield from interleave_gen([chain(0), chain(1)])

        def v_chain(b, ts):
            bB, t = yield from acquire()
            S.mm(bB, t[:, 0:384], [(hrhs(k)[:, 128 * b:128 * b + 128], RA[:, k, 640:1024]) for k in range(8)],
                 reads=[RAB, hb_])
            yield
            z = tmp[:, ts, :]
            w = tmp[:, ts + 1, :]
            S.op("dve", lambda e: e.tensor_tensor(out=z, in0=t[:, 0:384], in1=rowb[:, 0, :], op=ALU.add),
                 reads=[bB, rowbB], writes=[tmpB[ts]])
            rel(bB)
            yield
            yield from gelu_chain(z, w, z, tmpB[ts], tmpB[ts + 1], tmpB[ts])
            smb = smB[b % 2]
            smt = sm[:, b % 2, :]
            S.op("dve", lambda e: e.bn_stats(out=smt[:, 0:6], in_=z), reads=[tmpB[ts]], writes=[smb]); yield
            S.op("dve", lambda e: e.bn_aggr(out=smt[:, 8:10], in_=smt[:, 0:6]), reads=[smb], writes=[smb]); yield
            S.op("act", lambda e: e.activation(out=smt[:, 10:11], in_=smt[:, 9:10], func=AF.Ln, bias=EPS), reads=[smb], writes=[smb]); yield
            S.op("act", lambda e: e.activation(out=smt[:, 10:11], in_=smt[:, 10:11], func=AF.Exp, scale=-0.5), reads=[smb], writes=[smb]); yield
            S.op("dve", lambda e: e.tensor_scalar(out=z, in0=z, scalar1=smt[:, 8:9], scalar2=smt[:, 10:11], op0=ALU.subtract, op1=ALU.mult),
                 reads=[tmpB[ts], smb], writes=[tmpB[ts]]); yield
            S.op("dve", lambda e: e.tensor_tensor(out=z, in0=z, in1=rowb[:, 1, :], op=ALU.mult), reads=[tmpB[ts], rowbB], writes=[tmpB[ts]]); yield
            if prev_key:
                yield ("wait", prev_key + "_sgmm")
            S.op("dve", lambda e: e.tensor_tensor(out=vln[:, b, :], in0=z, in1=rowb[:, 2, :], op=ALU.add), reads=[tmpB[ts], rowbB], writes=[vlnB[b]]); yield

        def sg_s1():
            yield from u_pair(0)
            yield from u_pair(2)
            yield from interleave_gen([v_chain(b, 2 * b) for b in range(nblk)])

        def sg_s2():
            for h0 in (0, 2):
                bB, t = yield from acquire()
                for j in range(2):
                    h = h0 + j
                    for b in range(nblk):
                        S.mm(bB, t[0:96, 256 * j + 128 * b:256 * j + 128 * b + 128],
                             [(vln[:, b, 96 * h:96 * h + 96], sgw[:, li, h, :]),
                              (ones[0:1, 0:96], sgbr[0:1, li * 512 + 128 * h:li * 512 + 128 * h + 128])],
                             reads=[vlnB[b], sgwB, onesB, sgbrB])
                if h0 == 2:
                    S.set_flag(key + "_sgmm")
                yield
                for j in range(2):
                    h = h0 + j
                    S.op("dve", lambda e, h=h, j=j: e.tensor_tensor(out=yb[0:96, h, 0:n], in0=t[0:96, 256 * j:256 * j + n], in1=ub[0:96, h, 0:n], op=ALU.mult),
                         reads=[bB, ubB[h]], writes=[ybB[h]])
                    if h == 3:
                        S.set_flag(key + "_sg")
                    if j == 1:
                        rel(bB)
                    yield

        def glu_chain(c):
            bB, t = yield from acquire()
            pa, pg = t[:, 0:256], t[:, 256:512]
            S.mm(bB, pa[:, 0:n], [(RB_[:, k, 128 * c:128 * c + 128], hrhs(k)) for k in range(8)], reads=[RBB, hb_])
            S.mm(bB, pg[:, 0:n], [(RB_[:, k, 384 + 128 * c:384 + 128 * c + 128], hrhs(k)) for k in range(8)], reads=[RBB, hb_])
            yield
            w = gs[:, c, 0:n]
            S.op("act", lambda e: e.activation(out=w, in_=pg[:, 0:n], func=AF.Exp, scale=-1.0, bias=ppn[:, 9 * l + c:9 * l + c + 1]),
                 reads=[bB, ppnB], writes=[gsB[c]]); yield
            S.op("act", lambda e: e.activation(out=w, in_=w, func=AF.Ln, bias=1.0), reads=[gsB[c]], writes=[gsB[c]]); yield
            S.op("act", lambda e: e.activation(out=w, in_=w, func=AF.Exp, scale=-1.0), reads=[gsB[c]], writes=[gsB[c]]); yield
            if prev_key:
                yield ("wait", prev_key + "_conv")
            S.op("dve", lambda e: e.scalar_tensor_tensor(out=hbuf[:, c, 32:32 + n], in0=pa[:, 0:n],
                                                        scalar=pp[:, BCA_(l, c):BCA_(l, c) + 1], in1=w, op0=ALU.add, op1=ALU.mult),
                 reads=[bB, gsB[c], ppB], writes=[hbufB[c]])
            rel(bB)
            yield

        def conv_mm(c):
            bB, t = yield from acquire()
            pc = t[:, 0:n]
            S.mm(bB, pc, [(dg[:, 31 * c + k, :], hbuf[:, c, 2 + k:2 + k + n]) for k in range(CONV_K)], reads=[dgB[c], hbufB[c]])
            if sub.is_pre:
                S.op("dve", lambda e: e.tensor_scalar(out=hbuf[:, c, 0:32], in0=hbuf[:, c, n:n + 32], scalar1=mask[:, 0:1],
                                                     scalar2=None, op0=ALU.mult), reads=[hbufB[c], maskB], writes=[hbufB[c]])
            else:
                S.op("dve", lambda e: e.tensor_copy(out=hbuf[:, c, 0:32], in_=hbuf[:, c, n:n + 32]), reads=[hbufB[c]], writes=[hbufB[c]])
            return bB, pc

        def conv_chunk(c, bB, pc):
            cbcol = pp[:, CB_(l, c):CB_(l, c) + 1]
            S.op("act", lambda e: e.activation(out=cc[:, c, 0:n], in_=pc, func=AF.Identity, bias=cbcol),
                 reads=[bB, ppB], writes=[ccB[c]])
            rel(bB)
            yield
            S.op("act", lambda e: e.activation(out=sq[:, c, 0:n], in_=cc[:, c, 0:n], func=AF.Copy), reads=[ccB[c]], writes=[sqB]); yield
            S.op("act", lambda e: e.activation(out=sq[:, 3 + c, 0:n], in_=cc[:, c, 0:n], func=AF.Square),
                 reads=[ccB[c]], writes=[sqB]); yield

        mt = st[:, 2, 0:n]
        vt = st[:, 3, 0:n]

        def silu_chain(c):
            t_ = cc[:, c, 0:n]
            w = yT[:, c, 0:n]
            S.op("dve", lambda e: e.tensor_tensor(out=t_, in0=t_, in1=mt, op=ALU.subtract), reads=[ccB[c], stmB], writes=[ccB[c]]); yield
            S.op("dve", lambda e: e.tensor_tensor(out=t_, in0=t_, in1=vt, op=ALU.mult), reads=[ccB[c], stvB], writes=[ccB[c]]); yield
            S.op("act", lambda e: e.activation(out=w, in_=t_, func=AF.Exp, scale=ppn[:, 9 * l + 3 + c:9 * l + 4 + c],
                                               bias=ppn[:, 9 * l + 6 + c:9 * l + 7 + c]), reads=[ccB[c], ppnB], writes=[yTcB[c]]); yield
            S.op("dve", lambda e: e.tensor_scalar(out=t_, in0=t_, scalar1=pp[:, CLG_(l, c):CLG_(l, c) + 1],
                                                 scalar2=pp[:, CLB_(l, c):CLB_(l, c) + 1], op0=ALU.mult, op1=ALU.add),
                 reads=[ccB[c], ppB], writes=[ccB[c]]); yield
            S.op("act", lambda e: e.activation(out=w, in_=w, func=AF.Ln, bias=1.0), reads=[yTcB[c]], writes=[yTcB[c]]); yield
            S.op("act", lambda e: e.activation(out=w, in_=w, func=AF.Exp, scale=-1.0), reads=[yTcB[c]], writes=[yTcB[c]]); yield
            S.op("dve", lambda e: e.tensor_tensor(out=yc[:, c, 0:n], in0=t_, in1=w, op=ALU.mult), reads=[ccB[c], yTcB[c]], writes=[ycB[c]]); yield

        def conv_s1():
            yield from interleave_gen([glu_chain(c) for c in range(3)])

        def conv_s2():
            cm = []
            for c in range(3):
                cm.append((yield from conv_mm(c)))
            S.set_flag(key + "_conv")
            yield
            yield from lock_acquire("norm")
            yield from interleave_gen([conv_chunk(c, cm[c][0], cm[c][1]) for c in range(3)])
            bS, tS = yield from acquire()
            ps_, pq = tS[:, 0:n], tS[:, 256:256 + n]
            S.mm(bS, ps_, [(ones[:, :], sq[:, c, 0:n]) for c in range(3)], reads=[sqB, onesB])
            S.mm(bS, pq, [(ones[:, :], sq[:, 3 + c, 0:n]) for c in range(3)], reads=[sqB, onesB])
            lock_release("norm")
            yield
            S.op("act", lambda e: e.activation(out=mt, in_=ps_, func=AF.Identity, scale=1.0 / W_C), reads=[bS], writes=[stmB]); yield
            S.op("dve", lambda e: e.tensor_tensor(out=vt, in0=mt, in1=mt, op=ALU.mult), reads=[stmB], writes=[stvB]); yield
            S.op("dve", lambda e: e.scalar_tensor_tensor(out=vt, in0=pq, scalar=1.0 / W_C, in1=vt, op0=ALU.mult, op1=ALU.subtract),
                 reads=[bS, stvB], writes=[stvB])
            rel(bS)
            yield
            S.op("act", lambda e: e.activation(out=vt, in_=vt, func=AF.Ln, bias=EPS), reads=[stvB], writes=[stvB]); yield
            S.op("act", lambda e: e.activation(out=vt, in_=vt, func=AF.Exp, scale=-0.5), reads=[stvB], writes=[stvB]); yield
            yield from interleave_gen([silu_chain(c) for c in range(3)])

        def wout_mm(m, bB, ap):
            ms = slice(m * 128, (m + 1) * 128)
            pairs = [(RO[:, 0, ms], ya[:, 0, 0:n]), (RO[:, 1, ms], ya[:, 1, 0:n])]
            pairs += [(RO[0:96, 2 + h, ms], yb[0:96, h, 0:n]) for h in range(4)]
            pairs += [(RO[:, 6, ms], yc[:, 0, 0:n]), (RO[:, 7, ms], yc[:, 1, 0:n]), (w9[:, ms], yc[:, 2, 0:n])]
            S.mm(bB, ap[:, 0:n], pairs, reads=[ROB, w9B] + yaB + ybB + ycB)

        def s3():
            for m0 in range(0, 8, 2):
                bB, t = yield from acquire()
                for j in range(2):
                    wout_mm(m0 + j, bB, t[:, 256 * j:256 * j + 256])
                yield
                if (m0 // 2) % 2 == 0:
                    S.op("act", lambda e: e.activation(out=yT[:, m0:m0 + 2, 0:n], in_=t[:, :].rearrange("p (a t) -> p a t", a=2)[:, :, 0:n], func=AF.Copy),
                         reads=[bB], writes=[yTcB[m0], yTcB[m0 + 1]])
                else:
                    S.op("dve", lambda e: e.tensor_copy(out=yT[:, m0:m0 + 2, 0:n], in_=t[:, :].rearrange("p (a t) -> p a t", a=2)[:, :, 0:n]),
                         reads=[bB], writes=[yTcB[m0], yTcB[m0 + 1]])
                rel(bB)
                yield
            yield from post_norm_gen(l, 1, sub, par, lambda c: yT[:, c, 0:n], lambda c: yTcB[c], yT[:, :, 0:n])
            yield

        def with_done(gen, flag):
            yield from gen
            S.set_flag(flag)
            yield

        def s3_stream():
            for f_ in ("_dconv", "_dsg", "_dpool"):
                yield ("wait", key + f_)
            yield from s3()

        s1_streams = [glu_chain(c) for c in range(3)] + [sg_s1(), pool_s1()]
        s23_streams = [with_done(conv_s2(), key + "_dconv"), with_done(sg_s2(), key + "_dsg"), with_done(pool_s2(), key + "_dpool")]
        if not (sub.is_pre and last_layer):
            s23_streams.append(s3_stream())
        return s1_streams, s23_streams

    def delayed(gen, k):
        for _ in range(k):
            yield
        yield from gen

    qbufs = [(qT, qTcB), (uT, q2cB)]

    def attn_q_gen(p, li, l, sub, qsel):
        n = sub.n
        ld = pl_loads[(p, li)]
        RQ, RQB = ring[ld["wq"] % 4], ringB[ld["wq"] % 4]
        hv, hb_ = hview(sub)
        qT, qTcB = qbufs[qsel]
        for m in range(0, 8, 2):
            bB, t = yield from acquire()
            for j in range(2):
                S.mm(bB, t[:, 256 * j:256 * j + n], [(RQ[:, k, (m + j) * 128:(m + j + 1) * 128], hv(k)) for k in range(8)], reads=[RQB, hb_])
            yield
            ap3 = t[:, :].rearrange("p (a t) -> p a t", a=2)[:, :, 0:n]
            if (m // 2) % 2 == 0:
                S.op("act", lambda e, m=m, ap3=ap3: e.activation(out=qT[:, m:m + 2, 0:n], in_=ap3, func=AF.Copy), reads=[bB], writes=[qTcB[m // 2]], dur=0.6)
            else:
                S.op("dve", lambda e, m=m, ap3=ap3: e.tensor_copy(out=qT[:, m:m + 2, 0:n], in_=ap3), reads=[bB], writes=[qTcB[m // 2]], dur=0.6)
            rel(bB)
            yield

    def attn_main_gen(p, li, l, sub, par, qsel):
        n, off, s = sub.n, sub.off, sub.slot
        ld = pl_loads[(p, li)]
        RO, ROB = ring[ld["wo"] % 4], ringB[ld["wo"] % 4]
        qT, qTcB = qbufs[qsel]

        def head_gen(h):
            hp = h % 2
            bB, t = yield from acquire()
            for mc in range(2):
                S.mm(bB, t[:, 256 * mc:256 * mc + n], [(kT[:, li, 2 * h + dc, mc * 128:(mc + 1) * 128], qT[:, 2 * h + dc, 0:n]) for dc in range(2)],
                     reads=[kTB, qTcB[h]])
            yield
            S.op("act", lambda e: e.activation(out=ET[:, hp, :, 0:n], in_=t[:, :].rearrange("p (a t) -> p a t", a=2)[:, :, 0:n],
                                               func=AF.Exp, scale=1.0 / 16.0), reads=[bB], writes=[ETB[hp]])
            rel(bB)
            yield
            bS, tS = yield from acquire()
            S.mm(bS, tS[:, 0:n], [(ones[:, :], ET[:, hp, 0, 0:n]), (ones[:, :], ET[:, hp, 1, 0:n])], reads=[onesB, ETB[hp]])
            bO, tO = yield from acquire()
            for dc in range(2):
                S.mm(bO, tO[:, 256 * dc:256 * dc + n], [(vv[:, li, mc, (2 * h + dc) * 128:(2 * h + dc + 1) * 128], ET[:, hp, mc, 0:n]) for mc in range(2)],
                     reads=[vvB, ETB[hp]])
            yield
            S.op("act", lambda e: e.activation(out=rec[:, hp, 0:n], in_=tS[:, 0:n], func=AF.Ln), reads=[bS], writes=[recB[hp]])
            rel(bS)
            yield
            S.op("act", lambda e: e.activation(out=rec[:, hp, 0:n], in_=rec[:, hp, 0:n], func=AF.Exp, scale=-1.0), reads=[recB[hp]], writes=[recB[hp]]); yield
            for dc in range(2):
                S.op("dve", lambda e, dc=dc: e.tensor_tensor(out=oT[:, 2 * h + dc, 0:n], in0=tO[:, 256 * dc:256 * dc + n], in1=rec[:, hp, 0:n], op=ALU.mult),
                     reads=[bO, recB[hp]], writes=[oTB])
                if dc == 1:
                    rel(bO)
                yield

        yield from interleave_gen([head_gen(0), head_gen(1)])
        yield from interleave_gen([head_gen(2), head_gen(3)])
        for m in range(0, 8, 2):
            bB, t = yield from acquire()
            for j in range(2):
                S.mm(bB, t[:, 256 * j:256 * j + n], [(RO[:, k, (m + j) * 128:(m + j + 1) * 128], oT[:, k, 0:n]) for k in range(8)], reads=[ROB, oTB])
            yield
            ap3 = t[:, :].rearrange("p (a t) -> p a t", a=2)[:, :, 0:n]
            if (m // 2) % 2 == 0:
                S.op("act", lambda e, m=m, ap3=ap3: e.activation(out=yT[:, m:m + 2, 0:n], in_=ap3, func=AF.Copy), reads=[bB], writes=[yTcB[m], yTcB[m + 1]], dur=0.6)
            else:
                S.op("dve", lambda e, m=m, ap3=ap3: e.tensor_copy(out=yT[:, m:m + 2, 0:n], in_=ap3), reads=[bB], writes=[yTcB[m], yTcB[m + 1]], dur=0.6)
            rel(bB)
            yield
        yield from post_norm_gen(l, 3, sub, par, lambda c: yT[:, c, 0:n], lambda c: yTcB[c], yT[:, :, 0:n])
        yield


    def ffn_w1(p, li, l, q, sub, ubuf):
        return drive(ffn_w1_gen(p, li, l, q, sub, ubuf))

    def ffn_w1_gen(p, li, l, q, sub, ubuf, wait_flag=None):
        n, off, s = sub.n, sub.off, sub.slot
        ld = pl_loads[(p, li)]
        R1, R1B = ring[ld[("w1", q)] % 4], ringB[ld[("w1", q)] % 4]
        uT, uTB = ubuf
        if wait_flag:
            yield ("wait", wait_flag)
        for f in range(0, 8, 2):
            bB, t = yield from acquire()
            for j in range(2):
                S.mm(bB, t[:, 256 * j:256 * j + n], [(R1[:, k, (f + j) * 128:(f + j + 1) * 128], hTf[:, k, off:off + n]) for k in range(8)],
                     reads=[R1B, hfB[s]])
            yield
            ap3 = t[:, :].rearrange("p (a t) -> p a t", a=2)[:, :, 0:n]
            S.op("act", lambda e, ap3=ap3: e.activation(out=rl[:, :, 0:n], in_=ap3, func=AF.Relu), reads=[bB], writes=rlB, dur=0.6)
            rel(bB)
            yield
            S.op("dve", lambda e, f=f: e.tensor_tensor(out=uT[:, f:f + 2, 0:n], in0=rl[:, :, 0:n], in1=rl[:, :, 0:n], op=ALU.mult),
                 reads=rlB, writes=[uTB], dur=0.6)
            yield

    def ffn_w2(p, li, l, q, sub, ubuf):
        n, off, s = sub.n, sub.off, sub.slot
        ld = pl_loads[(p, li)]
        R2, R2B = ring[ld[("w2", q)] % 4], ringB[ld[("w2", q)] % 4]
        uT, uTB = ubuf

        def y_evac(m, bB, ap3, gi):
            if q == 0:
                S.op("act", lambda e: e.activation(out=yacc[:, m:m + 2, off:off + n], in_=ap3, func=AF.Copy),
                     reads=[bB], writes=[yaccB[s]])
            else:
                S.op("dve", lambda e: e.tensor_tensor(out=yacc[:, m:m + 2, off:off + n], in0=ap3, in1=yacc[:, m:m + 2, off:off + n], op=ALU.add),
                     reads=[bB, yaccB[s]], writes=[yaccB[s]])

        grouped2(list(range(8)),
                 lambda m, bB, ap: S.mm(bB, ap[:, 0:n], [(R2[:, f, m * 128:(m + 1) * 128], uT[:, f, 0:n]) for f in range(8)], reads=[R2B, uTB]),
                 y_evac, n)


    import os as _os
    MAXPH = int(_os.environ.get("K_MAXPH", "100000"))
    phc = [0]

    def ph_ok():
        phc[0] += 1
        return phc[0] <= MAXPH


    S.alias_barrier([yTB], yTcB)
    stored = {}
    par_ctr = [0]

    def npar():
        par_ctr[0] += 1
        return par_ctr[0] % 2

    for p in range(NPASS):
        subs = []
        if p == 0:
            subs.append(SubT(0, 0, HALO, True, False, 0))
        for i in range(NSUBP):
            subs.append(SubT(1 + i, HALO + SUB * i, SUB, False, (p == 0 and i == 0), HALO + p * PASS_T + SUB * i))
        if p == 0:
            for sub in subs:
                S.dma("sp", xres[:, :, sub.off:sub.off + sub.n], xT[:, :, sub.dram0:sub.dram0 + sub.n].rearrange("c p t -> p c t"),
                      writes=[xB[sub.slot]])
        for li, l in enumerate(LAYERS):
            last_layer = (li == NLY - 1)
            ld = pl_loads[(p, li)]
            for r_ in range(3):
                S.dma("sp", rowb[:, r_, :], rowp_d[l, r_:r_ + 1, :].partition_broadcast(128), writes=[rowbB])
            if not ph_ok():
                continue
            S.alias_barrier(yaccB + hfB, mixer_bufs)
            if p == 0:
                S.op("dve", lambda e: e.memset(zA[:, :, 0:16], 0.0), writes=zAB)
                S.op("dve", lambda e: e.memset(hbuf[:, :, 0:32], 0.0), writes=hbufB)
            else:
                S.op("dve", lambda e: e.tensor_copy(out=zA[:, :, 0:16], in_=stZ[:, li, :, :]), reads=[stZB], writes=zAB)
                S.op("dve", lambda e: e.tensor_copy(out=hbuf[:, :, 0:32], in_=stH[:, li, :, :]), reads=[stHB], writes=hbufB)
            pars = [npar() for _ in subs]
            pre_norm(l, 0, subs[0], pars[0])
            def diag_gen():
                for c_ in range(3):
                    a_ = CW_(l, c_, 0)
                    S.op("dve", lambda e, c_=c_, a_=a_: e.tensor_tensor(
                        out=dg[:, 31 * c_:31 * c_ + 31, :],
                        in0=ident[:, :].unsqueeze(1).broadcast_to([128, CONV_K, 128]),
                        in1=pp[:, a_:a_ + CONV_K].unsqueeze(2).broadcast_to([128, CONV_K, 128]), op=ALU.mult),
                        reads=[identB, ppB], writes=[dgB[c_]], dur=4.5)
                    yield

            S.alias_barrier([uTB], gsB)
            if len(subs) > 1:
                pre_norm(l, 0, subs[1], pars[1])
            stages = [mixer_body(p, li, l, sub, pars[i], last_layer, (f"{p}_{li}_{subs[i - 1].slot}" if i > 0 else None))
                      for i, sub in enumerate(subs)]
            asubs = [sb_ for sb_ in subs if not (sb_.is_pre and last_layer)]
            early_attn = (p >= 1)
            apars = [npar() for _ in asubs]

            def chain2(a, b):
                yield from a
                yield from b
            run_scheduled(S, stages[0][0] + [diag_gen()])
            if len(subs) == 1:
                done(ld["inA"]); done(ld["inB"])
                S.op("dve", lambda e: e.tensor_copy(out=stZ[:, li, :, :], in_=zA[:, :, 0:16]), reads=zAB, writes=[stZB])
            for i, sub in enumerate(subs):
                g_ = list(stages[i][1])
                if i + 1 < len(subs):
                    g_ = stages[i + 1][0] + g_
                elif early_attn:
                    S.alias_barrier(zAB + [T1B, T2B] + tmpB, qTcB)
                    g_.append(chain2(pre_norm_gen(l, 2, asubs[0], apars[0]), attn_q_gen(p, li, l, asubs[0], 0)))
                run_scheduled(S, g_)
                if i + 1 == len(subs) - 1:
                    done(ld["inA"]); done(ld["inB"])
                    S.op("dve", lambda e: e.tensor_copy(out=stZ[:, li, :, :], in_=zA[:, :, 0:16]), reads=zAB, writes=[stZB])
                if i + 2 < len(subs):
                    pre_norm(l, 0, subs[i + 2], pars[i + 2])
            S.op("dve", lambda e: e.tensor_copy(out=stH[:, li, :, :], in_=hbuf[:, :, 0:32]), reads=hbufB, writes=[stHB])
            done(ld["out"])
            if not ph_ok():
                continue
            if p == 0:
                S.alias_barrier(yTcB, [yTB])
                kv_prologue(li, l)
                S.alias_barrier([yTB], yTcB)
            S.alias_barrier(attn_alias_src, attn_bufs)
            S.alias_barrier(mixer_bufs, hfB)
            S.alias_barrier([uTB] + gsB, q2cB)
            pars = apars
            early_ffn = (len(asubs) >= 4 and len(asubs) % 2 == 0)
            if not early_attn:
                pre_norm(l, 2, asubs[0], pars[0])
            if len(asubs) > 1:
                pre_norm(l, 2, asubs[1], pars[1])
            if not early_attn:
                run_scheduled(S, [attn_q_gen(p, li, l, asubs[0], 0)])

            for i, sub in enumerate(asubs):
                g_ = [attn_main_gen(p, li, l, sub, pars[i], i % 2)]
                if i + 1 < len(asubs):
                    qg = attn_q_gen(p, li, l, asubs[i + 1], (i + 1) % 2)
                    if i + 1 >= 2:
                        qg = chain2(pre_norm_gen(l, 2, asubs[i + 1], pars[i + 1]), qg)
                    g_.append(qg)
                if i >= 1:
                    g_.append(pre_norm_gen(l, 4, asubs[i - 1], npar(), True))
                if early_ffn and i == len(asubs) - 1:
                    S.alias_barrier(mixer_bufs, yaccB + hfB)
                    g_.append(ffn_w1_gen(p, li, l, 0, asubs[0], (hT[:, :, 0:SUB], hB[0])))
                run_scheduled(S, g_)
            done(ld["wq"]); done(ld["wo"])
            S.alias_barrier(attn_bufs, attn_alias_src)
            if not ph_ok():
                continue
            S.alias_barrier(mixer_bufs, yaccB + hfB)
            S.alias_barrier(gsB + q2cB, [uTB])
            items = [(q, sub) for q in range(4) for sub in asubs]
            ubufs = [(hT[:, :, 0:SUB], hB[0]), (uT, uTB)]

            def ub_of(ix):
                return ubufs[ix % 2]

            if len(asubs) == 1:
                pre_norm(l, 4, asubs[-1], npar(), ffn=True)
            if not early_ffn:
                ffn_w1(p, li, l, items[0][0], items[0][1], ub_of(0))
            if len(asubs) > 1:
                pre_norm(l, 4, asubs[-1], npar(), ffn=True)
            for ix, (q, sub) in enumerate(items):
                last_of_q = (sub is asubs[-1])
                if last_of_q:
                    done(ld[("w1", q)])
                nxt = items[ix + 1] if ix + 1 < len(items) else None
                pipelined = nxt is not None
                if pipelined:
                    ffn_w1(p, li, l, nxt[0], nxt[1], ub_of(ix + 1))
                ffn_w2(p, li, l, q, sub, ub_of(ix))
                if last_of_q:
                    done(ld[("w2", q)])
                if True:
                    if q == 3:
                        post_norm_residual(l, 5, sub, npar(), lambda c, sub=sub: yacc[:, c, sub.off:sub.off + sub.n], lambda c, sub=sub: yaccB[sub.slot], yacc[:, :, sub.off:sub.off + sub.n])
                        if last_layer and not sub.is_pre and not stored.get((p, sub.slot)):
                            stored[(p, sub.slot)] = True
                            o0 = sub.dram0 - HALO
                            S.dma("sp", outT[:, :, o0:o0 + sub.n].rearrange("c p t -> p c t"), xres[:, :, sub.off:sub.off + sub.n], reads=[xB[sub.slot]])
                            if p + 1 < NPASS:
                                d1 = sub.dram0 + PASS_T
                                S.dma("sp", xres[:, :, sub.off:sub.off + sub.n], xT[:, :, d1:d1 + sub.n].rearrange("c p t -> p c t"),
                                      writes=[xB[sub.slot]])
                if nxt is not None and not pipelined:
                    ffn_w1(p, li, l, nxt[0], nxt[1], ub_of(ix + 1))
        for sub in subs:
            if sub.is_pre or stored.get((p, sub.slot)):
                continue
            o0 = sub.dram0 - HALO
            S.dma("sp", outT[:, :, o0:o0 + sub.n].rearrange("c p t -> p c t"), xres[:, :, sub.off:sub.off + sub.n], reads=[xB[sub.slot]])
    S.final_wait("sp", S.allb)
    return nc


def _pack_pp(inp):
    pp = np.zeros((128, DEPTH * NL), np.float32)

    def col(v):
        return np.ascontiguousarray(v.reshape(-1, 128).T)

    wins = (2, 4, 8, 16)
    for l in range(DEPTH):
        for i, name in enumerate(["pre_mix_g", "post_mix_g", "pre_x_g", "post_x_g", "pre_ff_g", "post_ff_g", "mem_g"]):
            pp[:, G_(l, i, 0):G_(l, i, 0) + 8] = col(inp[name][l])
        b = inp["b_in"][l]
        pp[:, BA_(l, 0):BA_(l, 0) + 2] = col(b[0:256])
        for h in range(4):
            pp[0:96, BU_(l, h)] = b[256 + 96 * h:256 + 96 * h + 96]
        pp[:, BCA_(l, 0):BCA_(l, 0) + 3] = col(b[1024:1408])
        pp[:, BCG_(l, 0):BCG_(l, 0) + 3] = col(b[1408:1792])
        pp[:, PSC_(l, 0):PSC_(l, 0) + 2] = col(inp["pool_scale"][l])
        for c in range(2):
            for hh in range(2):
                pp[64 * hh:64 * hh + 64, IW_(l, c)] = np.float32(1.0) / np.float32(wins[2 * c + hh])
        pp[:, CB_(l, 0):CB_(l, 0) + 3] = col(inp["conv_b"][l])
        pp[:, CLG_(l, 0):CLG_(l, 0) + 3] = col(inp["conv_ln_g"][l])
        pp[:, CLB_(l, 0):CLB_(l, 0) + 3] = col(inp["conv_ln_b"][l])
        cw = inp["conv_w"][l]
        for c in range(3):
            pp[:, CW_(l, c, 0):CW_(l, c, 0) + 31] = cw[:, 128 * c:128 * c + 128].T
    return pp


def _common_inputs(inp):
    f = lambda a: np.ascontiguousarray(np.asarray(a, dtype=np.float32))
    com = {}
    com["pp"] = _pack_pp({k: np.asarray(v, np.float32) for k, v in inp.items()})
    rowp = np.zeros((DEPTH, 3, 384), np.float32)
    for l in range(DEPTH):
        rowp[l, 0] = inp["b_in"][l][640:1024]
        rowp[l, 1] = inp["sg_ln_g"][l]
        rowp[l, 2] = inp["sg_ln_b"][l]
    com["rowp"] = rowp
    com["sgb"] = f(np.asarray(inp["sg_b"]).reshape(DEPTH, 512))
    com["sgwT"] = f(np.transpose(np.asarray(inp["sg_w"]), (0, 1, 3, 2)))
    com["poolw"] = f(inp["pool_w"])
    com["ident"] = np.eye(128, dtype=np.float32)
    com["triu"] = np.triu(np.ones((128, 128), np.float32))
    for k in ("w_in", "w_out", "wq", "wk", "wv", "wo", "w_ff1", "w_ff2"):
        com[k] = f(inp[k])
    return com


def _core_inputs(x, mem, c):
    b, j = c // 4, c % 4
    t0 = j * TPC
    own = x[b, t0:t0 + TPC]
    halo = x[b, t0 - HALO:t0] if j > 0 else np.zeros((HALO, D), np.float32)
    xf = np.concatenate([halo, own], axis=0)
    d = {}
    d["xT"] = np.ascontiguousarray(xf.T).reshape(8, 128, HALO + TPC)
    d["memT"] = np.ascontiguousarray(mem[b].T).reshape(8, 128, 256)
    d["mask"] = np.full((128, 1), 1.0 if j > 0 else 0.0, np.float32)
    wins = (2, 4, 8, 16)
    ic = np.zeros((128, 2, 16), np.float32)
    tpos = np.arange(16, dtype=np.float32) + 1.0
    for cc_ in range(2):
        for hh in range(2):
            w = np.float32(wins[2 * cc_ + hh])
            cnt = np.minimum(tpos, w) if j == 0 else np.full(16, w, np.float32)
            ic[64 * hh:64 * hh + 64, cc_, :] = (np.float32(1.0) / cnt)[None, :]
    d["icnt"] = ic
    return d


FUSED = True
_PROG = {}


def _get_prog(layers):
    key = tuple(layers)
    if key not in _PROG:
        _PROG[key] = build_program(list(layers))
    return _PROG[key]


def _run(layers, x, mem, com):
    nc = _get_prog(layers)
    in_maps = []
    for c in range(NCORE):
        d = dict(com)
        d.update(_core_inputs(x, mem, c))
        in_maps.append(d)
    res = run_bass_kernel_spmd(nc, in_maps, core_ids=list(range(NCORE)))
    out = np.empty((BATCH, SEQ, D), np.float32)
    for c in range(NCORE):
        b, j = c // 4, c % 4
        o = np.asarray(res.results[c]["outT"]).reshape(D, TPC)
        out[b, j * TPC:(j + 1) * TPC] = o.T
    return out


def kernel(**inputs):
    inp = {k: np.asarray(v) for k, v in inputs.items()}
    x = np.asarray(inp["x"], np.float32)
    mem = np.asarray(inp["mem"], np.float32)
    com = _common_inputs(inp)
    if FUSED:
        return _run(range(DEPTH), x, mem, com)
    for l in range(DEPTH):
        x = _run([l], x, mem, com)
    return x
```
